# Optimizing a Trainium2 kernel written in Bass

```python
import math
import jax, jax.numpy as jnp
from jax import lax
import numpy as np

D_MODEL = 1024
BATCH = 8
SEQ = 4096
DEPTH = 2

D_CONV = 256
CONV_GROUPS = 4
CONV_WIDTH = 3
D_GMLP = 256
GMLP_HEADS = 4
GMLP_HEAD_DIM = D_GMLP // GMLP_HEADS
GMLP_CHUNK = 128
HEAD_DIM = 64
NSA_HEADS = 8
NSA_KV_HEADS = 2
NSA_GROUP = NSA_HEADS // NSA_KV_HEADS
D_NSA = NSA_HEADS * HEAD_DIM
D_KV = NSA_KV_HEADS * HEAD_DIM
D_MIX = D_CONV + D_GMLP + D_NSA
D_IN_PROJ = 3 * D_CONV + 2 * D_GMLP + D_NSA + 6 * D_KV + 3 * NSA_HEADS
CMP_LEN = 32
CMP_STRIDE = 16
CMP_HIDDEN = 128
SEL_BLOCK = 64
N_SELECT = 16
WINDOW = 512
Q_BLOCK = 128
D_FF = 2816
ALPHA = (2 * DEPTH) ** 0.25
BETA = (8 * DEPTH) ** -0.25
LN_EPS = 1e-5
NEG = -1e30

kernel_name = "hymba_style_conv_gmlp_nsa_macaron_deepnorm"


def layer_norm(x, g, b):
    xf = x.astype(jnp.float32)
    mu = jnp.mean(xf, axis=-1, keepdims=True)
    var = jnp.mean(jnp.square(xf - mu), axis=-1, keepdims=True)
    return ((xf - mu) * lax.rsqrt(var + LN_EPS) * g + b).astype(x.dtype)


def swiglu(x, w_gate, w_up, w_down):
    return (jax.nn.silu(x @ w_gate) * (x @ w_up)) @ w_down


def masked_softmax(s, mask):
    p = jax.nn.softmax(jnp.where(mask, s.astype(jnp.float32), NEG), axis=-1)
    return p * mask


def short_conv(h, b_gate, c_gate, w_conv):
    z = c_gate * h
    y = lax.conv_general_dilated(
        z, w_conv[:, None, :], window_strides=(1,),
        padding=[(CONV_WIDTH - 1, 0)],
        dimension_numbers=("NWC", "WIO", "NWC"),
        feature_group_count=D_CONV)
    return b_gate * y


def gmlp_spatial(uv, ln_g, ln_b, w_s, b_s):
    bn, s = uv.shape[:2]
    z = jax.nn.gelu(uv)
    u, v = z[..., :D_GMLP], z[..., D_GMLP:]
    v = layer_norm(v, ln_g, ln_b)
    v = v.reshape(bn, s // GMLP_CHUNK, GMLP_CHUNK, GMLP_HEADS, GMLP_HEAD_DIM)
    tril = jnp.tril(jnp.ones((GMLP_CHUNK, GMLP_CHUNK), dtype=bool))
    w = jnp.where(tril[None], w_s, 0.0)
    sv = jnp.einsum("hpq,bcqhd->bcphd", w, v) + b_s.T[:, :, None]
    return u * sv.reshape(bn, s, D_GMLP)


def compress_kv(k, pe, w1, b1, w2):
    s = k.shape[1]
    n_cmp = (s - CMP_LEN) // CMP_STRIDE + 1
    idx = np.arange(n_cmp)[:, None] * CMP_STRIDE + np.arange(CMP_LEN)[None, :]
    kb = k[:, idx]
    h = jax.nn.gelu(jnp.einsum("bnlgd,ldf->bngf", kb + pe[:, None, :], w1) + b1)
    return jnp.einsum("bngf,fe->bnge", h, w2)


def cmp_to_sel_overlap(n_cmp, n_blk):
    st = np.arange(n_cmp) * CMP_STRIDE
    bs = np.arange(n_blk) * SEL_BLOCK
    ov = np.minimum(st[:, None] + CMP_LEN, bs[None, :] + SEL_BLOCK) - np.maximum(st[:, None], bs[None, :])
    return jnp.asarray(np.clip(ov, 0, None) / CMP_LEN, dtype=jnp.float32)


def nsa_attention(q, k_cmp, v_cmp, k_slc, v_slc, k_win, v_win, gate_logits,
                  ck_pe, ck_w1, ck_b1, ck_w2, cv_pe, cv_w1, cv_b1, cv_w2):
    bn, s = q.shape[:2]
    G, R, dh = NSA_KV_HEADS, NSA_GROUP, HEAD_DIM
    q = q.reshape(bn, s, G, R, dh)
    k_cmp, v_cmp, k_slc, v_slc, k_win, v_win = [
        a.reshape(bn, s, G, dh) for a in (k_cmp, v_cmp, k_slc, v_slc, k_win, v_win)]
    scale = dh ** -0.5
    n_cmp = (s - CMP_LEN) // CMP_STRIDE + 1
    n_blk = s // SEL_BLOCK
    n_sel = min(N_SELECT, n_blk)
    t = jnp.arange(s)

    kc = compress_kv(k_cmp, ck_pe, ck_w1, ck_b1, ck_w2)
    vc = compress_kv(v_cmp, cv_pe, cv_w1, cv_b1, cv_w2)
    sc = jnp.einsum("bsgrd,bngd->bgrsn", q, kc) * scale
    cmp_end = jnp.arange(n_cmp) * CMP_STRIDE + CMP_LEN - 1
    pc = masked_softmax(sc, cmp_end[None, :] <= t[:, None])
    o_cmp = jnp.einsum("bgrsn,bngd->bsgrd", pc.astype(vc.dtype), vc)

    imp = jnp.einsum("bgrsn,nj->bgsj", pc, cmp_to_sel_overlap(n_cmp, n_blk))
    j = jnp.arange(n_blk)
    cur = (t // SEL_BLOCK)[:, None]
    valid = j[None, :] <= cur
    forced = (j[None, :] == 0) | (j[None, :] == cur) | (j[None, :] == cur - 1)
    score = jnp.where(valid, imp, NEG)
    score = jnp.where(forced, -NEG, score)
    _, sel = lax.top_k(score, n_sel)

    ks_blk = k_slc.reshape(bn, n_blk, SEL_BLOCK, G, dh).transpose(0, 3, 1, 2, 4)
    vs_blk = v_slc.reshape(bn, n_blk, SEL_BLOCK, G, dh).transpose(0, 3, 1, 2, 4)
    kw_pad = jnp.pad(k_win, ((0, 0), (WINDOW, 0), (0, 0), (0, 0)))
    vw_pad = jnp.pad(v_win, ((0, 0), (WINDOW, 0), (0, 0), (0, 0)))
    bi = jnp.arange(bn)[:, None, None, None]
    gi = jnp.arange(G)[None, :, None, None]

    def query_block(i):
        s0 = i * Q_BLOCK
        qb = lax.dynamic_slice_in_dim(q, s0, Q_BLOCK, axis=1)
        tq = s0 + jnp.arange(Q_BLOCK)
        idx = lax.dynamic_slice_in_dim(sel, s0, Q_BLOCK, axis=2)
        kg = ks_blk[bi, gi, idx]
        vg = vs_blk[bi, gi, idx]
        ss = jnp.einsum("bqgrd,bgqnkd->bgrqnk", qb, kg) * scale
        kpos = idx[..., None] * SEL_BLOCK + jnp.arange(SEL_BLOCK)
        ms = (kpos <= tq[None, None, :, None, None])[:, :, None]
        ps = masked_softmax(ss.reshape(bn, G, R, Q_BLOCK, n_sel * SEL_BLOCK),
                            ms.reshape(bn, G, 1, Q_BLOCK, n_sel * SEL_BLOCK))
        ps = ps.reshape(bn, G, R, Q_BLOCK, n_sel, SEL_BLOCK).astype(vg.dtype)
        o_s = jnp.einsum("bgrqnk,bgqnkd->bqgrd", ps, vg)
        kw = lax.dynamic_slice_in_dim(kw_pad, s0, Q_BLOCK + WINDOW, axis=1)
        vw = lax.dynamic_slice_in_dim(vw_pad, s0, Q_BLOCK + WINDOW, axis=1)
        kp = s0 - WINDOW + jnp.arange(Q_BLOCK + WINDOW)
        dist = tq[:, None] - kp[None, :]
        mw = (dist >= 0) & (dist < WINDOW) & (kp[None, :] >= 0)
        sw = jnp.einsum("bqgrd,bkgd->bgrqk", qb, kw) * scale
        pw = masked_softmax(sw, mw).astype(vw.dtype)
        o_w = jnp.einsum("bgrqk,bkgd->bqgrd", pw, vw)
        return o_s, o_w

    o_s, o_w = lax.map(query_block, jnp.arange(s // Q_BLOCK))
    o_s = o_s.transpose(1, 0, 2, 3, 4, 5).reshape(bn, s, G, R, dh)
    o_w = o_w.transpose(1, 0, 2, 3, 4, 5).reshape(bn, s, G, R, dh)

    g = jax.nn.sigmoid(gate_logits).reshape(bn, s, G, R, 3)
    o = g[..., 0, None] * o_cmp + g[..., 1, None] * o_s + g[..., 2, None] * o_w
    return o.reshape(bn, s, D_NSA)


def setup_inputs(seed: int = 0) -> dict:
    key = jax.random.key(seed)
    keys = iter(jax.random.split(key, 48))
    L = DEPTH

    def nrm(shape, scale):
        return scale * jax.random.normal(next(keys), shape, jnp.float32)

    def gain(shape):
        return 1.0 + nrm(shape, 0.01)

    inp = {}
    inp["x"] = nrm((BATCH, SEQ, D_MODEL), 1.0)
    inp["ffn1_gate"] = nrm((L, D_MODEL, D_FF), D_MODEL ** -0.5)
    inp["ffn1_up"] = nrm((L, D_MODEL, D_FF), D_MODEL ** -0.5)
    inp["ffn1_down"] = nrm((L, D_FF, D_MODEL), BETA * D_FF ** -0.5)
    inp["ln1_g"] = gain((L, D_MODEL))
    inp["ln1_b"] = nrm((L, D_MODEL), 0.01)
    inp["w_in"] = nrm((L, D_MODEL, D_IN_PROJ), D_MODEL ** -0.5)
    inp["conv_w"] = nrm((L, CONV_WIDTH, D_CONV), CONV_WIDTH ** -0.5)
    inp["gmlp_ln_g"] = gain((L, D_GMLP))
    inp["gmlp_ln_b"] = nrm((L, D_GMLP), 0.01)
    inp["gmlp_w"] = nrm((L, GMLP_HEADS, GMLP_CHUNK, GMLP_CHUNK), GMLP_CHUNK ** -0.5)
    inp["gmlp_b"] = gain((L, GMLP_HEADS, GMLP_CHUNK))
    inp["ck_pe"] = nrm((L, CMP_LEN, HEAD_DIM), 0.1)
    inp["ck_w1"] = nrm((L, CMP_LEN, HEAD_DIM, CMP_HIDDEN), (CMP_LEN * HEAD_DIM) ** -0.5)
    inp["ck_b1"] = nrm((L, CMP_HIDDEN), 0.01)
    inp["ck_w2"] = nrm((L, CMP_HIDDEN, HEAD_DIM), CMP_HIDDEN ** -0.5)
    inp["cv_pe"] = nrm((L, CMP_LEN, HEAD_DIM), 0.1)
    inp["cv_w1"] = nrm((L, CMP_LEN, HEAD_DIM, CMP_HIDDEN), (CMP_LEN * HEAD_DIM) ** -0.5)
    inp["cv_b1"] = nrm((L, CMP_HIDDEN), 0.01)
    inp["cv_w2"] = nrm((L, CMP_HIDDEN, HEAD_DIM), CMP_HIDDEN ** -0.5)
    inp["w_out"] = nrm((L, D_MIX, D_MODEL), BETA * D_MIX ** -0.5)
    inp["ln2_g"] = gain((L, D_MODEL))
    inp["ln2_b"] = nrm((L, D_MODEL), 0.01)
    inp["ffn2_gate"] = nrm((L, D_MODEL, D_FF), D_MODEL ** -0.5)
    inp["ffn2_up"] = nrm((L, D_MODEL, D_FF), D_MODEL ** -0.5)
    inp["ffn2_down"] = nrm((L, D_FF, D_MODEL), BETA * D_FF ** -0.5)
    inp["ln3_g"] = gain((L, D_MODEL))
    inp["ln3_b"] = nrm((L, D_MODEL), 0.01)
    return inp


def reference(x, ffn1_gate, ffn1_up, ffn1_down, ln1_g, ln1_b, w_in, conv_w,
              gmlp_ln_g, gmlp_ln_b, gmlp_w, gmlp_b,
              ck_pe, ck_w1, ck_b1, ck_w2, cv_pe, cv_w1, cv_b1, cv_w2,
              w_out, ln2_g, ln2_b, ffn2_gate, ffn2_up, ffn2_down, ln3_g, ln3_b):
    split_points = list(np.cumsum([D_CONV, D_CONV, D_CONV, 2 * D_GMLP, D_NSA,
                                   D_KV, D_KV, D_KV, D_KV, D_KV, D_KV]))
    split_points = [int(p) for p in split_points]
    for l in range(DEPTH):
        x = layer_norm(ALPHA * x + 0.5 * swiglu(x, ffn1_gate[l], ffn1_up[l], ffn1_down[l]),
                       ln1_g[l], ln1_b[l])
        proj = x @ w_in[l]
        (h_a, b_a, c_a, uv, q, kc, vc, ksl, vsl, kw, vw, gl) = jnp.split(proj, split_points, axis=-1)
        y_a = short_conv(h_a, b_a, c_a, conv_w[l])
        y_b = gmlp_spatial(uv, gmlp_ln_g[l], gmlp_ln_b[l], gmlp_w[l], gmlp_b[l])
        y_c = nsa_attention(q, kc, vc, ksl, vsl, kw, vw, gl,
                            ck_pe[l], ck_w1[l], ck_b1[l], ck_w2[l],
                            cv_pe[l], cv_w1[l], cv_b1[l], cv_w2[l])
        mix = jnp.concatenate([y_a, y_b, y_c], axis=-1) @ w_out[l]
        x = layer_norm(ALPHA * x + mix, ln2_g[l], ln2_b[l])
        x = layer_norm(ALPHA * x + 0.5 * swiglu(x, ffn2_gate[l], ffn2_up[l], ffn2_down[l]),
                       ln3_g[l], ln3_b[l])
    return x
```

```python
import numpy as np
from contextlib import ExitStack
import concourse.bass as bass
import concourse.mybir as mybir
from concourse.bass_utils import run_bass_kernel_spmd

F32 = mybir.dt.float32
BF16 = mybir.dt.bfloat16
AF = mybir.ActivationFunctionType
ALU = mybir.AluOpType

D = 1024
S = 4096
L = 2
DFF = 2816
T = 512
NT = S // T
NST = T // 128
ALPHA = (2 * L) ** 0.25
EPS = 1e-5
EPSP = EPS / ALPHA ** 2
SCALE = 64 ** -0.5
NEGM = -30000.0
NFB = 11
NOB = 4
SPW = 412


class Flow:
    ENGS = ("pe", "act", "dve", "pool", "sp")

    def __init__(self, nc, stack):
        self.nc = nc
        self.stack = stack
        self.engs = {}
        for name in self.ENGS:
            sem = stack.enter_context(nc.semaphore(f"s_{name}"))
            self.engs[name] = dict(prog=[], sem=sem, count=0, waited={})
        self.lastw = {}
        self.readers = {}
        self.dsems = {}
        self.n_ins = 0
        self.n_wait = 0

    def _deps(self, eng, reads, writes):
        deps = {}

        def need(tok):
            sem, val, src = tok
            if src == eng and eng in ("pe", "sp"):
                return
            k = id(sem)
            if k not in deps or deps[k][1] < val:
                deps[k] = (sem, val)

        for r in reads:
            t = self.lastw.get(r)
            if t is not None:
                need(t)
        for w in writes:
            t = self.lastw.get(w)
            if t is not None:
                need(t)
            for t in self.readers.get(w, {}).values():
                need(t)
        E = self.engs[eng]
        for k, (sem, val) in deps.items():
            if E["waited"].get(k, 0) < val:
                E["prog"].append(("wait", sem, val))
                E["waited"][k] = val
                self.n_wait += 1

    def _commit(self, tok, reads, writes):
        k = id(tok[0])
        for r in reads:
            self.readers.setdefault(r, {})[k] = tok
        for w in writes:
            self.lastw[w] = tok
            self.readers[w] = {}

    def op(self, eng, fn, reads=(), writes=(), inc=True):
        self._deps(eng, reads, writes)
        E = self.engs[eng]
        if inc:
            E["count"] += 1
            tok = (E["sem"], E["count"], eng)
            E["prog"].append(("ins", fn, E["sem"], 1))
        else:
            tok = (E["sem"], E["count"] + 1, eng)
            E["prog"].append(("ins0", fn))
        self._commit(tok, reads, writes)
        self.n_ins += 1

    def dma(self, eng, fn, reads=(), writes=(), dsem=None):
        if dsem is None:
            dsem = ("d", writes[0] if writes else reads[0])
        dsem = (dsem, eng)
        if dsem not in self.dsems:
            sem = self.stack.enter_context(self.nc.semaphore(f"sd{len(self.dsems)}"))
            self.dsems[dsem] = [sem, 0, set()]
        Dm = self.dsems[dsem]
        self._deps(eng, reads, writes)
        Dm[1] += 16
        tok = (Dm[0], Dm[1], "dma")
        self.engs[eng]["prog"].append(("dma", fn, Dm[0], 1))
        Dm[2].update(writes)
        self._commit(tok, reads, writes)
        self.n_ins += 1

    def seal(self, dsem):
        for key, Dm in self.dsems.items():
            if key[0] != dsem:
                continue
            tok = (Dm[0], Dm[1], "dma")
            for k in Dm[2]:
                if k in self.lastw and self.lastw[k][0] is Dm[0]:
                    self.lastw[k] = tok

    def wait_all_dma(self, eng="sp"):
        E = self.engs[eng]
        for name, Dm in self.dsems.items():
            if Dm[1] > 0:
                E["prog"].append(("wait", Dm[0], Dm[1]))

    def emit(self):
        nc = self.nc
        handles = dict(pe="tensor", act="scalar", dve="vector", pool="gpsimd", sp="sync")
        with nc.Block() as block:
            for name in self.ENGS:
                prog = self.engs[name]["prog"]

                def body(eng, prog=prog):
                    for item in prog:
                        if item[0] == "wait":
                            eng.wait_ge(item[1], item[2])
                        elif item[0] == "ins0":
                            item[1](eng)
                        elif item[0] == "ins":
                            item[1](eng).then_inc(item[2], 1)
                        else:
                            item[1](eng).then_inc(item[2], 16)

                getattr(block, handles[name])(body)


def _win_cols():
    q0 = 1280
    cols = []
    for m in range(4):
        cols += list(range(q0 + m * 64, q0 + (m + 1) * 64)) + list(range(q0 + (m + 4) * 64, q0 + (m + 5) * 64))
    cols += list(range(2048, 2176))
    cols += list(range(2304, 2432))
    cols += list(range(1792, 1856)) + list(range(1920, 1984))
    cols += list(range(1856, 1920)) + list(range(1984, 2048))
    cols += list(range(0, 256))
    cols += list(range(512, 768))
    cols += list(range(256, 512))
    cols += list(range(768, 1024))
    cols += list(range(1024, 1280))
    cols += list(range(2560, 2584)) + [-1] * 104
    cols += [-1] * 128
    cols += list(range(2176, 2304))
    cols += list(range(2432, 2560))
    cols += [-1] * 256
    return np.array(cols)


def _wout_rows():
    rows = list(range(0, 512))
    for m in range(4):
        rows += list(range(512 + m * 64, 512 + (m + 1) * 64)) + list(range(512 + (m + 4) * 64, 512 + (m + 5) * 64))
    return np.array(rows)


def _consts():
    c = {}
    c["IDENT"] = np.eye(128, dtype=np.float32)
    qq = np.arange(128)
    c["TRILT"] = (qq[:, None] <= qq[None, :]).astype(np.float32)
    k = np.arange(128)[:, None, None]
    a = (np.arange(8) - 4)[None, :, None]
    q = np.arange(512)[None, None, :]
    dist = q - 128 * a - k
    c["WM"] = np.where((dist >= 0) & (dist < 512), 0.0, NEGM).astype(np.float32).reshape(128, 8 * 512)
    ip = np.arange(4)[None, :, None]
    c["CM"] = np.where(16 * k + 15 <= 512 * ip + q, 0.0, NEGM).astype(np.float32).reshape(128, 4 * 512)
    j = np.arange(64)[:, None]
    cc = np.arange(4096)[None, :]
    em1 = np.where(j == cc // 64, -NEGM, 0.0).astype(np.float32)
    c["EM"] = np.concatenate([em1, em1], axis=0)
    ql = np.arange(128)[:, None]
    m = np.arange(128)[None, :] - 64
    cur = ql // 64
    rt = np.zeros((128, 128), np.float32)
    rt[(m == cur) | (m == cur - 1)] = 1e30
    rt[m > cur] = -1e30
    c["RT"] = rt
    ov = np.zeros((128, 2, 65), np.float32)
    for ch in range(2):
        for nl in range(128):
            npr = 128 * ch + nl
            if npr == 0:
                continue
            n = npr - 1
            st = 16 * n
            for jb in range(64):
                o = min(st + 32, jb * 64 + 64) - max(st, jb * 64)
                if o > 0:
                    ov[nl, ch, jb] = o / 32.0
            ov[nl, ch, 64] = 1.0
    c["OV"] = ov.reshape(128, 130)
    c["SELG"] = np.eye(24, dtype=np.float32)
    return c


def _pack(inp):
    f = np.float32
    P = {}
    GU = np.empty((L, 2, NFB, 128, 4096), f)
    DN = np.empty((L, 2, NOB, 128, 5632), f)
    for l in range(L):
        for w, nm in enumerate(("ffn1", "ffn2")):
            g = inp[nm + "_gate"][l].reshape(8, 128, NFB, 256).transpose(2, 1, 0, 3)
            u = inp[nm + "_up"][l].reshape(8, 128, NFB, 256).transpose(2, 1, 0, 3)
            GU[l, w] = np.stack([g, u], axis=2).reshape(NFB, 128, 4096)
            DN[l, w] = inp[nm + "_down"][l].reshape(22, 128, NOB, 256).transpose(2, 1, 0, 3).reshape(NOB, 128, 5632)
    P["GU"] = GU.reshape(L * 2 * NFB * 128, 4096)
    P["DN"] = DN.reshape(L * 2 * NOB * 128, 5632)
    cols = _win_cols()
    valid = cols >= 0
    WIN = np.empty((L, 6, 128, 4096), f)
    WOUT = np.empty((L, 2, 128, 4096), f)
    W1 = np.empty((L, 128, 4096), f)
    SP = np.zeros((L, 128, SPW), f)
    GW = np.empty((L, 128, 512), f)
    GB = np.empty((L, 1, 512), f)
    rows = _wout_rows()
    for l in range(L):
        wp = np.zeros((1024, 3072), f)
        wp[:, valid] = inp["w_in"][l][:, cols[valid]]
        WIN[l] = wp.reshape(8, 128, 6, 512).transpose(2, 1, 0, 3).reshape(6, 128, 4096)
        wo = inp["w_out"][l][rows, :]
        WOUT[l] = wo.reshape(8, 128, 2, 512).transpose(2, 1, 0, 3).reshape(2, 128, 4096)
        W1[l] = np.concatenate([inp["ck_w1"][l].transpose(1, 0, 2), inp["cv_w1"][l].transpose(1, 0, 2)], axis=0).reshape(128, 4096)
        for jn, nm in enumerate(("ln1_g", "ln1_b", "ln2_g", "ln2_b", "ln3_g", "ln3_b")):
            SP[l, :, jn * 8:(jn + 1) * 8] = inp[nm][l].reshape(8, 128).T
        SP[l, :, 48:54] = inp["conv_w"][l].reshape(3, 2, 128).transpose(2, 1, 0).reshape(128, 6)
        SP[l, :, 54:56] = inp["gmlp_ln_g"][l].reshape(2, 128).T
        SP[l, :, 56:58] = inp["gmlp_ln_b"][l].reshape(2, 128).T
        SP[l, :, 58] = inp["ck_b1"][l]
        SP[l, :, 59] = inp["cv_b1"][l]
        SP[l, :64, 60:92] = inp["ck_pe"][l].T
        SP[l, 64:, 60:92] = inp["cv_pe"][l].T
        SP[l, :, 92:156] = inp["ck_w2"][l]
        SP[l, :, 284:348] = inp["ck_w2"][l]
        SP[l, :, 348:412] = inp["cv_w2"][l]
        GW[l] = inp["gmlp_w"][l].transpose(2, 0, 1).reshape(128, 512)
        GB[l] = inp["gmlp_b"][l].reshape(1, 512)
    P["WIN"] = WIN.reshape(L * 6 * 128, 4096)
    P["WOUT"] = WOUT.reshape(L * 2 * 128, 4096)
    P["W1"] = W1.reshape(L * 128, 4096)
    P["SP"] = SP.reshape(L * 128, SPW)
    P["GW"] = GW.reshape(L * 128, 512)
    P["GB"] = GB.reshape(L, 512)
    P.update(_consts())
    return P


SHAPES = dict(GU=(L * 2 * NFB * 128, 4096), DN=(L * 2 * NOB * 128, 5632), WIN=(L * 6 * 128, 4096),
              WOUT=(L * 2 * 128, 4096), W1=(L * 128, 4096), SP=(L * 128, SPW), GW=(L * 128, 512), GB=(L, 512),
              IDENT=(128, 128), TRILT=(128, 128), WM=(128, 4096), CM=(128, 2048), EM=(128, 4096), RT=(128, 128),
              OV=(128, 130), SELG=(24, 24))


def build(n_tiles=NT, n_layers=L, dbg=None, stages=99):
    nc = bass.Bass("TRN2", target_bir_lowering=False)
    dram = {k: nc.dram_tensor(k, list(s), F32, kind="ExternalInput").ap() for k, s in SHAPES.items()}
    x_d = nc.dram_tensor("x", [NT * 128, 8 * T], F32, kind="ExternalInput").ap()
    o_d = nc.dram_tensor("out", [NT * 128, 8 * T], F32, kind="ExternalOutput").ap()
    scr = {k: nc.dram_tensor(k + "b", list(SHAPES[k]), BF16, kind="Internal").ap() for k in ("GU", "DN", "WIN", "WOUT", "W1")}
    xs_d = nc.dram_tensor("xs", [NT * 128, 8 * T], F32, kind="Internal").ap()
    dbg_out = {}

    with ExitStack() as st:
        fl = Flow(nc, st)
        _sb = {}

        def sb(name, shape, dt=F32):
            t = st.enter_context(nc.sbuf_tensor(name, list(shape), dt))
            _sb[name] = t
            return t

        psb = [st.enter_context(nc.psum_tensor(f"ps{b}", [128, 512], F32)) for b in range(8)]
        ps_state = dict(n=0, held=set())

        def ps_get(hold=False):
            for _ in range(16):
                b = ps_state["n"] % 8
                ps_state["n"] += 1
                if b not in ps_state["held"]:
                    if hold:
                        ps_state["held"].add(b)
                    return b
            raise RuntimeError("no psum bank")

        def ps_rel(b):
            ps_state["held"].discard(b)

        def PK(b):
            return ("ps", b)

        def dump(name, ap, key, shape, dt=F32):
            if dbg is None or name not in dbg:
                return
            if key == "x32all":
                fl.op("dve", lambda e: e.memset(gj[0][0:1, 0:1], 1.0), reads=X32ALL + [("gj", 0)], writes=["x32all", ("gj", 0)])
            o = nc.dram_tensor("dbg_" + name, list(shape), dt, kind="ExternalOutput").ap()
            dbg_out[name] = o
            fl.dma("pool", lambda e: e.dma_start(out=o, in_=ap), reads=[key], dsem="dbg")

        ident = sb("ident", [128, 128])
        identb = sb("identb", [128, 128], BF16)
        trilt = sb("trilt", [128, 128])
        wm = sb("wm", [128, 8, 512], BF16)
        cm = sb("cm", [128, 4, 512], BF16)
        rt = sb("rt", [128, 128])
        ovt = sb("ovt", [128, 2, 65], BF16)
        selg = sb("selg", [24, 24])
        ones24 = sb("ones24", [24, 128], BF16)
        gj = [sb(f"gj{i}", [24, T], BF16) for i in range(2)]
        ones = sb("ones", [128, 128])
        onesb = sb("onesb", [1, 64], BF16)
        spt = sb("spt", [128, L, SPW])
        w2b = sb("w2b", [128, L, 320], BF16)
        gwm = sb("gwm", [128, L, 4, 128], BF16)
        gbr = sb("gbr", [1, L, 512], BF16)
        cbias = sb("cbias", [128, L, 2])

        def cload(eng, dst, src, key):
            fl.dma(eng, lambda e: e.dma_start(out=dst, in_=src), writes=[key], dsem="const")

        cload("sp", ident[:], dram["IDENT"], "ident")
        cload("sp", trilt[:], dram["TRILT"], "trilt")
        cload("sp", rt[:], dram["RT"], "rt")
        cload("pool", identb[:], dram["IDENT"], "identb")
        cload("pool", wm[:], dram["WM"].rearrange("p (a q) -> p a q", a=8), "wm")
        cload("pool", cm[:], dram["CM"].rearrange("p (a q) -> p a q", a=4), "cm")
        cload("pool", ovt[:], dram["OV"].rearrange("p (a q) -> p a q", a=2), "ovt")
        cload("sp", selg[:], dram["SELG"], "selg")
        for l in range(L):
            cload("sp", spt[:, l, :], dram["SP"][l * 128:(l + 1) * 128, :], "spt")
            cload("pool", w2b[:, l, :], dram["SP"][l * 128:(l + 1) * 128, 92:412], "w2b")
            cload("pool", gbr[:, l, :], dram["GB"][l:l + 1, :], "gbr")
        fl.seal("const")
        fl.op("dve", lambda e: e.memset(ones[:], 1.0), writes=["ones"])
        fl.op("dve", lambda e: e.memset(onesb[:], 1.0), writes=["onesb"])
        fl.op("dve", lambda e: e.memset(ones24[:], 1.0), writes=["ones24"])

        pc_groups = []

        def precast(name, r0, r1, grp):
            if grp == (0, "f1"):
                grp = (0, "f1", name, r0)
            if grp not in pc_groups:
                pc_groups.append(grp)
            fl.dma("pool", lambda e: e.dma_start(out=scr[name][r0:r1, :], in_=dram[name][r0:r1, :]),
                   writes=[("pc", grp)], dsem=("pc", grp))

        pc_pending = []
        for l in range(n_layers):
            for fbk in range(NFB):
                r = ((l * 2 + 0) * NFB + fbk) * 128
                pc_pending.append(("GU", r, r + 128, (l, "f1")))
            for ob in range(NOB):
                r = ((l * 2 + 0) * NOB + ob) * 128
                pc_pending.append(("DN", r, r + 128, (l, "f1")))
            for u in range(6):
                r = (l * 6 + u) * 128
                pc_pending.append(("WIN", r, r + 128, (l, "mix")))
            pc_pending.append(("W1", l * 128, (l + 1) * 128, (l, "mix")))
            for u in range(2):
                r = (l * 2 + u) * 128
                pc_pending.append(("WOUT", r, r + 128, (l, "mix")))
            for fbk in range(NFB):
                r = ((l * 2 + 1) * NFB + fbk) * 128
                pc_pending.append(("GU", r, r + 128, (l, "f2")))
            for ob in range(NOB):
                r = ((l * 2 + 1) * NOB + ob) * 128
                pc_pending.append(("DN", r, r + 128, (l, "f2")))
        pc_l1 = [p for p in pc_pending if p[3][0] == 1]
        pc_l0 = [p for p in pc_pending if p[3][0] == 0]
        def emit_pc(items):
            grps = []
            for (name, r0, r1, grp) in items:
                precast(name, r0, r1, grp)
            for g_ in list(pc_groups):
                fl.seal(("pc", g_))

        gwf = sb("gwf", [128, 4, 128])
        for l in range(n_layers):
            fl.dma("sp", lambda e, l=l: e.dma_start(out=gwf[:], in_=dram["GW"][l * 128:(l + 1) * 128, :].rearrange("p (h q) -> p h q", h=4)),
                   writes=["gwf"])
            for hh in range(4):
                fl.op("dve", lambda e, l=l, hh=hh: e.tensor_tensor(out=gwm[:, l, hh, :], in0=gwf[:, hh, :], in1=trilt[:], op=ALU.mult),
                      reads=["gwf", "trilt"], writes=["gwm"])

        NA, NB = 3, 2
        ringA = [sb(f"ringA{i}", [128, 4096], BF16) for i in range(NA)]
        ringB = [sb(f"ringB{i}", [128, 12, 256], BF16) for i in range(NB)]
        rstate = dict(a=0, b=0)

        def streamA(name, row0, grp):
            s_ = rstate["a"] % NA
            rstate["a"] += 1
            t = ringA[s_]
            if grp == (0, "f1"):
                grp = (0, "f1", name, row0)
            fl.dma("sp", lambda e: e.dma_start(out=t[:], in_=scr[name][row0:row0 + 128, :]),
                   reads=[("pc", grp)], writes=[("rA", s_)])
            return t, ("rA", s_)

        def streamB(row0, grp, fc0, nfc):
            s_ = rstate["b"] % NB
            rstate["b"] += 1
            t = ringB[s_]
            if grp == (0, "f1"):
                grp = (0, "f1", "DN", row0)
            fl.dma("sp", lambda e: e.dma_start(out=t[:, 0:nfc, :], in_=scr["DN"][row0:row0 + 128, fc0 * 256:(fc0 + nfc) * 256].rearrange("p (f c) -> p f c", f=nfc)),
                   reads=[("pc", grp)], writes=[("rB", s_)])
            return t, ("rB", s_)

        x32 = sb("x32", [128, 8, T])
        X32ALL = [("x32", m_) for m_ in range(8)]
        XBALL = [("xb", m_) for m_ in range(8)]
        xb = sb("xb", [128, 8, T], BF16)
        hbuf = sb("hbuf", [128, 12, T], BF16)
        NSCR = 7
        scrp = [sb(f"scrp{i}", [128, T]) for i in range(NSCR)]
        scr_n = dict(n=0)

        def scr_get():
            k_ = scr_n["n"] % NSCR
            scr_n["n"] += 1
            return scrp[k_], ("scrp", k_)
        mean = sb("mean", [128, T])
        rstd = sb("rstd", [128, T])
        ybuf = sb("ybuf", [128, 8, T], BF16)
        YKEYS = [[("ybuf", 0)], [("ybuf", 1)], [("ybuf", 2, 0), ("ybuf", 2, 1)], [("ybuf", 3, 0), ("ybuf", 3, 1)]] + [[("yc", m_), ("yc", m_ + 4)] for m_ in range(4)]
        YBALL = [k_ for ks_ in YKEYS[:4] for k_ in ks_]
        qs = [sb(f"qs{h}", [128, T], BF16) for h in range(8)]
        g24 = sb("g24", [24, T], BF16)
        ha = sb("ha", [128, 2, T], BF16)
        ub = sb("ub", [128, 2, T], BF16)
        vg = sb("vg", [128, 2, T])
        vnT = sb("vnT", [128, NST, 256], BF16)
        pt = [sb(f"pt{i}", [128, T], BF16) for i in range(4)]
        kb = sb("kb", [128, 32, 64], BF16)
        impa = sb("impa", [128, NST, 64])
        rd4 = sb("rd4", [128, NST, 4])
        m8a = sb("m8a", [128, NST, 8])
        m8b = sb("m8b", [128, NST, 8])
        sc2 = sb("sc2", [128, NST, 64])
        selm = sb("selm", [128, 2 * NST, 64])
        impsb = sb("impsb", [128, NST // 2, 512])
        state = []
        for l in range(1):
            stl = dict(
                ke=[sb(f"ke{l}_{g}", [128, S], BF16) for g in range(2)],
                vsl=sb(f"vsl{l}", [128, S // 128, 192], BF16),
                kw=sb(f"kw{l}", [128, 2, T], BF16),
                vw=sb(f"vw{l}", [128, 2 * NST, 192], BF16),
                kcvc=[sb(f"kcvc{l}_{g}", [128, T + 16], BF16) for g in range(2)],
                ht=sb(f"ht{l}", [128, 2, 2, 256], BF16),
                kct=sb(f"kct{l}", [128, 256], BF16),
                vce=sb(f"vce{l}", [128, 2, 192], BF16),
                zc=sb(f"zc{l}", [128, 2, T + 2]),
            )
            state.append(stl)
            cload("pool", stl["ke"][0][64:128, :], dram["EM"][64:128, :], ("ke", 0))
            cload("pool", stl["ke"][1][0:64, :], dram["EM"][0:64, :], ("ke", 1))
            fl.seal("const")
            for nm in ("vsl", "vw", "vce"):
                t_ = stl[nm]
                fl.op("pool", lambda e, t_=t_: e.memset(t_[:], 1.0), writes=[(nm, l)])
            fl.op("pool", lambda e, t_=stl["vce"]: e.memset(t_[0:1, 0, :], 0.0), writes=[("vce", l)])
            fl.op("pool", lambda e, t_=stl["ht"]: e.memset(t_[:], 0.0), writes=[("ht", l)])
            fl.op("pool", lambda e, t_=stl["kct"]: e.memset(t_[:], 0.0), writes=[("kct", l)])
            for g in range(2):
                fl.op("pool", lambda e, t_=stl["kcvc"][g]: e.memset(t_[:], 0.0), writes=[("kcvc", l, g)])
            fl.op("pool", lambda e, t_=stl["zc"]: e.memset(t_[:], 0.0), writes=[("zc", 0), ("zc", 1)])

        import os as _os
        DENSE = bool(_os.environ.get("DENSE"))

        def mm(out_ap, lhsT, rhs, start, stop, reads, wkey, chain=False):
            fl.op("pe", lambda e: e.matmul(out_ap, lhsT=lhsT, rhs=rhs, start=start, stop=stop), reads=reads, writes=[wkey], inc=(bool(stop) or not chain or DENSE))

        sqb = [sb(f"sqb{i}", [128, T], BF16) for i in range(3)]
        sqb_n = dict(n=0)
        ones16 = sb("ones16", [128, 128], BF16)
        fl.op("dve", lambda e: e.memset(ones16[:], 1.0), writes=["ones16"])

        def ln_begin():
            return dict(b1=ps_get(hold=True), b2=ps_get(hold=True), n=0)

        def ln_feed(st_, m):
            b1, b2, k = st_["b1"], st_["b2"], st_["n"]
            mm(psb[b1][:, :], ones[:], x32[:, m, :], k == 0, k == 7, ["ones", ("x32", m)], PK(b1))
            q_ = sqb_n["n"] % 3
            sqb_n["n"] += 1
            fl.op("act", lambda e: e.activation(out=sqb[q_][:], in_=x32[:, m, :], func=AF.Square), reads=[("x32", m)], writes=[("sqb", q_)])
            mm(psb[b2][:, :], ones16[:], sqb[q_][:], k == 0, k == 7, ["ones16", ("sqb", q_)], PK(b2))
            st_["n"] += 1

        def ln_finish(l, j, st_, last_stage=False):
            b1, b2 = st_["b1"], st_["b2"]
            assert st_["n"] == 8
            fl.op("dve", lambda e: e.tensor_scalar(out=mean[:], in0=psb[b1][:, :], scalar1=1.0 / D, scalar2=None, op0=ALU.mult),
                  reads=[PK(b1)], writes=["mean"])
            fl.op("dve", lambda e: e.tensor_tensor(out=rstd[:], in0=mean[:], in1=mean[:], op=ALU.mult), reads=["mean"], writes=["rstd"])
            fl.op("dve", lambda e: e.scalar_tensor_tensor(out=rstd[:], in0=psb[b2][:, :], scalar=1.0 / D, in1=rstd[:], op0=ALU.mult, op1=ALU.subtract),
                  reads=[PK(b2), "rstd"], writes=["rstd"])
            ps_rel(b1)
            ps_rel(b2)
            fl.op("dve", lambda e: e.tensor_scalar(out=rstd[:], in0=rstd[:], scalar1=EPSP, scalar2=None, op0=ALU.add), reads=["rstd"], writes=["rstd"])
            fl.op("act", lambda e: e.activation(out=rstd[:], in_=rstd[:], func=AF.Ln), reads=["rstd"], writes=["rstd"])
            fl.op("act", lambda e: e.activation(out=rstd[:], in_=rstd[:], func=AF.Exp, scale=-0.5), reads=["rstd"], writes=["rstd"])
            for m0 in range(0, 8, 2):
                pr = [(m0, ) + scr_get(), (m0 + 1, ) + scr_get()]
                for (m, t_, tk) in pr:
                    fl.op("dve", lambda e, t_=t_, m=m: e.tensor_tensor(out=t_[:], in0=x32[:, m, :], in1=mean[:], op=ALU.subtract),
                          reads=[("x32", m), "mean"], writes=[tk])
                for (m, t_, tk) in pr:
                    fl.op("dve", lambda e, t_=t_: e.tensor_tensor(out=t_[:], in0=t_[:], in1=rstd[:], op=ALU.mult),
                          reads=[tk, "rstd"], writes=[tk])
                for (m, t_, tk) in pr:
                    gcol = 2 * j * 8 + m
                    bcol = (2 * j + 1) * 8 + m
                    if last_stage:
                        fl.op("act", lambda e, t_=t_, m=m, gcol=gcol, bcol=bcol: e.activation(
                            out=x32[:, m, :], in_=t_[:], func=AF.Identity, bias=spt[:, l, bcol:bcol + 1], scale=spt[:, l, gcol:gcol + 1]),
                            reads=[tk, "spt"], writes=[("x32", m)])
                    else:
                        fl.op("act", lambda e, t_=t_, m=m, gcol=gcol, bcol=bcol: e.activation(
                            out=xb[:, m, :], in_=t_[:], func=AF.Identity, bias=spt[:, l, bcol:bcol + 1], scale=spt[:, l, gcol:gcol + 1]),
                            reads=[tk, "spt"], writes=[("xb", m)])
                for (m, t_, tk) in pr:
                    gcol = 2 * j * 8 + m
                    bcol = (2 * j + 1) * 8 + m
                    if last_stage:
                        continue
                    fl.op("pool", lambda e, t_=t_, m=m, gcol=gcol, bcol=bcol: e.tensor_scalar(
                        out=x32[:, m, :], in0=t_[:], scalar1=spt[:, l, gcol:gcol + 1], scalar2=spt[:, l, bcol:bcol + 1], op0=ALU.mult, op1=ALU.add),
                        reads=[tk, "spt"], writes=[("x32", m)])

        def ffn(l, w):
            grp = (l, "f1" if w == 0 else "f2")
            for hf, (fb0, fb1) in enumerate(((0, 6), (6, NFB))):
                for fbk in range(fb0, fb1):
                    slot, skey = streamA("GU", ((l * 2 + w) * NFB + fbk) * 128, grp)
                    sv = slot[:].rearrange("p (a d c) -> p a d c", a=2, d=8)
                    for half in range(2):
                        fcl = (fbk - fb0) * 2 + half
                        bg = ps_get()
                        bu = ps_get()
                        for dc in range(8):
                            mm(psb[bg][:, :], sv[:, 0, dc, half * 128:(half + 1) * 128], xb[:, dc, :], dc == 0, dc == 7, [skey, ("xb", dc)], PK(bg), chain=True)
                        for dc in range(8):
                            mm(psb[bu][:, :], sv[:, 1, dc, half * 128:(half + 1) * 128], xb[:, dc, :], dc == 0, dc == 7, [skey, ("xb", dc)], PK(bu), chain=True)
                        s_, sk = scr_get()
                        fl.op("act", lambda e, s_=s_, bg=bg: e.activation(out=s_[:], in_=psb[bg][:, :], func=AF.Silu), reads=[PK(bg)], writes=[sk])
                        fl.op("dve", lambda e, s_=s_, bu=bu, fcl=fcl: e.tensor_tensor(out=hbuf[:, fcl, :], in0=psb[bu][:, :], in1=s_[:], op=ALU.mult),
                              reads=[PK(bu), sk], writes=[("hbuf", fcl)])
                nfc = (fb1 - fb0) * 2
                if hf == 1 and w == 1 and NEXT_TILE[0] is not None:
                    prefetch_xb(*NEXT_TILE[0])
                lns = ln_begin() if hf == 1 else None
                pend_m = None
                for ob in range(NOB):
                    slot, skey = streamB(((l * 2 + w) * NOB + ob) * 128, grp, fb0 * 2, nfc)
                    for half in range(2):
                        m = ob * 2 + half
                        by = ps_get()
                        for fc in range(nfc):
                            mm(psb[by][:, :], slot[:, fc, half * 128:(half + 1) * 128], hbuf[:, fc, :], fc == 0, fc == nfc - 1, [skey, ("hbuf", fc)], PK(by), chain=True)
                        if lns is not None and pend_m is not None:
                            ln_feed(lns, pend_m)
                        pend_m = m
                        fl.op("dve", lambda e, by=by, m=m: e.scalar_tensor_tensor(out=x32[:, m, :], in0=psb[by][:, :], scalar=0.5 / ALPHA, in1=x32[:, m, :],
                                                                                    op0=ALU.mult, op1=ALU.add), reads=[PK(by), ("x32", m)], writes=[("x32", m)])
            ln_feed(lns, pend_m)
            ln_finish(l, 0 if w == 0 else 2, lns, last_stage=(w == 1))

        def x_src(l):
            return x_d if l == 0 else xs_d

        def prefetch_xb(l, i):
            src = x_src(l)
            rd = [("xs", i)] if l > 0 else []
            fl.dma("pool", lambda e: e.dma_start(out=xb[:].rearrange("p m t -> p (m t)"), in_=src[i * 128:(i + 1) * 128, :]), reads=rd, writes=XBALL, dsem="xbf")

        def load_x32(l, i):
            src = x_src(l)
            rd = [("xs", i)] if l > 0 else []
            fl.dma("pool", lambda e: e.dma_start(out=x32[:].rearrange("p m t -> p (m t)"), in_=src[i * 128:(i + 1) * 128, :]), reads=rd, writes=X32ALL, dsem="xf")

        def store_x(i):
            fl.dma("pool", lambda e: e.dma_start(out=o_d[i * 128:(i + 1) * 128, :], in_=x32[:].rearrange("p m t -> p (m t)")), reads=X32ALL, dsem="out")

        def spill_x(i):
            fl.dma("pool", lambda e: e.dma_start(out=xs_d[i * 128:(i + 1) * 128, :], in_=x32[:].rearrange("p m t -> p (m t)")), reads=X32ALL, writes=[("xs", i)], dsem="xs")

        def mixer(l, i):
            stl = state[0]
            grp = (l, "mix")
            ke, vsl, kw, vw, kcvc, ht, kct, vce = (stl[k] for k in ("ke", "vsl", "kw", "vw", "kcvc", "ht", "kct", "vce"))
            wslot = i % 2
            zc = stl["zc"]

            def inproj_chunk(slot, skey, c, M=128):
                b = ps_get()
                sv = slot[:].rearrange("p (d c) -> p d c", d=8)
                for dc in range(8):
                    mm(psb[b][0:M, :], sv[:, dc, c * 128:c * 128 + M], xb[:, dc, :], dc == 0, dc == 7, [skey, ("xb", dc)], PK(b), chain=True)
                return b

            if i == 0:
                fl.op("pool", lambda e: e.memset(zc[:, :, 0:2], 0.0), writes=[("zc", 0), ("zc", 1)])
                for g in range(2):
                    fl.op("pool", lambda e, g=g: e.memset(kcvc[g][:, 0:16], 0.0), writes=[("kcvc", l, g)])
            else:
                for g in range(2):
                    fl.op("pool", lambda e, g=g: e.tensor_copy(out=kcvc[g][:, 0:16], in_=kcvc[g][:, T:T + 16]), reads=[("kcvc", l, g)], writes=[("kcvc", l, g)])
            slot, skey = streamA("WIN", (l * 6 + 0) * 128, grp)
            for c in range(4):
                b = inproj_chunk(slot, skey, c)
                fl.op("act", lambda e, b=b, c=c: e.copy(out=qs[c][0:64, :], in_=psb[b][0:64, :]), reads=[PK(b)], writes=[("qs", c)])
                fl.op("act", lambda e, b=b, c=c: e.copy(out=qs[c + 4][64:128, :], in_=psb[b][64:128, :]), reads=[PK(b)], writes=[("qs", c + 4)])
            slot, skey = streamA("WIN", (l * 6 + 1) * 128, grp)
            b = inproj_chunk(slot, skey, 0)
            fl.op("act", lambda e, b=b: e.copy(out=ke[0][0:64, i * T:(i + 1) * T], in_=psb[b][0:64, :]), reads=[PK(b)], writes=[("ke", 0)])
            fl.op("act", lambda e, b=b: e.copy(out=ke[1][64:128, i * T:(i + 1) * T], in_=psb[b][64:128, :]), reads=[PK(b)], writes=[("ke", 1)])
            b = inproj_chunk(slot, skey, 1)
            fl.op("dve", lambda e, b=b: e.tensor_copy(out=kw[:, wslot, :], in_=psb[b][:, :]), reads=[PK(b)], writes=[("kw", l)])
            for g in range(2):
                b = inproj_chunk(slot, skey, 2 + g)
                fl.op("act", lambda e, b=b, g=g: e.copy(out=kcvc[g][:, 16:16 + T], in_=psb[b][:, :]), reads=[PK(b)], writes=[("kcvc", l, g)])
            slot, skey = streamA("WIN", (l * 6 + 2) * 128, grp)
            for cc in range(2):
                b = inproj_chunk(slot, skey, cc)
                fl.op("act", lambda e, b=b, cc=cc: e.copy(out=ha[:, cc, :], in_=psb[b][:, :]), reads=[PK(b)], writes=[("ha", cc)])
            for cc in range(2):
                b = inproj_chunk(slot, skey, 2 + cc)
                fl.op("dve", lambda e, b=b, cc=cc: e.tensor_tensor(out=zc[:, cc, 2:2 + T], in0=psb[b][:, :], in1=ha[:, cc, :], op=ALU.mult),
                      reads=[PK(b), ("ha", cc)], writes=[("zc", cc)])
            slot, skey = streamA("WIN", (l * 6 + 3) * 128, grp)
            for cc in range(2):
                b = inproj_chunk(slot, skey, cc)
                w0 = spt[:, l, 48 + cc * 3 + 0:48 + cc * 3 + 1]
                w1_ = spt[:, l, 48 + cc * 3 + 1:48 + cc * 3 + 2]
                w2_ = spt[:, l, 48 + cc * 3 + 2:48 + cc * 3 + 3]
                cacc, ck_ = scr_get()
                fl.op("dve", lambda e, cc=cc, w2_=w2_, cacc=cacc: e.tensor_scalar(out=cacc[:], in0=zc[:, cc, 2:2 + T], scalar1=w2_, scalar2=None, op0=ALU.mult),
                      reads=[("zc", cc), "spt"], writes=[ck_])
                fl.op("dve", lambda e, cc=cc, w1_=w1_, cacc=cacc: e.scalar_tensor_tensor(out=cacc[:], in0=zc[:, cc, 1:1 + T], scalar=w1_, in1=cacc[:], op0=ALU.mult, op1=ALU.add),
                      reads=[("zc", cc), "spt", ck_], writes=[ck_])
                fl.op("dve", lambda e, cc=cc, w0=w0, cacc=cacc: e.scalar_tensor_tensor(out=cacc[:], in0=zc[:, cc, 0:T], scalar=w0, in1=cacc[:], op0=ALU.mult, op1=ALU.add),
                      reads=[("zc", cc), "spt", ck_], writes=[ck_])
                fl.op("dve", lambda e, b=b, cc=cc, cacc=cacc: e.tensor_tensor(out=ybuf[:, cc, :], in0=psb[b][:, :], in1=cacc[:], op=ALU.mult),
                      reads=[PK(b), ck_], writes=[("ybuf", cc)])
                fl.op("pool", lambda e, cc=cc: e.tensor_copy(out=zc[:, cc, 0:2], in_=zc[:, cc, T:T + 2]), reads=[("zc", cc)], writes=[("zc", cc)])
            for cc in range(2):
                b = inproj_chunk(slot, skey, 2 + cc)
                fl.op("act", lambda e, b=b, cc=cc: e.activation(out=ub[:, cc, :], in_=psb[b][:, :], func=AF.Gelu_apprx_tanh), reads=[PK(b)], writes=[("ub", cc)])
            slot, skey = streamA("WIN", (l * 6 + 4) * 128, grp)
            for cc in range(2):
                b = inproj_chunk(slot, skey, cc)
                fl.op("act", lambda e, b=b, cc=cc: e.activation(out=vg[:, cc, :], in_=psb[b][:, :], func=AF.Gelu_apprx_tanh), reads=[PK(b)], writes=[("vg", cc)])
            b = inproj_chunk(slot, skey, 2, M=24)
            fl.op("act", lambda e, b=b: e.activation(out=g24[:], in_=psb[b][0:24, :], func=AF.Sigmoid), reads=[PK(b)], writes=["g24"])
            slot, skey = streamA("WIN", (l * 6 + 5) * 128, grp)
            sv5 = slot[:].rearrange("p (d c) -> p d c", d=8)
            for stn in range(NST):
                b = ps_get()
                for dc in range(8):
                    mm(psb[b][:, 0:256], xb[:, dc, stn * 128:(stn + 1) * 128], sv5[:, dc, 0:256], dc == 0, dc == 7, [skey, ("xb", dc)], PK(b))
                kt = i * NST + stn
                wk = wslot * NST + stn
                fl.op("act", lambda e, b=b, kt=kt: e.copy(out=vsl[:, kt, 0:64], in_=psb[b][:, 0:64]), reads=[PK(b)], writes=[("vsl", l)])
                fl.op("act", lambda e, b=b, kt=kt: e.copy(out=vsl[:, kt, 128:192], in_=psb[b][:, 64:128]), reads=[PK(b)], writes=[("vsl", l)])
                fl.op("dve", lambda e, b=b, wk=wk: e.tensor_copy(out=vw[:, wk, 0:64], in_=psb[b][:, 128:192]), reads=[PK(b), ("vsl", l)], writes=[("vw", l)])
                fl.op("dve", lambda e, b=b, wk=wk: e.tensor_copy(out=vw[:, wk, 128:192], in_=psb[b][:, 192:256]), reads=[PK(b), ("vsl", l)], writes=[("vw", l)])
            dump(f"qt_{l}_{i}", qs[0][:], ("qs", 0), [128, T], BF16)
            if i < 2:
                for h_ in range(8):
                    g_ = h_ // 4
                    fl.op("pool", lambda e, h_=h_, g_=g_: e.memset(qs[h_][(1 - g_) * 64:(1 - g_) * 64 + 64, :], 0.0), writes=[("qs", h_)])

            if stages < 2.1:
                return
            import os
            SKIPG = os.environ.get("SKIPG", "")
            b1 = ps_get()
            b2 = ps_get()
            for cc in range(2):
                mm(psb[b1][:, :], ones[:], vg[:, cc, :], cc == 0, cc == 1, ["ones", ("vg", cc)], PK(b1))
            for cc in range(2):
                s_, sk = scr_get()
                fl.op("act", lambda e, s_=s_, cc=cc: e.activation(out=s_[:], in_=vg[:, cc, :], func=AF.Square), reads=[("vg", cc)], writes=[sk])
                mm(psb[b2][:, :], ones[:], s_[:], cc == 0, cc == 1, ["ones", sk], PK(b2))
            fl.op("dve", lambda e: e.tensor_scalar(out=mean[:], in0=psb[b1][:, :], scalar1=1.0 / 256, scalar2=None, op0=ALU.mult), reads=[PK(b1)], writes=["mean"])
            fl.op("pool", lambda e: e.tensor_tensor(out=rstd[:], in0=mean[:], in1=mean[:], op=ALU.mult), reads=["mean"], writes=["rstd"])
            fl.op("dve", lambda e: e.scalar_tensor_tensor(out=rstd[:], in0=psb[b2][:, :], scalar=1.0 / 256, in1=rstd[:], op0=ALU.mult, op1=ALU.subtract),
                  reads=[PK(b2), "rstd"], writes=["rstd"])
            fl.op("dve", lambda e: e.tensor_scalar(out=rstd[:], in0=rstd[:], scalar1=EPS, scalar2=None, op0=ALU.add), reads=["rstd"], writes=["rstd"])
            fl.op("act", lambda e: e.activation(out=rstd[:], in_=rstd[:], func=AF.Ln), reads=["rstd"], writes=["rstd"])
            fl.op("act", lambda e: e.activation(out=rstd[:], in_=rstd[:], func=AF.Exp, scale=-0.5), reads=["rstd"], writes=["rstd"])
            for cc in range(2):
                fl.op("dve", lambda e, cc=cc: e.tensor_tensor(out=vg[:, cc, :], in0=vg[:, cc, :], in1=mean[:], op=ALU.subtract), reads=[("vg", cc), "mean"], writes=[("vg", cc)])
                fl.op("dve", lambda e, cc=cc: e.tensor_tensor(out=vg[:, cc, :], in0=vg[:, cc, :], in1=rstd[:], op=ALU.mult), reads=[("vg", cc), "rstd"], writes=[("vg", cc)])
                fl.op("act", lambda e, cc=cc: e.activation(out=vg[:, cc, :], in_=vg[:, cc, :], func=AF.Identity, bias=spt[:, l, 56 + cc:57 + cc], scale=spt[:, l, 54 + cc:55 + cc]),
                      reads=[("vg", cc), "spt"], writes=[("vg", cc)])
            for stn in range(NST if "t" not in SKIPG else 0):
                b = ps_get()
                for cc in range(2):
                    fl.op("pe", lambda e, b=b, cc=cc, stn=stn: e.transpose(out=psb[b][:, cc * 128:(cc + 1) * 128], in_=vg[:, cc, stn * 128:(stn + 1) * 128], identity=ident[:]),
                          reads=[("vg", cc), "ident"], writes=[PK(b)])
                fl.op("act", lambda e, b=b, stn=stn: e.copy(out=vnT[:, stn, :], in_=psb[b][:, 0:256]), reads=[PK(b)], writes=["vnT"])
            for hd in range(4 if "h" not in SKIPG else 0):
                cc, hh = hd // 2, hd % 2
                b = ps_get()
                for stn in range(NST):
                    mm(psb[b][0:64, stn * 128:(stn + 1) * 128], vnT[:, stn, hd * 64:(hd + 1) * 64], gwm[:, l, hd, :], True, False, ["vnT", "gwm"], PK(b))
                    mm(psb[b][0:64, stn * 128:(stn + 1) * 128], onesb[0:1, :], gbr[0:1, l, hd * 128:(hd + 1) * 128], False, True, ["onesb", "gbr"], PK(b))
                fl.op("dve", lambda e, b=b, cc=cc, hh=hh: e.tensor_tensor(out=ybuf[hh * 64:(hh + 1) * 64, 2 + cc, :], in0=psb[b][0:64, :], in1=ub[hh * 64:(hh + 1) * 64, cc, :], op=ALU.mult),
                      reads=[PK(b), ("ub", cc)], writes=[("ybuf", 2 + cc, hh)])

            if stages < 2.2:
                return
            w1slot, w1key = streamA("W1", l * 128, grp)
            w1v = w1slot[:].rearrange("p (a f) -> p a f", a=32)
            import os
            if i == 0:
                bpe = [ps_get(), ps_get()]
                for kv in range(2):
                    for lp in range(32):
                        mm(psb[bpe[kv]][:, 0:1], w1v[kv * 64:(kv + 1) * 64, lp, :], spt_b[kv * 64:(kv + 1) * 64, l, lp:lp + 1], lp == 0, lp == 31, [w1key, "sptb"], PK(bpe[kv]), chain=True)
                for kv in range(2):
                    fl.op("dve", lambda e, kv=kv: e.tensor_tensor(out=cbias[:, l, kv:kv + 1], in0=psb[bpe[kv]][:, 0:1], in1=spt[:, l, 58 + kv:59 + kv], op=ALU.add),
                          reads=[PK(bpe[kv]), "spt"], writes=["cbias"])
            for g in range(2):
                for lp in range(32):
                    fl.op("dve", lambda e, g=g, lp=lp: e.tensor_copy(out=kb[:, lp, g * 32:(g + 1) * 32], in_=kcvc[g][:, lp:lp + 497:16]),
                          reads=[("kcvc", l, g)], writes=[("kb", g, lp)])
            bkv = [ps_get(), ps_get()]
            for kv in range(2):
                for lp in range(32):
                    mm(psb[bkv[kv]][:, 0:64], w1v[kv * 64:(kv + 1) * 64, lp, :], kb[kv * 64:(kv + 1) * 64, lp, :], lp == 0, lp == 31,
                       [w1key, ("kb", 0, lp), ("kb", 1, lp)], PK(bkv[kv]), chain=True)
            for kv in range(2):
                fl.op("act", lambda e, kv=kv: e.activation(out=ht[:, kv, :, i * 32:(i + 1) * 32], in_=psb[bkv[kv]][:, 0:64].rearrange("p (g n) -> p g n", g=2),
                                                             func=AF.Gelu_apprx_tanh, bias=cbias[:, l, kv:kv + 1]), reads=[PK(bkv[kv]), "cbias"], writes=[("ht", l)])
            b = ps_get()
            for g in range(2):
                mm(psb[b][:, 0:32], w2b[:, l, g * 128:(g + 1) * 128], ht[:, 0, g, i * 32:(i + 1) * 32], g == 0, g == 1, ["w2b", ("ht", l)], PK(b))
            fl.op("act", lambda e, b=b: e.copy(out=kct[:, i * 32:(i + 1) * 32], in_=psb[b][:, 0:32]), reads=[PK(b)], writes=[("kct", l)])
            cch = i // 4
            b = ps_get()
            for g in range(2):
                mm(psb[b][:, g * 64:(g + 1) * 64], ht[:, 1, g, cch * 128:(cch + 1) * 128], w2b[:, l, 256:320], True, True, [("ht", l), "w2b"], PK(b))
            fl.op("act", lambda e, b=b: e.copy(out=vce[:, cch, 0:64], in_=psb[b][:, 0:64]), reads=[PK(b)], writes=[("vce", l)])
            fl.op("act", lambda e, b=b: e.copy(out=vce[:, cch, 128:192], in_=psb[b][:, 64:128]), reads=[PK(b)], writes=[("vce", l)])
            if cch == 0:
                fl.op("pool", lambda e: e.memset(vce[0:1, 0, :], 0.0), writes=[("vce", l)])
            dump(f"kct_{l}_{i}", kct[:], ("kct", l), [128, 256], BF16)

            if stages < 2.3:
                return
            def gate_bcast(h, br):
                g = h // 4
                D0 = (1 - g) * 64
                bgt = ps_get()
                jcol = h * 3 + br
                k_ = (h * 3 + br) % 2
                fl.op("dve", lambda e: e.tensor_scalar(out=gj[k_][:], in0=g24[:], scalar1=selg[:, jcol:jcol + 1], scalar2=None, op0=ALU.mult),
                      reads=["g24", "selg"], writes=[("gj", k_)])
                mm(psb[bgt][:, :], ones24[:], gj[k_][:], True, True, ["ones24", ("gj", k_)], PK(bgt))
                gs_, gk = scr_get()
                fl.op("act", lambda e: e.copy(out=gs_[D0:D0 + 64, :], in_=psb[bgt][D0:D0 + 64, :]), reads=[PK(bgt)], writes=[gk])
                return gs_, gk

            def combine_multi(h, items):
                g, m = h // 4, h % 4
                N0, D0 = g * 64, (1 - g) * 64
                st_ = []
                for (br, acc, first) in items:
                    gs_, gk = gate_bcast(h, br)
                    cf_, ck = scr_get()
                    st_.append((br, acc, first, gs_, gk, cf_, ck))
                for (br, acc, first, gs_, gk, cf_, ck) in st_:
                    fl.op("dve", lambda e, acc=acc, cf_=cf_: e.tensor_scalar(out=cf_[D0:D0 + 64, :], in0=psb[acc][D0:D0 + 64, :], scalar1=1e-30, scalar2=None, op0=ALU.add),
                          reads=[PK(acc)], writes=[ck])
                for (br, acc, first, gs_, gk, cf_, ck) in st_:
                    fl.op("dve", lambda e, cf_=cf_: e.reciprocal(out=cf_[D0:D0 + 64, :], in_=cf_[D0:D0 + 64, :]), reads=[ck], writes=[ck])
                for (br, acc, first, gs_, gk, cf_, ck) in st_:
                    fl.op("pool", lambda e, cf_=cf_, gs_=gs_: e.tensor_tensor(out=cf_[D0:D0 + 64, :], in0=cf_[D0:D0 + 64, :], in1=gs_[D0:D0 + 64, :], op=ALU.mult),
                          reads=[ck, gk], writes=[ck])
                for (br, acc, first, gs_, gk, cf_, ck) in st_:
                    if first:
                        fl.op("dve", lambda e, acc=acc, cf_=cf_: e.tensor_tensor(out=ybuf[N0:N0 + 64, 4 + m, :], in0=psb[acc][N0:N0 + 64, :], in1=cf_[D0:D0 + 64, :], op=ALU.mult),
                              reads=[PK(acc), ck], writes=[("yc", h)])
                    else:
                        fl.op("dve", lambda e, acc=acc, cf_=cf_, gs_=gs_: e.tensor_tensor(out=gs_[N0:N0 + 64, :], in0=psb[acc][N0:N0 + 64, :], in1=cf_[D0:D0 + 64, :], op=ALU.mult),
                              reads=[PK(acc), ck, gk], writes=[gk])
                for (br, acc, first, gs_, gk, cf_, ck) in st_:
                    if not first:
                        fl.op("pool", lambda e, gs_=gs_: e.tensor_tensor(out=ybuf[N0:N0 + 64, 4 + m, :], in0=ybuf[N0:N0 + 64, 4 + m, :], in1=gs_[N0:N0 + 64, :], op=ALU.add),
                              reads=[("yc", h), gk], writes=[("yc", h)])

            ptn = dict(n=0)

            def next_pt():
                k_ = ptn["n"] % 4
                ptn["n"] += 1
                return k_

            nch = 1 if i < 4 else 2
            topkB1 = []
            cmp_prev = []
            do_sel = i >= 2
            for g in range(2):
                bimp = [ps_get(hold=True) for _ in range(NST // 2)] if do_sel else None
                for r in range(4):
                    h = g * 4 + r
                    m = r
                    acc = ps_get(hold=True)
                    kcs = []
                    bss = []
                    for c in range(nch):
                        bs = ps_get()
                        bss.append(bs)
                        ip = i - 4 * c
                        masked = ip < 4
                        mm(psb[bs][:, :], kct[g * 64:(g + 1) * 64, c * 128:(c + 1) * 128], qs[h][g * 64:(g + 1) * 64, :], True, not masked,
                           [("kct", l), ("qs", h)], PK(bs))
                        if masked:
                            mm(psb[bs][:, :], identb[:], cm[:, ip, :], False, True, ["identb", "cm"], PK(bs))
                    for c in range(nch):
                        bs = bss[c]
                        k_ = next_pt()
                        kcs.append(k_)
                        fl.op("act", lambda e, bs=bs, k_=k_: e.activation(out=pt[k_][:], in_=psb[bs][:, :], func=AF.Exp, scale=SCALE), reads=[PK(bs)], writes=[("pt", k_)])
                    for c in range(nch):
                        k_ = kcs[c]
                        mm(psb[acc][:, :], vce[:, c, g * 64:g * 64 + 128], pt[k_][:], c == 0, c == nch - 1, [("vce", l), ("pt", k_)], PK(acc))
                    if do_sel:
                        for stn in range(NST):
                            bi = bimp[stn // 2]
                            o0 = (stn % 2) * 256 + r * 64
                            for c in range(nch):
                                mm(psb[bi][:, o0:o0 + 64], pt[kcs[c]][:, stn * 128:(stn + 1) * 128], ovt[:, c, 0:64], c == 0, c == nch - 1, [("pt", kcs[c]), "ovt"], PK(bi))
                    combine_multi(h, [(0, acc, True)])
                    for b_ in cmp_prev:
                        ps_rel(b_)
                    cmp_prev[:] = [acc]
                if do_sel:
                    def topkA(g=g, bimp=bimp):
                        SR = range(NST)
                        for bk_ in range(NST // 2):
                            fl.op("dve", lambda e, bk_=bk_, bimp=bimp: e.tensor_copy(out=impsb[:, bk_, :], in_=psb[bimp[bk_]][:, :]), reads=[PK(bimp[bk_])], writes=[("impsb", bk_)])
                        v3s = [impsb[:, stn // 2, (stn % 2) * 256:(stn % 2) * 256 + 256].rearrange("p (r c) -> p r c", r=4) for stn in SR]
                        for r in range(4):
                            for stn in SR:
                                fl.op("dve", lambda e, stn=stn, r=r, v3=v3s[stn]: e.reduce_sum(out=rd4[:, stn, r:r + 1], in_=v3[:, r, :], axis=mybir.AxisListType.X),
                                      reads=[("impsb", stn // 2)], writes=[("rd4", stn)])
                        for stn in SR:
                            fl.op("dve", lambda e, stn=stn: e.tensor_scalar(out=rd4[:, stn, :], in0=rd4[:, stn, :], scalar1=1e-30, scalar2=None, op0=ALU.add),
                                  reads=[("rd4", stn)], writes=[("rd4", stn)])
                        for stn in SR:
                            fl.op("dve", lambda e, stn=stn: e.reciprocal(out=rd4[:, stn, :], in_=rd4[:, stn, :]), reads=[("rd4", stn)], writes=[("rd4", stn)])
                        for stn in SR:
                            fl.op("dve", lambda e, stn=stn, v3=v3s[stn]: e.tensor_scalar(out=impa[:, stn, :], in0=v3[:, 0, 0:64], scalar1=rd4[:, stn, 0:1], scalar2=None, op0=ALU.mult),
                                  reads=[("impsb", stn // 2), ("rd4", stn)], writes=[("impa", stn)])
                        for r in range(1, 4):
                            for stn in SR:
                                fl.op("dve", lambda e, stn=stn, r=r, v3=v3s[stn]: e.scalar_tensor_tensor(out=impa[:, stn, :], in0=v3[:, r, 0:64], scalar=rd4[:, stn, r:r + 1], in1=impa[:, stn, :],
                                                                                             op0=ALU.mult, op1=ALU.add),
                                      reads=[("impsb", stn // 2), ("rd4", stn), ("impa", stn)], writes=[("impa", stn)])
                        for stn in SR:
                            sg_ = i * NST + stn
                            fl.op("dve", lambda e, stn=stn, sg_=sg_: e.tensor_tensor(out=impa[:, stn, :], in0=impa[:, stn, :], in1=rt[:, 64 - 2 * sg_:128 - 2 * sg_], op=ALU.add),
                                  reads=[("impa", stn), "rt"], writes=[("impa", stn)])
                        for stn in SR:
                            fl.op("dve", lambda e, stn=stn: e.memset(impa[:, stn, 0:1], 1e30), reads=[("impa", stn)], writes=[("impa", stn)])
                        for stn in SR:
                            fl.op("dve", lambda e, stn=stn: e.max(out=m8a[:, stn, :], in_=impa[:, stn, :]), reads=[("impa", stn)], writes=[("m8a", stn)])
                        for stn in SR:
                            fl.op("dve", lambda e, stn=stn: e.match_replace(out=sc2[:, stn, :], in_to_replace=m8a[:, stn, :], in_values=impa[:, stn, :], imm_value=-3e38),
                                  reads=[("impa", stn), ("m8a", stn)], writes=[("sc2", stn)])
                        for stn in SR:
                            fl.op("dve", lambda e, stn=stn: e.max(out=m8b[:, stn, :], in_=sc2[:, stn, :]), reads=[("sc2", stn)], writes=[("m8b", stn)])
                        for stn in SR:
                            fl.op("dve", lambda e, stn=stn: e.tensor_scalar(out=selm[:, g * NST + stn, :], in0=impa[:, stn, :], scalar1=m8b[:, stn, 7:8], scalar2=1.0, op0=ALU.is_ge, op1=ALU.subtract),
                                  reads=[("impa", stn), ("m8b", stn)], writes=[("selm", g, stn)])
                    def topkB(g=g):
                        SR = range(NST)
                        h0_ = g * 4
                        rs_ = slice((1 - g) * 64, (1 - g) * 64 + 64)
                        for stn in SR:
                            bt = ps_get()
                            fl.op("pe", lambda e, bt=bt, stn=stn: e.transpose(out=psb[bt][0:64, 0:128], in_=selm[:, g * NST + stn, :], identity=ident[:]), reads=[("selm", g, stn), "ident"], writes=[PK(bt)])
                            fl.op("act", lambda e, bt=bt, stn=stn, h0_=h0_, rs_=rs_: e.copy(out=qs[h0_][rs_, stn * 128:(stn + 1) * 128], in_=psb[bt][0:64, 0:128]),
                                  reads=[PK(bt)], writes=[("qs", h0_)])
                        for r_ in range(1, 4):
                            h_ = g * 4 + r_
                            fl.op("dve", lambda e, h_=h_, h0_=h0_, rs_=rs_: e.tensor_copy(out=qs[h_][rs_, :], in_=qs[h0_][rs_, :]), reads=[("qs", h0_)], writes=[("qs", h_)])
                    if g == 0:
                        topkA()
                        topkB0 = topkB
                    else:
                        topkB0()
                        topkA()
                        topkB1.append(topkB)
                    for bb in bimp:
                        ps_rel(bb)
            if do_sel:
                pass

            if stages < 2.4:
                return
            for b_ in cmp_prev:
                ps_rel(b_)
            cmp_prev[:] = []
            prev_accs = []
            for h in range(8):
                if h == 4 and topkB1:
                    topkB1[0]()
                g, m = h // 4, h % 4
                gs = slice(g * 64, (g + 1) * 64)
                acc_s = ps_get(hold=True)
                acc_w = ps_get(hold=True)
                tasks = []
                nks = NST * i + NST
                for kt in range(nks):
                    tasks.append(("s", kt, kt == 0, kt == nks - 1))
                wk = [a for a in ([-1, -4, -3, -2, 0, 1, 2, 3] if i >= 1 else [0, 1, 2, 3])]
                for n_, a in enumerate(wk):
                    tasks.append(("w", a, n_ == 0, n_ == len(wk) - 1))
                st_ = [t_ for t_ in tasks if t_[0] == "s"]
                wt_ = [t_ for t_ in tasks if t_[0] == "w"]
                order = []
                while st_ or wt_:
                    if st_:
                        order.append(st_.pop(0))
                    if wt_:
                        order.append(wt_.pop(0))

                def score(task):
                    kind, idx, first, last = task
                    bs = ps_get()
                    if kind == "s":
                        kt = idx
                        a = kt - NST * i
                        q0 = 128 * a if a > 0 else 0
                        q1 = T
                        need_sel = do_sel
                        need_c = a >= 0
                        mm(psb[bs][:, q0:q1], ke[g][:, kt * 128:(kt + 1) * 128], qs[h][:, q0:q1], True, not need_c, [("ke", g), ("qs", h)], PK(bs))
                        if need_c:
                            mm(psb[bs][:, q0:q1], identb[:], wm[:, a + 4, q0:q1], False, True, ["identb", "wm"], PK(bs))
                        vap = vsl[:, kt, g * 64:g * 64 + 128]
                        vkey = ("vsl", l)
                        acc = acc_s
                    else:
                        a = idx
                        kt = NST * i + a
                        q0 = 128 * a if a > 0 else 0
                        q1 = T if a >= -1 else 128 * (a + 5)
                        sl_ = (kt // NST) % 2
                        c0 = (kt % NST) * 128
                        mm(psb[bs][:, q0:q1], kw[gs, sl_, c0:c0 + 128], qs[h][gs, q0:q1], True, False, [("kw", l), ("qs", h)], PK(bs))
                        mm(psb[bs][:, q0:q1], identb[:], wm[:, a + 4, q0:q1], False, True, ["identb", "wm"], PK(bs))
                        vap = vw[:, sl_ * NST + kt % NST, g * 64:g * 64 + 128]
                        vkey = ("vw", l)
                        acc = acc_w
                    k_ = next_pt()
                    fl.op("act", lambda e: e.activation(out=pt[k_][:, q0:q1], in_=psb[bs][:, q0:q1], func=AF.Exp, scale=SCALE), reads=[PK(bs)], writes=[("pt", k_)])
                    return (acc, vap, vkey, k_, q0, q1, first, last)

                def pv(info):
                    acc, vap, vkey, k_, q0, q1, first, last = info
                    mm(psb[acc][:, q0:q1], vap, pt[k_][:, q0:q1], first, last, [vkey, ("pt", k_)], PK(acc))

                pend = []
                for task in order:
                    pend.append(score(task))
                    if len(pend) > 2:
                        pv(pend.pop(0))
                while pend:
                    pv(pend.pop(0))
                combine_multi(h, [(1, acc_s, False), (2, acc_w, False)])
                for b_ in prev_accs:
                    ps_rel(b_)
                prev_accs[:] = [acc_s, acc_w]
            for b_ in prev_accs:
                ps_rel(b_)
            if dbg:
                fl.op("dve", lambda e: e.memset(gj[0][0:1, 0:1], 1.0), reads=[("yc", hh_) for hh_ in range(8)] + YBALL + [("gj", 0)], writes=["ybuf_all", ("gj", 0)])
            dump(f"y_{l}_{i}", ybuf[:], "ybuf_all", [128, 8, T], BF16)

            if stages < 2.5:
                return
            lns = ln_begin()
            pend_m = None
            for u in range(2):
                slot, skey = streamA("WOUT", (l * 2 + u) * 128, grp)
                sv = slot[:].rearrange("p (k c) -> p k c", k=8)
                for c in range(4):
                    m = u * 4 + c
                    b = ps_get()
                    for kc in range(8):
                        mm(psb[b][:, :], sv[:, kc, c * 128:(c + 1) * 128], ybuf[:, kc, :], kc == 0, kc == 7, [skey] + YKEYS[kc], PK(b))
                    if pend_m is not None:
                        ln_feed(lns, pend_m)
                    pend_m = m
                    fl.op("dve", lambda e, b=b, m=m: e.scalar_tensor_tensor(out=x32[:, m, :], in0=psb[b][:, :], scalar=1.0 / ALPHA, in1=x32[:, m, :], op0=ALU.mult, op1=ALU.add),
                          reads=[PK(b), ("x32", m)], writes=[("x32", m)])
            ln_feed(lns, pend_m)
            ln_finish(l, 1, lns)

        spt_b = sb("spt_b", [128, L, 32], BF16)
        for l in range(L):
            fl.op("dve", lambda e, l=l: e.tensor_copy(out=spt_b[:, l, :], in_=spt[:, l, 60:92]), reads=["spt"], writes=["sptb"])

        NEXT_TILE = [None]
        for l in range(n_layers):
            for i in range(n_tiles):
                if l == 0 and i == 0:
                    prefetch_xb(0, 0)
                load_x32(l, i)
                NEXT_TILE[0] = (l, i + 1) if i + 1 < n_tiles else ((l + 1, 0) if l + 1 < n_layers else None)
                if l == 0 and i == 0:
                    emit_pc([p for p in pc_l0 if p[3][1] == "f1"])
                if l == 0 and i == 1:
                    emit_pc(pc_l1[:len(pc_l1) // 2])
                if l == 0 and i == 2:
                    emit_pc(pc_l1[len(pc_l1) // 2:])
                if l == 0 and i == n_tiles - 1 and n_tiles < 3 and n_layers > 1:
                    emit_pc(pc_l1)
                if stages >= 1:
                    ffn(l, 0)
                if l == 0 and i == 0:
                    emit_pc([p for p in pc_l0 if p[3][1] != "f1"])
                if stages >= 1:
                    dump(f"x1_{l}_{i}", x32[:], "x32all", [128, 8, T])
                if stages >= 2:
                    mixer(l, i)
                    dump(f"x2_{l}_{i}", x32[:], "x32all", [128, 8, T])
                if stages >= 3:
                    ffn(l, 1)
                    dump(f"x3_{l}_{i}", x32[:], "x32all", [128, 8, T])
                if l == n_layers - 1:
                    store_x(i)
                else:
                    spill_x(i)

        fl.wait_all_dma("sp")
        fl.emit()
        print('sbuf_bytes_remaining', nc.sbuf_bytes_remaining)
        stats = dict(n_ins=fl.n_ins, n_wait=fl.n_wait, counts={k: v["count"] for k, v in fl.engs.items()}, nsem=len(fl.dsems))
    return nc, dbg_out, stats


_CACHE = {}


def kernel(**inputs):
    P = _pack({k: np.asarray(v) for k, v in inputs.items()})
    x = np.ascontiguousarray(np.asarray(inputs["x"], dtype=np.float32))
    B = x.shape[0]
    if "nc" not in _CACHE:
        _CACHE["nc"] = build()[0]
    nc = _CACHE["nc"]
    in_maps = []
    for b in range(B):
        d = dict(P)
        d["x"] = np.ascontiguousarray(x[b].reshape(NT, T, 8, 128).transpose(0, 3, 2, 1).reshape(NT * 128, 8 * T))
        in_maps.append(d)
    res = run_bass_kernel_spmd(nc, in_maps, core_ids=list(range(B)))
    out = np.stack([np.asarray(r["out"]).reshape(NT, 128, 8, T).transpose(0, 3, 2, 1).reshape(S, D) for r in res.results], axis=0)
    return out.astype(np.float32)
```

```python
import numpy as np
from contextlib import ExitStack
import concourse.bass as bass
import concourse.mybir as mybir
from concourse.bass_utils import run_bass_kernel_spmd

F32 = mybir.dt.float32
BF16 = mybir.dt.bfloat16
AF = mybir.ActivationFunctionType
ALU = mybir.AluOpType

D = 1024
S = 4096
L = 2
DFF = 2816
T = 512
NT = S // T
NST = T // 128
ALPHA = (2 * L) ** 0.25
EPS = 1e-5
EPSP = EPS / ALPHA ** 2
SCALE = 64 ** -0.5
NEGM = -30000.0
NFB = 11
NOB = 4
SPW = 412


class Flow:
    ENGS = ("pe", "act", "dve", "pool", "sp")

    def __init__(self, nc, stack):
        self.nc = nc
        self.stack = stack
        self.engs = {}
        for name in self.ENGS:
            sem = stack.enter_context(nc.semaphore(f"s_{name}"))
            self.engs[name] = dict(prog=[], sem=sem, count=0, waited={})
        self.lastw = {}
        self.readers = {}
        self.dsems = {}
        self.n_ins = 0
        self.n_wait = 0

    def _deps(self, eng, reads, writes):
        deps = {}

        def need(tok):
            sem, val, src = tok
            if src == eng and eng in ("pe", "sp"):
                return
            k = id(sem)
            if k not in deps or deps[k][1] < val:
                deps[k] = (sem, val)

        for r in reads:
            t = self.lastw.get(r)
            if t is not None:
                need(t)
        for w in writes:
            t = self.lastw.get(w)
            if t is not None:
                need(t)
            for t in self.readers.get(w, {}).values():
                need(t)
        E = self.engs[eng]
        for k, (sem, val) in deps.items():
            if E["waited"].get(k, 0) < val:
                E["prog"].append(("wait", sem, val))
                E["waited"][k] = val
                self.n_wait += 1

    def _commit(self, tok, reads, writes):
        k = id(tok[0])
        for r in reads:
            self.readers.setdefault(r, {})[k] = tok
        for w in writes:
            self.lastw[w] = tok
            self.readers[w] = {}

    def op(self, eng, fn, reads=(), writes=(), inc=True):
        self._deps(eng, reads, writes)
        E = self.engs[eng]
        if inc:
            E["count"] += 1
            tok = (E["sem"], E["count"], eng)
            E["prog"].append(("ins", fn, E["sem"], 1))
        else:
            tok = (E["sem"], E["count"] + 1, eng)
            E["prog"].append(("ins0", fn))
        self._commit(tok, reads, writes)
        self.n_ins += 1

    def dma(self, eng, fn, reads=(), writes=(), dsem=None):
        if dsem is None:
            dsem = ("d", writes[0] if writes else reads[0])
        dsem = (dsem, eng)
        if dsem not in self.dsems:
            sem = self.stack.enter_context(self.nc.semaphore(f"sd{len(self.dsems)}"))
            self.dsems[dsem] = [sem, 0, set()]
        Dm = self.dsems[dsem]
        self._deps(eng, reads, writes)
        Dm[1] += 16
        tok = (Dm[0], Dm[1], "dma")
        self.engs[eng]["prog"].append(("dma", fn, Dm[0], 1))
        Dm[2].update(writes)
        self._commit(tok, reads, writes)
        self.n_ins += 1

    def seal(self, dsem):
        for key, Dm in self.dsems.items():
            if key[0] != dsem:
                continue
            tok = (Dm[0], Dm[1], "dma")
            for k in Dm[2]:
                if k in self.lastw and self.lastw[k][0] is Dm[0]:
                    self.lastw[k] = tok

    def wait_all_dma(self, eng="sp"):
        E = self.engs[eng]
        for name, Dm in self.dsems.items():
            if Dm[1] > 0:
                E["prog"].append(("wait", Dm[0], Dm[1]))

    def emit(self):
        nc = self.nc
        handles = dict(pe="tensor", act="scalar", dve="vector", pool="gpsimd", sp="sync")
        with nc.Block() as block:
            for name in self.ENGS:
                prog = self.engs[name]["prog"]

                def body(eng, prog=prog):
                    for item in prog:
                        if item[0] == "wait":
                            eng.wait_ge(item[1], item[2])
                        elif item[0] == "ins0":
                            item[1](eng)
                        elif item[0] == "ins":
                            item[1](eng).then_inc(item[2], 1)
                        else:
                            item[1](eng).then_inc(item[2], 16)

                getattr(block, handles[name])(body)


def _win_cols():
    q0 = 1280
    cols = []
    for m in range(4):
        cols += list(range(q0 + m * 64, q0 + (m + 1) * 64)) + list(range(q0 + (m + 4) * 64, q0 + (m + 5) * 64))
    cols += list(range(2048, 2176))
    cols += list(range(2304, 2432))
    cols += list(range(1792, 1856)) + list(range(1920, 1984))
    cols += list(range(1856, 1920)) + list(range(1984, 2048))
    cols += list(range(0, 256))
    cols += list(range(512, 768))
    cols += list(range(256, 512))
    cols += list(range(768, 1024))
    cols += list(range(1024, 1280))
    cols += list(range(2560, 2584)) + [-1] * 104
    cols += [-1] * 128
    cols += list(range(2176, 2304))
    cols += list(range(2432, 2560))
    cols += [-1] * 256
    return np.array(cols)


def _wout_rows():
    rows = list(range(0, 512))
    for m in range(4):
        rows += list(range(512 + m * 64, 512 + (m + 1) * 64)) + list(range(512 + (m + 4) * 64, 512 + (m + 5) * 64))
    return np.array(rows)


def _consts():
    c = {}
    c["IDENT"] = np.eye(128, dtype=np.float32)
    qq = np.arange(128)
    c["TRILT"] = (qq[:, None] <= qq[None, :]).astype(np.float32)
    k = np.arange(128)[:, None, None]
    a = (np.arange(8) - 4)[None, :, None]
    q = np.arange(512)[None, None, :]
    dist = q - 128 * a - k
    c["WM"] = np.where((dist >= 0) & (dist < 512), 0.0, NEGM).astype(np.float32).reshape(128, 8 * 512)
    ip = np.arange(4)[None, :, None]
    c["CM"] = np.where(16 * k + 15 <= 512 * ip + q, 0.0, NEGM).astype(np.float32).reshape(128, 4 * 512)
    j = np.arange(64)[:, None]
    cc = np.arange(4096)[None, :]
    em1 = np.where(j == cc // 64, -NEGM, 0.0).astype(np.float32)
    c["EM"] = np.concatenate([em1, em1], axis=0)
    ql = np.arange(128)[:, None]
    m = np.arange(128)[None, :] - 64
    cur = ql // 64
    rt = np.zeros((128, 128), np.float32)
    rt[(m == cur) | (m == cur - 1)] = 1e30
    rt[m > cur] = -1e30
    c["RT"] = rt
    ov = np.zeros((128, 2, 65), np.float32)
    for ch in range(2):
        for nl in range(128):
            npr = 128 * ch + nl
            if npr == 0:
                continue
            n = npr - 1
            st = 16 * n
            for jb in range(64):
                o = min(st + 32, jb * 64 + 64) - max(st, jb * 64)
                if o > 0:
                    ov[nl, ch, jb] = o / 32.0
            ov[nl, ch, 64] = 1.0
    c["OV"] = ov.reshape(128, 130)
    c["SELG"] = np.eye(24, dtype=np.float32)
    return c


def _pack(inp):
    f = np.float32
    P = {}
    GU = np.empty((L, 2, NFB, 128, 4096), f)
    DN = np.empty((L, 2, NOB, 128, 5632), f)
    for l in range(L):
        for w, nm in enumerate(("ffn1", "ffn2")):
            g = inp[nm + "_gate"][l].reshape(8, 128, NFB, 256).transpose(2, 1, 0, 3)
            u = inp[nm + "_up"][l].reshape(8, 128, NFB, 256).transpose(2, 1, 0, 3)
            GU[l, w] = np.stack([g, u], axis=2).reshape(NFB, 128, 4096)
            DN[l, w] = inp[nm + "_down"][l].reshape(22, 128, NOB, 256).transpose(2, 1, 0, 3).reshape(NOB, 128, 5632)
    P["GU"] = GU.reshape(L * 2 * NFB * 128, 4096)
    P["DN"] = DN.reshape(L * 2 * NOB * 128, 5632)
    cols = _win_cols()
    valid = cols >= 0
    WIN = np.empty((L, 6, 128, 4096), f)
    WOUT = np.empty((L, 2, 128, 4096), f)
    W1 = np.empty((L, 128, 4096), f)
    SP = np.zeros((L, 128, SPW), f)
    GW = np.empty((L, 128, 512), f)
    GB = np.empty((L, 1, 512), f)
    rows = _wout_rows()
    for l in range(L):
        wp = np.zeros((1024, 3072), f)
        wp[:, valid] = inp["w_in"][l][:, cols[valid]]
        WIN[l] = wp.reshape(8, 128, 6, 512).transpose(2, 1, 0, 3).reshape(6, 128, 4096)
        wo = inp["w_out"][l][rows, :]
        WOUT[l] = wo.reshape(8, 128, 2, 512).transpose(2, 1, 0, 3).reshape(2, 128, 4096)
        W1[l] = np.concatenate([inp["ck_w1"][l].transpose(1, 0, 2), inp["cv_w1"][l].transpose(1, 0, 2)], axis=0).reshape(128, 4096)
        for jn, nm in enumerate(("ln1_g", "ln1_b", "ln2_g", "ln2_b", "ln3_g", "ln3_b")):
            SP[l, :, jn * 8:(jn + 1) * 8] = inp[nm][l].reshape(8, 128).T
        SP[l, :, 48:54] = inp["conv_w"][l].reshape(3, 2, 128).transpose(2, 1, 0).reshape(128, 6)
        SP[l, :, 54:56] = inp["gmlp_ln_g"][l].reshape(2, 128).T
        SP[l, :, 56:58] = inp["gmlp_ln_b"][l].reshape(2, 128).T
        SP[l, :, 58] = inp["ck_b1"][l]
        SP[l, :, 59] = inp["cv_b1"][l]
        SP[l, :64, 60:92] = inp["ck_pe"][l].T
        SP[l, 64:, 60:92] = inp["cv_pe"][l].T
        SP[l, :, 92:156] = inp["ck_w2"][l]
        SP[l, :, 284:348] = inp["ck_w2"][l]
        SP[l, :, 348:412] = inp["cv_w2"][l]
        GW[l] = inp["gmlp_w"][l].transpose(2, 0, 1).reshape(128, 512)
        GB[l] = inp["gmlp_b"][l].reshape(1, 512)
    P["WIN"] = WIN.reshape(L * 6 * 128, 4096)
    P["WOUT"] = WOUT.reshape(L * 2 * 128, 4096)
    P["W1"] = W1.reshape(L * 128, 4096)
    P["SP"] = SP.reshape(L * 128, SPW)
    P["GW"] = GW.reshape(L * 128, 512)
    P["GB"] = GB.reshape(L, 512)
    P.update(_consts())
    return P


SHAPES = dict(GU=(L * 2 * NFB * 128, 4096), DN=(L * 2 * NOB * 128, 5632), WIN=(L * 6 * 128, 4096),
              WOUT=(L * 2 * 128, 4096), W1=(L * 128, 4096), SP=(L * 128, SPW), GW=(L * 128, 512), GB=(L, 512),
              IDENT=(128, 128), TRILT=(128, 128), WM=(128, 4096), CM=(128, 2048), EM=(128, 4096), RT=(128, 128),
              OV=(128, 130), SELG=(24, 24))


def build(n_tiles=NT, n_layers=L, dbg=None, stages=99):
    nc = bass.Bass("TRN2", target_bir_lowering=False)
    dram = {k: nc.dram_tensor(k, list(s), F32, kind="ExternalInput").ap() for k, s in SHAPES.items()}
    x_d = nc.dram_tensor("x", [NT * 128, 8 * T], F32, kind="ExternalInput").ap()
    o_d = nc.dram_tensor("out", [NT * 128, 8 * T], F32, kind="ExternalOutput").ap()
    scr = {k: nc.dram_tensor(k + "b", list(SHAPES[k]), BF16, kind="Internal").ap() for k in ("GU", "DN", "WIN", "WOUT", "W1")}
    xs_d = nc.dram_tensor("xs", [NT * 128, 8 * T], F32, kind="Internal").ap()
    dbg_out = {}

    with ExitStack() as st:
        fl = Flow(nc, st)
        _sb = {}

        def sb(name, shape, dt=F32):
            t = st.enter_context(nc.sbuf_tensor(name, list(shape), dt))
            _sb[name] = t
            return t

        psb = [st.enter_context(nc.psum_tensor(f"ps{b}", [128, 512], F32)) for b in range(8)]
        ps_state = dict(n=0, held=set())

        def ps_get(hold=False):
            for _ in range(16):
                b = ps_state["n"] % 8
                ps_state["n"] += 1
                if b not in ps_state["held"]:
                    if hold:
                        ps_state["held"].add(b)
                    return b
            raise RuntimeError("no psum bank")

        def ps_rel(b):
            ps_state["held"].discard(b)

        def PK(b):
            return ("ps", b)

        def dump(name, ap, key, shape, dt=F32):
            if dbg is None or name not in dbg:
                return
            if key == "x32all":
                fl.op("dve", lambda e: e.memset(gj[0][0:1, 0:1], 1.0), reads=X32ALL + [("gj", 0)], writes=["x32all", ("gj", 0)])
            o = nc.dram_tensor("dbg_" + name, list(shape), dt, kind="ExternalOutput").ap()
            dbg_out[name] = o
            fl.dma("pool", lambda e: e.dma_start(out=o, in_=ap), reads=[key], dsem="dbg")

        ident = sb("ident", [128, 128])
        identb = sb("identb", [128, 128], BF16)
        trilt = sb("trilt", [128, 128])
        wm = sb("wm", [128, 8, 512], BF16)
        cm = sb("cm", [128, 4, 512], BF16)
        rt = sb("rt", [128, 128])
        ovt = sb("ovt", [128, 2, 65], BF16)
        selg = sb("selg", [24, 24])
        ones24 = sb("ones24", [24, 128], BF16)
        gj = [sb(f"gj{i}", [24, T], BF16) for i in range(2)]
        ones = sb("ones", [128, 128])
        onesb = sb("onesb", [1, 64], BF16)
        spt = sb("spt", [128, L, SPW])
        w2b = sb("w2b", [128, L, 320], BF16)
        gwm = sb("gwm", [128, L, 4, 128], BF16)
        gbr = sb("gbr", [1, L, 512], BF16)
        cbias = sb("cbias", [128, L, 2])

        def cload(eng, dst, src, key):
            fl.dma(eng, lambda e: e.dma_start(out=dst, in_=src), writes=[key], dsem="const")

        cload("sp", ident[:], dram["IDENT"], "ident")
        cload("sp", trilt[:], dram["TRILT"], "trilt")
        cload("sp", rt[:], dram["RT"], "rt")
        cload("pool", identb[:], dram["IDENT"], "identb")
        cload("pool", wm[:], dram["WM"].rearrange("p (a q) -> p a q", a=8), "wm")
        cload("pool", cm[:], dram["CM"].rearrange("p (a q) -> p a q", a=4), "cm")
        cload("pool", ovt[:], dram["OV"].rearrange("p (a q) -> p a q", a=2), "ovt")
        cload("sp", selg[:], dram["SELG"], "selg")
        for l in range(L):
            cload("sp", spt[:, l, :], dram["SP"][l * 128:(l + 1) * 128, :], "spt")
            cload("pool", w2b[:, l, :], dram["SP"][l * 128:(l + 1) * 128, 92:412], "w2b")
            cload("pool", gbr[:, l, :], dram["GB"][l:l + 1, :], "gbr")
        fl.seal("const")
        fl.op("dve", lambda e: e.memset(ones[:], 1.0), writes=["ones"])
        fl.op("dve", lambda e: e.memset(onesb[:], 1.0), writes=["onesb"])
        fl.op("dve", lambda e: e.memset(ones24[:], 1.0), writes=["ones24"])

        pc_groups = []

        def precast(name, r0, r1, grp):
            if grp == (0, "f1"):
                grp = (0, "f1", name, r0)
            if grp not in pc_groups:
                pc_groups.append(grp)
            fl.dma("pool", lambda e: e.dma_start(out=scr[name][r0:r1, :], in_=dram[name][r0:r1, :]),
                   writes=[("pc", grp)], dsem=("pc", grp))

        pc_pending = []
        for l in range(n_layers):
            for fbk in range(NFB):
                r = ((l * 2 + 0) * NFB + fbk) * 128
                pc_pending.append(("GU", r, r + 128, (l, "f1")))
            for ob in range(NOB):
                r = ((l * 2 + 0) * NOB + ob) * 128
                pc_pending.append(("DN", r, r + 128, (l, "f1")))
            for u in range(6):
                r = (l * 6 + u) * 128
                pc_pending.append(("WIN", r, r + 128, (l, "mix")))
            pc_pending.append(("W1", l * 128, (l + 1) * 128, (l, "mix")))
            for u in range(2):
                r = (l * 2 + u) * 128
                pc_pending.append(("WOUT", r, r + 128, (l, "mix")))
            for fbk in range(NFB):
                r = ((l * 2 + 1) * NFB + fbk) * 128
                pc_pending.append(("GU", r, r + 128, (l, "f2")))
            for ob in range(NOB):
                r = ((l * 2 + 1) * NOB + ob) * 128
                pc_pending.append(("DN", r, r + 128, (l, "f2")))
        pc_l1 = [p for p in pc_pending if p[3][0] == 1]
        pc_l0 = [p for p in pc_pending if p[3][0] == 0]
        def emit_pc(items):
            grps = []
            for (name, r0, r1, grp) in items:
                precast(name, r0, r1, grp)
            for g_ in list(pc_groups):
                fl.seal(("pc", g_))

        gwf = sb("gwf", [128, 4, 128])
        for l in range(n_layers):
            fl.dma("sp", lambda e, l=l: e.dma_start(out=gwf[:], in_=dram["GW"][l * 128:(l + 1) * 128, :].rearrange("p (h q) -> p h q", h=4)),
                   writes=["gwf"])
            for hh in range(4):
                fl.op("dve", lambda e, l=l, hh=hh: e.tensor_tensor(out=gwm[:, l, hh, :], in0=gwf[:, hh, :], in1=trilt[:], op=ALU.mult),
                      reads=["gwf", "trilt"], writes=["gwm"])

        NA, NB = 3, 2
        ringA = [sb(f"ringA{i}", [128, 4096], BF16) for i in range(NA)]
        ringB = [sb(f"ringB{i}", [128, 12, 256], BF16) for i in range(NB)]
        rstate = dict(a=0, b=0)

        def streamA(name, row0, grp):
            s_ = rstate["a"] % NA
            rstate["a"] += 1
            t = ringA[s_]
            if grp == (0, "f1"):
                grp = (0, "f1", name, row0)
            fl.dma("sp", lambda e: e.dma_start(out=t[:], in_=scr[name][row0:row0 + 128, :]),
                   reads=[("pc", grp)], writes=[("rA", s_)])
            return t, ("rA", s_)

        def streamB(row0, grp, fc0, nfc):
            s_ = rstate["b"] % NB
            rstate["b"] += 1
            t = ringB[s_]
            if grp == (0, "f1"):
                grp = (0, "f1", "DN", row0)
            fl.dma("sp", lambda e: e.dma_start(out=t[:, 0:nfc, :], in_=scr["DN"][row0:row0 + 128, fc0 * 256:(fc0 + nfc) * 256].rearrange("p (f c) -> p f c", f=nfc)),
                   reads=[("pc", grp)], writes=[("rB", s_)])
            return t, ("rB", s_)

        x32 = sb("x32", [128, 8, T])
        X32ALL = [("x32", m_) for m_ in range(8)]
        XBALL = [("xb", m_) for m_ in range(8)]
        xb = sb("xb", [128, 8, T], BF16)
        hbuf = sb("hbuf", [128, 12, T], BF16)
        NSCR = 7
        scrp = [sb(f"scrp{i}", [128, T]) for i in range(NSCR)]
        scr_n = dict(n=0)

        def scr_get():
            k_ = scr_n["n"] % NSCR
            scr_n["n"] += 1
            return scrp[k_], ("scrp", k_)
        mean = sb("mean", [128, T])
        rstd = sb("rstd", [128, T])
        ybuf = sb("ybuf", [128, 8, T], BF16)
        YKEYS = [[("ybuf", 0)], [("ybuf", 1)], [("ybuf", 2, 0), ("ybuf", 2, 1)], [("ybuf", 3, 0), ("ybuf", 3, 1)]] + [[("yc", m_), ("yc", m_ + 4)] for m_ in range(4)]
        YBALL = [k_ for ks_ in YKEYS[:4] for k_ in ks_]
        qs = [sb(f"qs{h}", [128, T], BF16) for h in range(8)]
        g24 = sb("g24", [24, T], BF16)
        ha = sb("ha", [128, 2, T], BF16)
        ub = sb("ub", [128, 2, T], BF16)
        vg = sb("vg", [128, 2, T])
        vnT = sb("vnT", [128, NST, 256], BF16)
        pt = [sb(f"pt{i}", [128, T], BF16) for i in range(4)]
        kb = sb("kb", [128, 32, 64], BF16)
        impa = sb("impa", [128, NST, 64])
        rd4 = sb("rd4", [128, NST, 4])
        m8a = sb("m8a", [128, NST, 8])
        m8b = sb("m8b", [128, NST, 8])
        sc2 = sb("sc2", [128, NST, 64])
        selm = sb("selm", [128, 2 * NST, 64])
        impsb = sb("impsb", [128, NST // 2, 512])
        state = []
        for l in range(1):
            stl = dict(
                ke=[sb(f"ke{l}_{g}", [128, S], BF16) for g in range(2)],
                vsl=sb(f"vsl{l}", [128, S // 128, 192], BF16),
                kw=sb(f"kw{l}", [128, 2, T], BF16),
                vw=sb(f"vw{l}", [128, 2 * NST, 192], BF16),
                kcvc=[sb(f"kcvc{l}_{g}", [128, T + 16], BF16) for g in range(2)],
                ht=sb(f"ht{l}", [128, 2, 2, 256], BF16),
                kct=sb(f"kct{l}", [128, 256], BF16),
                vce=sb(f"vce{l}", [128, 2, 192], BF16),
                zc=sb(f"zc{l}", [128, 2, T + 2]),
            )
            state.append(stl)
            fl.dma("pool", lambda e: e.dma_start(out=stl["ke"][0][64:128, :], in_=dram["EM"][64:128, :]), writes=[("ke", 0)], dsem="const2")
            fl.dma("pool", lambda e: e.dma_start(out=stl["ke"][1][0:64, :], in_=dram["EM"][0:64, :]), writes=[("ke", 1)], dsem="const2")
            fl.seal("const2")
            for nm in ("vsl", "vw", "vce"):
                t_ = stl[nm]
                fl.op("pool", lambda e, t_=t_: e.memset(t_[:], 1.0), writes=[(nm, l)])
            fl.op("pool", lambda e, t_=stl["vce"]: e.memset(t_[0:1, 0, :], 0.0), writes=[("vce", l)])
            fl.op("pool", lambda e, t_=stl["ht"]: e.memset(t_[:], 0.0), writes=[("ht", l)])
            fl.op("pool", lambda e, t_=stl["kct"]: e.memset(t_[:], 0.0), writes=[("kct", l)])
            for g in range(2):
                fl.op("pool", lambda e, t_=stl["kcvc"][g]: e.memset(t_[:], 0.0), writes=[("kcvc", l, g)])
            fl.op("pool", lambda e, t_=stl["zc"]: e.memset(t_[:], 0.0), writes=[("zc", 0), ("zc", 1)])

        import os as _os
        DENSE = bool(_os.environ.get("DENSE"))

        def mm(out_ap, lhsT, rhs, start, stop, reads, wkey, chain=False):
            fl.op("pe", lambda e: e.matmul(out_ap, lhsT=lhsT, rhs=rhs, start=start, stop=stop), reads=reads, writes=[wkey], inc=(bool(stop) or not chain or DENSE))

        sqb = [sb(f"sqb{i}", [128, T], BF16) for i in range(3)]
        sqb_n = dict(n=0)
        ones16 = sb("ones16", [128, 128], BF16)
        fl.op("dve", lambda e: e.memset(ones16[:], 1.0), writes=["ones16"])

        def ln_begin():
            return dict(b1=ps_get(hold=True), b2=ps_get(hold=True), n=0)

        def ln_feed(st_, m):
            b1, b2, k = st_["b1"], st_["b2"], st_["n"]
            mm(psb[b1][:, :], ones[:], x32[:, m, :], k == 0, k == 7, ["ones", ("x32", m)], PK(b1))
            q_ = sqb_n["n"] % 3
            sqb_n["n"] += 1
            fl.op("act", lambda e: e.activation(out=sqb[q_][:], in_=x32[:, m, :], func=AF.Square), reads=[("x32", m)], writes=[("sqb", q_)])
            mm(psb[b2][:, :], ones16[:], sqb[q_][:], k == 0, k == 7, ["ones16", ("sqb", q_)], PK(b2))
            st_["n"] += 1

        def ln_finish(l, j, st_, last_stage=False):
            b1, b2 = st_["b1"], st_["b2"]
            assert st_["n"] == 8
            fl.op("dve", lambda e: e.tensor_scalar(out=mean[:], in0=psb[b1][:, :], scalar1=1.0 / D, scalar2=None, op0=ALU.mult),
                  reads=[PK(b1)], writes=["mean"])
            fl.op("dve", lambda e: e.tensor_tensor(out=rstd[:], in0=mean[:], in1=mean[:], op=ALU.mult), reads=["mean"], writes=["rstd"])
            fl.op("dve", lambda e: e.scalar_tensor_tensor(out=rstd[:], in0=psb[b2][:, :], scalar=1.0 / D, in1=rstd[:], op0=ALU.mult, op1=ALU.subtract),
                  reads=[PK(b2), "rstd"], writes=["rstd"])
            ps_rel(b1)
            ps_rel(b2)
            fl.op("dve", lambda e: e.tensor_scalar(out=rstd[:], in0=rstd[:], scalar1=EPSP, scalar2=None, op0=ALU.add), reads=["rstd"], writes=["rstd"])
            fl.op("act", lambda e: e.activation(out=rstd[:], in_=rstd[:], func=AF.Ln), reads=["rstd"], writes=["rstd"])
            fl.op("act", lambda e: e.activation(out=rstd[:], in_=rstd[:], func=AF.Exp, scale=-0.5), reads=["rstd"], writes=["rstd"])
            for m0 in range(0, 8, 2):
                pr = [(m0, ) + scr_get(), (m0 + 1, ) + scr_get()]
                for (m, t_, tk) in pr:
                    fl.op("dve", lambda e, t_=t_, m=m: e.tensor_tensor(out=t_[:], in0=x32[:, m, :], in1=mean[:], op=ALU.subtract),
                          reads=[("x32", m), "mean"], writes=[tk])
                for (m, t_, tk) in pr:
                    fl.op("dve", lambda e, t_=t_: e.tensor_tensor(out=t_[:], in0=t_[:], in1=rstd[:], op=ALU.mult),
                          reads=[tk, "rstd"], writes=[tk])
                for (m, t_, tk) in pr:
                    gcol = 2 * j * 8 + m
                    bcol = (2 * j + 1) * 8 + m
                    if last_stage:
                        fl.op("act", lambda e, t_=t_, m=m, gcol=gcol, bcol=bcol: e.activation(
                            out=x32[:, m, :], in_=t_[:], func=AF.Identity, bias=spt[:, l, bcol:bcol + 1], scale=spt[:, l, gcol:gcol + 1]),
                            reads=[tk, "spt"], writes=[("x32", m)])
                    else:
                        fl.op("act", lambda e, t_=t_, m=m, gcol=gcol, bcol=bcol: e.activation(
                            out=xb[:, m, :], in_=t_[:], func=AF.Identity, bias=spt[:, l, bcol:bcol + 1], scale=spt[:, l, gcol:gcol + 1]),
                            reads=[tk, "spt"], writes=[("xb", m)])
                for (m, t_, tk) in pr:
                    gcol = 2 * j * 8 + m
                    bcol = (2 * j + 1) * 8 + m
                    if last_stage:
                        continue
                    fl.op("pool", lambda e, t_=t_, m=m, gcol=gcol, bcol=bcol: e.tensor_scalar(
                        out=x32[:, m, :], in0=t_[:], scalar1=spt[:, l, gcol:gcol + 1], scalar2=spt[:, l, bcol:bcol + 1], op0=ALU.mult, op1=ALU.add),
                        reads=[tk, "spt"], writes=[("x32", m)])

        def ffn(l, w):
            grp = (l, "f1" if w == 0 else "f2")
            for hf, (fb0, fb1) in enumerate(((0, 6), (6, NFB))):
                for fbk in range(fb0, fb1):
                    slot, skey = streamA("GU", ((l * 2 + w) * NFB + fbk) * 128, grp)
                    sv = slot[:].rearrange("p (a d c) -> p a d c", a=2, d=8)
                    for half in range(2):
                        fcl = (fbk - fb0) * 2 + half
                        bg = ps_get()
                        bu = ps_get()
                        for dc in range(8):
                            mm(psb[bg][:, :], sv[:, 0, dc, half * 128:(half + 1) * 128], xb[:, dc, :], dc == 0, dc == 7, [skey, ("xb", dc)], PK(bg), chain=True)
                        for dc in range(8):
                            mm(psb[bu][:, :], sv[:, 1, dc, half * 128:(half + 1) * 128], xb[:, dc, :], dc == 0, dc == 7, [skey, ("xb", dc)], PK(bu), chain=True)
                        s_, sk = scr_get()
                        fl.op("act", lambda e, s_=s_, bg=bg: e.activation(out=s_[:], in_=psb[bg][:, :], func=AF.Silu), reads=[PK(bg)], writes=[sk])
                        fl.op("dve", lambda e, s_=s_, bu=bu, fcl=fcl: e.tensor_tensor(out=hbuf[:, fcl, :], in0=psb[bu][:, :], in1=s_[:], op=ALU.mult),
                              reads=[PK(bu), sk], writes=[("hbuf", fcl)])
                nfc = (fb1 - fb0) * 2
                if hf == 1 and w == 1 and NEXT_TILE[0] is not None:
                    prefetch_xb(*NEXT_TILE[0])
                lns = ln_begin() if hf == 1 else None
                pend_m = None
                for ob in range(NOB):
                    slot, skey = streamB(((l * 2 + w) * NOB + ob) * 128, grp, fb0 * 2, nfc)
                    for half in range(2):
                        m = ob * 2 + half
                        by = ps_get()
                        for fc in range(nfc):
                            mm(psb[by][:, :], slot[:, fc, half * 128:(half + 1) * 128], hbuf[:, fc, :], fc == 0, fc == nfc - 1, [skey, ("hbuf", fc)], PK(by), chain=True)
                        if lns is not None and pend_m is not None:
                            ln_feed(lns, pend_m)
                        pend_m = m
                        fl.op("dve", lambda e, by=by, m=m: e.scalar_tensor_tensor(out=x32[:, m, :], in0=psb[by][:, :], scalar=0.5 / ALPHA, in1=x32[:, m, :],
                                                                                    op0=ALU.mult, op1=ALU.add), reads=[PK(by), ("x32", m)], writes=[("x32", m)])
            ln_feed(lns, pend_m)
            ln_finish(l, 0 if w == 0 else 2, lns, last_stage=(w == 1))

        def x_src(l):
            return x_d if l == 0 else xs_d

        def prefetch_xb(l, i):
            src = x_src(l)
            rd = [("xs", i)] if l > 0 else []
            fl.dma("pool", lambda e: e.dma_start(out=xb[:].rearrange("p m t -> p (m t)"), in_=src[i * 128:(i + 1) * 128, :]), reads=rd, writes=XBALL, dsem="xbf")

        def load_x32(l, i):
            src = x_src(l)
            rd = [("xs", i)] if l > 0 else []
            fl.dma("pool", lambda e: e.dma_start(out=x32[:].rearrange("p m t -> p (m t)"), in_=src[i * 128:(i + 1) * 128, :]), reads=rd, writes=X32ALL, dsem="xf")

        def store_x(i):
            fl.dma("pool", lambda e: e.dma_start(out=o_d[i * 128:(i + 1) * 128, :], in_=x32[:].rearrange("p m t -> p (m t)")), reads=X32ALL, dsem="out")

        def spill_x(i):
            fl.dma("pool", lambda e: e.dma_start(out=xs_d[i * 128:(i + 1) * 128, :], in_=x32[:].rearrange("p m t -> p (m t)")), reads=X32ALL, writes=[("xs", i)], dsem=("xs", i))

        def mixer(l, i):
            stl = state[0]
            grp = (l, "mix")
            ke, vsl, kw, vw, kcvc, ht, kct, vce = (stl[k] for k in ("ke", "vsl", "kw", "vw", "kcvc", "ht", "kct", "vce"))
            wslot = i % 2
            zc = stl["zc"]

            def inproj_chunk(slot, skey, c, M=128):
                b = ps_get()
                sv = slot[:].rearrange("p (d c) -> p d c", d=8)
                for dc in range(8):
                    mm(psb[b][0:M, :], sv[:, dc, c * 128:c * 128 + M], xb[:, dc, :], dc == 0, dc == 7, [skey, ("xb", dc)], PK(b), chain=True)
                return b

            if i == 0:
                fl.op("pool", lambda e: e.memset(zc[:, :, 0:2], 0.0), writes=[("zc", 0), ("zc", 1)])
                for g in range(2):
                    fl.op("pool", lambda e, g=g: e.memset(kcvc[g][:, 0:16], 0.0), writes=[("kcvc", l, g)])
            else:
                for g in range(2):
                    fl.op("pool", lambda e, g=g: e.tensor_copy(out=kcvc[g][:, 0:16], in_=kcvc[g][:, T:T + 16]), reads=[("kcvc", l, g)], writes=[("kcvc", l, g)])
            slot, skey = streamA("WIN", (l * 6 + 0) * 128, grp)
            for c in range(4):
                b = inproj_chunk(slot, skey, c)
                fl.op("act", lambda e, b=b, c=c: e.copy(out=qs[c][0:64, :], in_=psb[b][0:64, :]), reads=[PK(b)], writes=[("qs", c)])
                fl.op("act", lambda e, b=b, c=c: e.copy(out=qs[c + 4][64:128, :], in_=psb[b][64:128, :]), reads=[PK(b)], writes=[("qs", c + 4)])
            slot, skey = streamA("WIN", (l * 6 + 1) * 128, grp)
            b = inproj_chunk(slot, skey, 0)
            fl.op("act", lambda e, b=b: e.copy(out=ke[0][0:64, i * T:(i + 1) * T], in_=psb[b][0:64, :]), reads=[PK(b)], writes=[("ke", 0)])
            fl.op("act", lambda e, b=b: e.copy(out=ke[1][64:128, i * T:(i + 1) * T], in_=psb[b][64:128, :]), reads=[PK(b)], writes=[("ke", 1)])
            b = inproj_chunk(slot, skey, 1)
            fl.op("dve", lambda e, b=b: e.tensor_copy(out=kw[:, wslot, :], in_=psb[b][:, :]), reads=[PK(b)], writes=[("kw", l)])
            for g in range(2):
                b = inproj_chunk(slot, skey, 2 + g)
                fl.op("act", lambda e, b=b, g=g: e.copy(out=kcvc[g][:, 16:16 + T], in_=psb[b][:, :]), reads=[PK(b)], writes=[("kcvc", l, g)])
            slot, skey = streamA("WIN", (l * 6 + 2) * 128, grp)
            for cc in range(2):
                b = inproj_chunk(slot, skey, cc)
                fl.op("act", lambda e, b=b, cc=cc: e.copy(out=ha[:, cc, :], in_=psb[b][:, :]), reads=[PK(b)], writes=[("ha", cc)])
            for cc in range(2):
                b = inproj_chunk(slot, skey, 2 + cc)
                fl.op("dve", lambda e, b=b, cc=cc: e.tensor_tensor(out=zc[:, cc, 2:2 + T], in0=psb[b][:, :], in1=ha[:, cc, :], op=ALU.mult),
                      reads=[PK(b), ("ha", cc)], writes=[("zc", cc)])
            slot, skey = streamA("WIN", (l * 6 + 3) * 128, grp)
            for cc in range(2):
                b = inproj_chunk(slot, skey, cc)
                w0 = spt[:, l, 48 + cc * 3 + 0:48 + cc * 3 + 1]
                w1_ = spt[:, l, 48 + cc * 3 + 1:48 + cc * 3 + 2]
                w2_ = spt[:, l, 48 + cc * 3 + 2:48 + cc * 3 + 3]
                cacc, ck_ = scr_get()
                fl.op("dve", lambda e, cc=cc, w2_=w2_, cacc=cacc: e.tensor_scalar(out=cacc[:], in0=zc[:, cc, 2:2 + T], scalar1=w2_, scalar2=None, op0=ALU.mult),
                      reads=[("zc", cc), "spt"], writes=[ck_])
                fl.op("dve", lambda e, cc=cc, w1_=w1_, cacc=cacc: e.scalar_tensor_tensor(out=cacc[:], in0=zc[:, cc, 1:1 + T], scalar=w1_, in1=cacc[:], op0=ALU.mult, op1=ALU.add),
                      reads=[("zc", cc), "spt", ck_], writes=[ck_])
                fl.op("dve", lambda e, cc=cc, w0=w0, cacc=cacc: e.scalar_tensor_tensor(out=cacc[:], in0=zc[:, cc, 0:T], scalar=w0, in1=cacc[:], op0=ALU.mult, op1=ALU.add),
                      reads=[("zc", cc), "spt", ck_], writes=[ck_])
                fl.op("dve", lambda e, b=b, cc=cc, cacc=cacc: e.tensor_tensor(out=ybuf[:, cc, :], in0=psb[b][:, :], in1=cacc[:], op=ALU.mult),
                      reads=[PK(b), ck_], writes=[("ybuf", cc)])
                fl.op("pool", lambda e, cc=cc: e.tensor_copy(out=zc[:, cc, 0:2], in_=zc[:, cc, T:T + 2]), reads=[("zc", cc)], writes=[("zc", cc)])
            for cc in range(2):
                b = inproj_chunk(slot, skey, 2 + cc)
                fl.op("act", lambda e, b=b, cc=cc: e.activation(out=ub[:, cc, :], in_=psb[b][:, :], func=AF.Gelu_apprx_tanh), reads=[PK(b)], writes=[("ub", cc)])
            slot, skey = streamA("WIN", (l * 6 + 4) * 128, grp)
            for cc in range(2):
                b = inproj_chunk(slot, skey, cc)
                fl.op("act", lambda e, b=b, cc=cc: e.activation(out=vg[:, cc, :], in_=psb[b][:, :], func=AF.Gelu_apprx_tanh), reads=[PK(b)], writes=[("vg", cc)])
            b = inproj_chunk(slot, skey, 2, M=24)
            fl.op("act", lambda e, b=b: e.activation(out=g24[:], in_=psb[b][0:24, :], func=AF.Sigmoid), reads=[PK(b)], writes=["g24"])
            slot, skey = streamA("WIN", (l * 6 + 5) * 128, grp)
            sv5 = slot[:].rearrange("p (d c) -> p d c", d=8)
            for stn in range(NST):
                b = ps_get()
                for dc in range(8):
                    mm(psb[b][:, 0:256], xb[:, dc, stn * 128:(stn + 1) * 128], sv5[:, dc, 0:256], dc == 0, dc == 7, [skey, ("xb", dc)], PK(b))
                kt = i * NST + stn
                wk = wslot * NST + stn
                fl.op("act", lambda e, b=b, kt=kt: e.copy(out=vsl[:, kt, 0:64], in_=psb[b][:, 0:64]), reads=[PK(b)], writes=[("vsl", l)])
                fl.op("act", lambda e, b=b, kt=kt: e.copy(out=vsl[:, kt, 128:192], in_=psb[b][:, 64:128]), reads=[PK(b)], writes=[("vsl", l)])
                fl.op("dve", lambda e, b=b, wk=wk: e.tensor_copy(out=vw[:, wk, 0:64], in_=psb[b][:, 128:192]), reads=[PK(b), ("vsl", l)], writes=[("vw", l)])
                fl.op("dve", lambda e, b=b, wk=wk: e.tensor_copy(out=vw[:, wk, 128:192], in_=psb[b][:, 192:256]), reads=[PK(b), ("vsl", l)], writes=[("vw", l)])
            dump(f"qt_{l}_{i}", qs[0][:], ("qs", 0), [128, T], BF16)
            if i < 2:
                for h_ in range(8):
                    g_ = h_ // 4
                    fl.op("pool", lambda e, h_=h_, g_=g_: e.memset(qs[h_][(1 - g_) * 64:(1 - g_) * 64 + 64, :], 0.0), writes=[("qs", h_)])

            if stages < 2.1:
                return
            import os
            SKIPG = os.environ.get("SKIPG", "")
            b1 = ps_get()
            b2 = ps_get()
            for cc in range(2):
                mm(psb[b1][:, :], ones[:], vg[:, cc, :], cc == 0, cc == 1, ["ones", ("vg", cc)], PK(b1))
            for cc in range(2):
                s_, sk = scr_get()
                fl.op("act", lambda e, s_=s_, cc=cc: e.activation(out=s_[:], in_=vg[:, cc, :], func=AF.Square), reads=[("vg", cc)], writes=[sk])
                mm(psb[b2][:, :], ones[:], s_[:], cc == 0, cc == 1, ["ones", sk], PK(b2))
            fl.op("dve", lambda e: e.tensor_scalar(out=mean[:], in0=psb[b1][:, :], scalar1=1.0 / 256, scalar2=None, op0=ALU.mult), reads=[PK(b1)], writes=["mean"])
            fl.op("pool", lambda e: e.tensor_tensor(out=rstd[:], in0=mean[:], in1=mean[:], op=ALU.mult), reads=["mean"], writes=["rstd"])
            fl.op("dve", lambda e: e.scalar_tensor_tensor(out=rstd[:], in0=psb[b2][:, :], scalar=1.0 / 256, in1=rstd[:], op0=ALU.mult, op1=ALU.subtract),
                  reads=[PK(b2), "rstd"], writes=["rstd"])
            fl.op("dve", lambda e: e.tensor_scalar(out=rstd[:], in0=rstd[:], scalar1=EPS, scalar2=None, op0=ALU.add), reads=["rstd"], writes=["rstd"])
            fl.op("act", lambda e: e.activation(out=rstd[:], in_=rstd[:], func=AF.Ln), reads=["rstd"], writes=["rstd"])
            fl.op("act", lambda e: e.activation(out=rstd[:], in_=rstd[:], func=AF.Exp, scale=-0.5), reads=["rstd"], writes=["rstd"])
            for cc in range(2):
                fl.op("dve", lambda e, cc=cc: e.tensor_tensor(out=vg[:, cc, :], in0=vg[:, cc, :], in1=mean[:], op=ALU.subtract), reads=[("vg", cc), "mean"], writes=[("vg", cc)])
                fl.op("dve", lambda e, cc=cc: e.tensor_tensor(out=vg[:, cc, :], in0=vg[:, cc, :], in1=rstd[:], op=ALU.mult), reads=[("vg", cc), "rstd"], writes=[("vg", cc)])
                fl.op("act", lambda e, cc=cc: e.activation(out=vg[:, cc, :], in_=vg[:, cc, :], func=AF.Identity, bias=spt[:, l, 56 + cc:57 + cc], scale=spt[:, l, 54 + cc:55 + cc]),
                      reads=[("vg", cc), "spt"], writes=[("vg", cc)])
            for stn in range(NST if "t" not in SKIPG else 0):
                b = ps_get()
                for cc in range(2):
                    fl.op("pe", lambda e, b=b, cc=cc, stn=stn: e.transpose(out=psb[b][:, cc * 128:(cc + 1) * 128], in_=vg[:, cc, stn * 128:(stn + 1) * 128], identity=ident[:]),
                          reads=[("vg", cc), "ident"], writes=[PK(b)])
                fl.op("act", lambda e, b=b, stn=stn: e.copy(out=vnT[:, stn, :], in_=psb[b][:, 0:256]), reads=[PK(b)], writes=["vnT"])
            for hd in range(4 if "h" not in SKIPG else 0):
                cc, hh = hd // 2, hd % 2
                b = ps_get()
                for stn in range(NST):
                    mm(psb[b][0:64, stn * 128:(stn + 1) * 128], vnT[:, stn, hd * 64:(hd + 1) * 64], gwm[:, l, hd, :], True, False, ["vnT", "gwm"], PK(b))
                    mm(psb[b][0:64, stn * 128:(stn + 1) * 128], onesb[0:1, :], gbr[0:1, l, hd * 128:(hd + 1) * 128], False, True, ["onesb", "gbr"], PK(b))
                fl.op("dve", lambda e, b=b, cc=cc, hh=hh: e.tensor_tensor(out=ybuf[hh * 64:(hh + 1) * 64, 2 + cc, :], in0=psb[b][0:64, :], in1=ub[hh * 64:(hh + 1) * 64, cc, :], op=ALU.mult),
                      reads=[PK(b), ("ub", cc)], writes=[("ybuf", 2 + cc, hh)])

            if stages < 2.2:
                return
            w1slot, w1key = streamA("W1", l * 128, grp)
            w1v = w1slot[:].rearrange("p (a f) -> p a f", a=32)
            import os
            if i == 0:
                bpe = [ps_get(), ps_get()]
                for kv in range(2):
                    for lp in range(32):
                        mm(psb[bpe[kv]][:, 0:1], w1v[kv * 64:(kv + 1) * 64, lp, :], spt_b[kv * 64:(kv + 1) * 64, l, lp:lp + 1], lp == 0, lp == 31, [w1key, "sptb"], PK(bpe[kv]), chain=True)
                for kv in range(2):
                    fl.op("dve", lambda e, kv=kv: e.tensor_tensor(out=cbias[:, l, kv:kv + 1], in0=psb[bpe[kv]][:, 0:1], in1=spt[:, l, 58 + kv:59 + kv], op=ALU.add),
                          reads=[PK(bpe[kv]), "spt"], writes=["cbias"])
            for g in range(2):
                for lp in range(32):
                    fl.op("dve", lambda e, g=g, lp=lp: e.tensor_copy(out=kb[:, lp, g * 32:(g + 1) * 32], in_=kcvc[g][:, lp:lp + 497:16]),
                          reads=[("kcvc", l, g)], writes=[("kb", g, lp)])
            bkv = [ps_get(), ps_get()]
            for kv in range(2):
                for lp in range(32):
                    mm(psb[bkv[kv]][:, 0:64], w1v[kv * 64:(kv + 1) * 64, lp, :], kb[kv * 64:(kv + 1) * 64, lp, :], lp == 0, lp == 31,
                       [w1key, ("kb", 0, lp), ("kb", 1, lp)], PK(bkv[kv]), chain=True)
            for kv in range(2):
                fl.op("act", lambda e, kv=kv: e.activation(out=ht[:, kv, :, i * 32:(i + 1) * 32], in_=psb[bkv[kv]][:, 0:64].rearrange("p (g n) -> p g n", g=2),
                                                             func=AF.Gelu_apprx_tanh, bias=cbias[:, l, kv:kv + 1]), reads=[PK(bkv[kv]), "cbias"], writes=[("ht", l)])
            b = ps_get()
            for g in range(2):
                mm(psb[b][:, 0:32], w2b[:, l, g * 128:(g + 1) * 128], ht[:, 0, g, i * 32:(i + 1) * 32], g == 0, g == 1, ["w2b", ("ht", l)], PK(b))
            fl.op("act", lambda e, b=b: e.copy(out=kct[:, i * 32:(i + 1) * 32], in_=psb[b][:, 0:32]), reads=[PK(b)], writes=[("kct", l)])
            cch = i // 4
            b = ps_get()
            for g in range(2):
                mm(psb[b][:, g * 64:(g + 1) * 64], ht[:, 1, g, cch * 128:(cch + 1) * 128], w2b[:, l, 256:320], True, True, [("ht", l), "w2b"], PK(b))
            fl.op("act", lambda e, b=b: e.copy(out=vce[:, cch, 0:64], in_=psb[b][:, 0:64]), reads=[PK(b)], writes=[("vce", l)])
            fl.op("act", lambda e, b=b: e.copy(out=vce[:, cch, 128:192], in_=psb[b][:, 64:128]), reads=[PK(b)], writes=[("vce", l)])
            if cch == 0:
                fl.op("pool", lambda e: e.memset(vce[0:1, 0, :], 0.0), writes=[("vce", l)])
            dump(f"kct_{l}_{i}", kct[:], ("kct", l), [128, 256], BF16)

            if stages < 2.3:
                return
            def gate_bcast(h, br):
                g = h // 4
                D0 = (1 - g) * 64
                bgt = ps_get()
                jcol = h * 3 + br
                k_ = (h * 3 + br) % 2
                fl.op("dve", lambda e: e.tensor_scalar(out=gj[k_][:], in0=g24[:], scalar1=selg[:, jcol:jcol + 1], scalar2=None, op0=ALU.mult),
                      reads=["g24", "selg"], writes=[("gj", k_)])
                mm(psb[bgt][:, :], ones24[:], gj[k_][:], True, True, ["ones24", ("gj", k_)], PK(bgt))
                gs_, gk = scr_get()
                fl.op("act", lambda e: e.copy(out=gs_[D0:D0 + 64, :], in_=psb[bgt][D0:D0 + 64, :]), reads=[PK(bgt)], writes=[gk])
                return gs_, gk

            def combine_multi(h, items):
                g, m = h // 4, h % 4
                N0, D0 = g * 64, (1 - g) * 64
                st_ = []
                for (br, acc, first) in items:
                    gs_, gk = gate_bcast(h, br)
                    cf_, ck = scr_get()
                    st_.append((br, acc, first, gs_, gk, cf_, ck))
                for (br, acc, first, gs_, gk, cf_, ck) in st_:
                    fl.op("dve", lambda e, acc=acc, cf_=cf_: e.tensor_scalar(out=cf_[D0:D0 + 64, :], in0=psb[acc][D0:D0 + 64, :], scalar1=1e-30, scalar2=None, op0=ALU.add),
                          reads=[PK(acc)], writes=[ck])
                for (br, acc, first, gs_, gk, cf_, ck) in st_:
                    fl.op("dve", lambda e, cf_=cf_: e.reciprocal(out=cf_[D0:D0 + 64, :], in_=cf_[D0:D0 + 64, :]), reads=[ck], writes=[ck])
                for (br, acc, first, gs_, gk, cf_, ck) in st_:
                    fl.op("pool", lambda e, cf_=cf_, gs_=gs_: e.tensor_tensor(out=cf_[D0:D0 + 64, :], in0=cf_[D0:D0 + 64, :], in1=gs_[D0:D0 + 64, :], op=ALU.mult),
                          reads=[ck, gk], writes=[ck])
                for (br, acc, first, gs_, gk, cf_, ck) in st_:
                    if first:
                        fl.op("dve", lambda e, acc=acc, cf_=cf_: e.tensor_tensor(out=ybuf[N0:N0 + 64, 4 + m, :], in0=psb[acc][N0:N0 + 64, :], in1=cf_[D0:D0 + 64, :], op=ALU.mult),
                              reads=[PK(acc), ck], writes=[("yc", h)])
                    else:
                        fl.op("dve", lambda e, acc=acc, cf_=cf_, gs_=gs_: e.tensor_tensor(out=gs_[N0:N0 + 64, :], in0=psb[acc][N0:N0 + 64, :], in1=cf_[D0:D0 + 64, :], op=ALU.mult),
                              reads=[PK(acc), ck, gk], writes=[gk])
                for (br, acc, first, gs_, gk, cf_, ck) in st_:
                    if not first:
                        fl.op("pool", lambda e, gs_=gs_: e.tensor_tensor(out=ybuf[N0:N0 + 64, 4 + m, :], in0=ybuf[N0:N0 + 64, 4 + m, :], in1=gs_[N0:N0 + 64, :], op=ALU.add),
                              reads=[("yc", h), gk], writes=[("yc", h)])

            ptn = dict(n=0)

            def next_pt():
                k_ = ptn["n"] % 4
                ptn["n"] += 1
                return k_

            nch = 1 if i < 4 else 2
            topkB1 = []
            cmp_prev = []
            do_sel = i >= 2
            for g in range(2):
                bimp = [ps_get(hold=True) for _ in range(NST // 2)] if do_sel else None
                for r in range(4):
                    h = g * 4 + r
                    m = r
                    acc = ps_get(hold=True)
                    kcs = []
                    bss = []
                    for c in range(nch):
                        bs = ps_get()
                        bss.append(bs)
                        ip = i - 4 * c
                        masked = ip < 4
                        mm(psb[bs][:, :], kct[g * 64:(g + 1) * 64, c * 128:(c + 1) * 128], qs[h][g * 64:(g + 1) * 64, :], True, not masked,
                           [("kct", l), ("qs", h)], PK(bs))
                        if masked:
                            mm(psb[bs][:, :], identb[:], cm[:, ip, :], False, True, ["identb", "cm"], PK(bs))
                    for c in range(nch):
                        bs = bss[c]
                        k_ = next_pt()
                        kcs.append(k_)
                        fl.op("act", lambda e, bs=bs, k_=k_: e.activation(out=pt[k_][:], in_=psb[bs][:, :], func=AF.Exp, scale=SCALE), reads=[PK(bs)], writes=[("pt", k_)])
                    for c in range(nch):
                        k_ = kcs[c]
                        mm(psb[acc][:, :], vce[:, c, g * 64:g * 64 + 128], pt[k_][:], c == 0, c == nch - 1, [("vce", l), ("pt", k_)], PK(acc))
                    if do_sel:
                        for stn in range(NST):
                            bi = bimp[stn // 2]
                            o0 = (stn % 2) * 256 + r * 64
                            for c in range(nch):
                                mm(psb[bi][:, o0:o0 + 64], pt[kcs[c]][:, stn * 128:(stn + 1) * 128], ovt[:, c, 0:64], c == 0, c == nch - 1, [("pt", kcs[c]), "ovt"], PK(bi))
                    combine_multi(h, [(0, acc, True)])
                    for b_ in cmp_prev:
                        ps_rel(b_)
                    cmp_prev[:] = [acc]
                if do_sel:
                    def topkA(g=g, bimp=bimp):
                        SR = range(NST)
                        for bk_ in range(NST // 2):
                            fl.op("dve", lambda e, bk_=bk_, bimp=bimp: e.tensor_copy(out=impsb[:, bk_, :], in_=psb[bimp[bk_]][:, :]), reads=[PK(bimp[bk_])], writes=[("impsb", bk_)])
                        v3s = [impsb[:, stn // 2, (stn % 2) * 256:(stn % 2) * 256 + 256].rearrange("p (r c) -> p r c", r=4) for stn in SR]
                        for r in range(4):
                            for stn in SR:
                                fl.op("dve", lambda e, stn=stn, r=r, v3=v3s[stn]: e.reduce_sum(out=rd4[:, stn, r:r + 1], in_=v3[:, r, :], axis=mybir.AxisListType.X),
                                      reads=[("impsb", stn // 2)], writes=[("rd4", stn)])
                        for stn in SR:
                            fl.op("dve", lambda e, stn=stn: e.tensor_scalar(out=rd4[:, stn, :], in0=rd4[:, stn, :], scalar1=1e-30, scalar2=None, op0=ALU.add),
                                  reads=[("rd4", stn)], writes=[("rd4", stn)])
                        for stn in SR:
                            fl.op("dve", lambda e, stn=stn: e.reciprocal(out=rd4[:, stn, :], in_=rd4[:, stn, :]), reads=[("rd4", stn)], writes=[("rd4", stn)])
                        for stn in SR:
                            fl.op("dve", lambda e, stn=stn, v3=v3s[stn]: e.tensor_scalar(out=impa[:, stn, :], in0=v3[:, 0, 0:64], scalar1=rd4[:, stn, 0:1], scalar2=None, op0=ALU.mult),
                                  reads=[("impsb", stn // 2), ("rd4", stn)], writes=[("impa", stn)])
                        for r in range(1, 4):
                            for stn in SR:
                                fl.op("dve", lambda e, stn=stn, r=r, v3=v3s[stn]: e.scalar_tensor_tensor(out=impa[:, stn, :], in0=v3[:, r, 0:64], scalar=rd4[:, stn, r:r + 1], in1=impa[:, stn, :],
                                                                                             op0=ALU.mult, op1=ALU.add),
                                      reads=[("impsb", stn // 2), ("rd4", stn), ("impa", stn)], writes=[("impa", stn)])
                        for stn in SR:
                            sg_ = i * NST + stn
                            fl.op("dve", lambda e, stn=stn, sg_=sg_: e.tensor_tensor(out=impa[:, stn, :], in0=impa[:, stn, :], in1=rt[:, 64 - 2 * sg_:128 - 2 * sg_], op=ALU.add),
                                  reads=[("impa", stn), "rt"], writes=[("impa", stn)])
                        for stn in SR:
                            fl.op("dve", lambda e, stn=stn: e.memset(impa[:, stn, 0:1], 1e30), reads=[("impa", stn)], writes=[("impa", stn)])
                        for stn in SR:
                            fl.op("dve", lambda e, stn=stn: e.max(out=m8a[:, stn, :], in_=impa[:, stn, :]), reads=[("impa", stn)], writes=[("m8a", stn)])
                        for stn in SR:
                            fl.op("dve", lambda e, stn=stn: e.match_replace(out=sc2[:, stn, :], in_to_replace=m8a[:, stn, :], in_values=impa[:, stn, :], imm_value=-3e38),
                                  reads=[("impa", stn), ("m8a", stn)], writes=[("sc2", stn)])
                        for stn in SR:
                            fl.op("dve", lambda e, stn=stn: e.max(out=m8b[:, stn, :], in_=sc2[:, stn, :]), reads=[("sc2", stn)], writes=[("m8b", stn)])
                        for stn in SR:
                            fl.op("dve", lambda e, stn=stn: e.tensor_scalar(out=selm[:, g * NST + stn, :], in0=impa[:, stn, :], scalar1=m8b[:, stn, 7:8], scalar2=1.0, op0=ALU.is_ge, op1=ALU.subtract),
                                  reads=[("impa", stn), ("m8b", stn)], writes=[("selm", g, stn)])
                    def topkB(g=g):
                        SR = range(NST)
                        h0_ = g * 4
                        rs_ = slice((1 - g) * 64, (1 - g) * 64 + 64)
                        for stn in SR:
                            bt = ps_get()
                            fl.op("pe", lambda e, bt=bt, stn=stn: e.transpose(out=psb[bt][0:64, 0:128], in_=selm[:, g * NST + stn, :], identity=ident[:]), reads=[("selm", g, stn), "ident"], writes=[PK(bt)])
                            fl.op("act", lambda e, bt=bt, stn=stn, h0_=h0_, rs_=rs_: e.copy(out=qs[h0_][rs_, stn * 128:(stn + 1) * 128], in_=psb[bt][0:64, 0:128]),
                                  reads=[PK(bt)], writes=[("qs", h0_)])
                        for r_ in range(1, 4):
                            h_ = g * 4 + r_
                            fl.op("dve", lambda e, h_=h_, h0_=h0_, rs_=rs_: e.tensor_copy(out=qs[h_][rs_, :], in_=qs[h0_][rs_, :]), reads=[("qs", h0_)], writes=[("qs", h_)])
                    if g == 0:
                        topkA()
                        topkB0 = topkB
                    else:
                        topkB0()
                        topkA()
                        topkB1.append(topkB)
                    for bb in bimp:
                        ps_rel(bb)
            if do_sel:
                pass

            if stages < 2.4:
                return
            for b_ in cmp_prev:
                ps_rel(b_)
            cmp_prev[:] = []
            prev_accs = []
            for h in range(8):
                if h == 4 and topkB1:
                    topkB1[0]()
                g, m = h // 4, h % 4
                gs = slice(g * 64, (g + 1) * 64)
                acc_s = ps_get(hold=True)
                acc_w = ps_get(hold=True)
                tasks = []
                nks = NST * i + NST
                for kt in range(nks):
                    tasks.append(("s", kt, kt == 0, kt == nks - 1))
                wk = [a for a in ([-1, -4, -3, -2, 0, 1, 2, 3] if i >= 1 else [0, 1, 2, 3])]
                for n_, a in enumerate(wk):
                    tasks.append(("w", a, n_ == 0, n_ == len(wk) - 1))
                st_ = [t_ for t_ in tasks if t_[0] == "s"]
                wt_ = [t_ for t_ in tasks if t_[0] == "w"]
                order = []
                while st_ or wt_:
                    if st_:
                        order.append(st_.pop(0))
                    if wt_:
                        order.append(wt_.pop(0))

                def score(task):
                    kind, idx, first, last = task
                    bs = ps_get()
                    if kind == "s":
                        kt = idx
                        a = kt - NST * i
                        q0 = 128 * a if a > 0 else 0
                        q1 = T
                        need_sel = do_sel
                        need_c = a >= 0
                        mm(psb[bs][:, q0:q1], ke[g][:, kt * 128:(kt + 1) * 128], qs[h][:, q0:q1], True, not need_c, [("ke", g), ("qs", h)], PK(bs))
                        if need_c:
                            mm(psb[bs][:, q0:q1], identb[:], wm[:, a + 4, q0:q1], False, True, ["identb", "wm"], PK(bs))
                        vap = vsl[:, kt, g * 64:g * 64 + 128]
                        vkey = ("vsl", l)
                        acc = acc_s
                    else:
                        a = idx
                        kt = NST * i + a
                        q0 = 128 * a if a > 0 else 0
                        q1 = T if a >= -1 else 128 * (a + 5)
                        sl_ = (kt // NST) % 2
                        c0 = (kt % NST) * 128
                        mm(psb[bs][:, q0:q1], kw[gs, sl_, c0:c0 + 128], qs[h][gs, q0:q1], True, False, [("kw", l), ("qs", h)], PK(bs))
                        mm(psb[bs][:, q0:q1], identb[:], wm[:, a + 4, q0:q1], False, True, ["identb", "wm"], PK(bs))
                        vap = vw[:, sl_ * NST + kt % NST, g * 64:g * 64 + 128]
                        vkey = ("vw", l)
                        acc = acc_w
                    k_ = next_pt()
                    fl.op("act", lambda e: e.activation(out=pt[k_][:, q0:q1], in_=psb[bs][:, q0:q1], func=AF.Exp, scale=SCALE), reads=[PK(bs)], writes=[("pt", k_)])
                    return (acc, vap, vkey, k_, q0, q1, first, last)

                def pv(info):
                    acc, vap, vkey, k_, q0, q1, first, last = info
                    mm(psb[acc][:, q0:q1], vap, pt[k_][:, q0:q1], first, last, [vkey, ("pt", k_)], PK(acc))

                pend = []
                for task in order:
                    pend.append(score(task))
                    if len(pend) > 2:
                        pv(pend.pop(0))
                while pend:
                    pv(pend.pop(0))
                combine_multi(h, [(1, acc_s, False), (2, acc_w, False)])
                for b_ in prev_accs:
                    ps_rel(b_)
                prev_accs[:] = [acc_s, acc_w]
            for b_ in prev_accs:
                ps_rel(b_)
            if dbg:
                fl.op("dve", lambda e: e.memset(gj[0][0:1, 0:1], 1.0), reads=[("yc", hh_) for hh_ in range(8)] + YBALL + [("gj", 0)], writes=["ybuf_all", ("gj", 0)])
            dump(f"y_{l}_{i}", ybuf[:], "ybuf_all", [128, 8, T], BF16)

            if stages < 2.5:
                return
            lns = ln_begin()
            pend_m = None
            for u in range(2):
                slot, skey = streamA("WOUT", (l * 2 + u) * 128, grp)
                sv = slot[:].rearrange("p (k c) -> p k c", k=8)
                for c in range(4):
                    m = u * 4 + c
                    b = ps_get()
                    for kc in range(8):
                        mm(psb[b][:, :], sv[:, kc, c * 128:(c + 1) * 128], ybuf[:, kc, :], kc == 0, kc == 7, [skey] + YKEYS[kc], PK(b))
                    if pend_m is not None:
                        ln_feed(lns, pend_m)
                    pend_m = m
                    fl.op("dve", lambda e, b=b, m=m: e.scalar_tensor_tensor(out=x32[:, m, :], in0=psb[b][:, :], scalar=1.0 / ALPHA, in1=x32[:, m, :], op0=ALU.mult, op1=ALU.add),
                          reads=[PK(b), ("x32", m)], writes=[("x32", m)])
            ln_feed(lns, pend_m)
            ln_finish(l, 1, lns)

        spt_b = sb("spt_b", [128, L, 32], BF16)
        for l in range(L):
            fl.op("dve", lambda e, l=l: e.tensor_copy(out=spt_b[:, l, :], in_=spt[:, l, 60:92]), reads=["spt"], writes=["sptb"])

        NEXT_TILE = [None]
        for l in range(n_layers):
            for i in range(n_tiles):
                if l == 0 and i == 0:
                    prefetch_xb(0, 0)
                load_x32(l, i)
                NEXT_TILE[0] = (l, i + 1) if i + 1 < n_tiles else ((l + 1, 0) if l + 1 < n_layers else None)
                if l == 0 and i == 0:
                    emit_pc([p for p in pc_l0 if p[3][1] == "f1"])
                if l == 0 and i == 1:
                    emit_pc(pc_l1[:len(pc_l1) // 2])
                if l == 0 and i == 2:
                    emit_pc(pc_l1[len(pc_l1) // 2:])
                if l == 0 and i == n_tiles - 1 and n_tiles < 3 and n_layers > 1:
                    emit_pc(pc_l1)
                if stages >= 1:
                    ffn(l, 0)
                if l == 0 and i == 0:
                    emit_pc([p for p in pc_l0 if p[3][1] != "f1"])
                if stages >= 1:
                    dump(f"x1_{l}_{i}", x32[:], "x32all", [128, 8, T])
                if stages >= 2:
                    mixer(l, i)
                    dump(f"x2_{l}_{i}", x32[:], "x32all", [128, 8, T])
                if stages >= 3:
                    ffn(l, 1)
                    dump(f"x3_{l}_{i}", x32[:], "x32all", [128, 8, T])
                if l == n_layers - 1:
                    store_x(i)
                else:
                    spill_x(i)

        fl.wait_all_dma("sp")
        fl.emit()
        print('sbuf_bytes_remaining', nc.sbuf_bytes_remaining)
        stats = dict(n_ins=fl.n_ins, n_wait=fl.n_wait, counts={k: v["count"] for k, v in fl.engs.items()}, nsem=len(fl.dsems))
    return nc, dbg_out, stats


_CACHE = {}


def kernel(**inputs):
    P = _pack({k: np.asarray(v) for k, v in inputs.items()})
    x = np.ascontiguousarray(np.asarray(inputs["x"], dtype=np.float32))
    B = x.shape[0]
    if "nc" not in _CACHE:
        _CACHE["nc"] = build()[0]
    nc = _CACHE["nc"]
    in_maps = []
    for b in range(B):
        d = dict(P)
        d["x"] = np.ascontiguousarray(x[b].reshape(NT, T, 8, 128).transpose(0, 3, 2, 1).reshape(NT * 128, 8 * T))
        in_maps.append(d)
    res = run_bass_kernel_spmd(nc, in_maps, core_ids=list(range(B)))
    out = np.stack([np.asarray(r["out"]).reshape(NT, 128, 8, T).transpose(0, 3, 2, 1).reshape(S, D) for r in res.results], axis=0)
    return out.astype(np.float32)
```

```python
import numpy as np
from contextlib import ExitStack
import concourse.bass as bass
import concourse.mybir as mybir
from concourse.bass_utils import run_bass_kernel_spmd

F32 = mybir.dt.float32
BF16 = mybir.dt.bfloat16
AF = mybir.ActivationFunctionType
ALU = mybir.AluOpType

D = 1024
S = 4096
L = 2
DFF = 2816
T = 512
NT = S // T
NST = T // 128
ALPHA = (2 * L) ** 0.25
EPS = 1e-5
EPSP = EPS / ALPHA ** 2
SCALE = 64 ** -0.5
NEGM = -30000.0
NFB = 11
NOB = 4
SPW = 412


class Flow:
    ENGS = ("pe", "act", "dve", "pool", "sp")

    def __init__(self, nc, stack):
        self.nc = nc
        self.stack = stack
        self.engs = {}
        for name in self.ENGS:
            sem = stack.enter_context(nc.semaphore(f"s_{name}"))
            self.engs[name] = dict(prog=[], sem=sem, count=0, waited={})
        self.lastw = {}
        self.readers = {}
        self.dsems = {}
        self.n_ins = 0
        self.n_wait = 0

    def _deps(self, eng, reads, writes, skip_sem=None):
        deps = {}

        def need(tok):
            sem, val, src = tok
            if src == eng and eng in ("pe", "sp"):
                return
            if sem is skip_sem:
                return
            k = id(sem)
            if k not in deps or deps[k][1] < val:
                deps[k] = (sem, val)

        for r in reads:
            t = self.lastw.get(r)
            if t is not None:
                need(t)
        for w in writes:
            t = self.lastw.get(w)
            if t is not None:
                need(t)
            for t in self.readers.get(w, {}).values():
                need(t)
        E = self.engs[eng]
        for k, (sem, val) in deps.items():
            if E["waited"].get(k, 0) < val:
                E["prog"].append(("wait", sem, val))
                E["waited"][k] = val
                self.n_wait += 1

    def _commit(self, tok, reads, writes):
        k = id(tok[0])
        for r in reads:
            self.readers.setdefault(r, {})[k] = tok
        for w in writes:
            self.lastw[w] = tok
            self.readers[w] = {}

    def op(self, eng, fn, reads=(), writes=(), inc=True):
        self._deps(eng, reads, writes)
        E = self.engs[eng]
        if inc:
            E["count"] += 1
            tok = (E["sem"], E["count"], eng)
            E["prog"].append(("ins", fn, E["sem"], 1))
        else:
            tok = (E["sem"], E["count"] + 1, eng)
            E["prog"].append(("ins0", fn))
        self._commit(tok, reads, writes)
        self.n_ins += 1

    def dma(self, eng, fn, reads=(), writes=(), dsem=None):
        if dsem is None:
            dsem = ("d", writes[0] if writes else reads[0])
        dsem = (dsem, eng)
        if dsem not in self.dsems:
            sem = self.stack.enter_context(self.nc.semaphore(f"sd{len(self.dsems)}"))
            self.dsems[dsem] = [sem, 0, set()]
        Dm = self.dsems[dsem]
        self._deps(eng, reads, writes, skip_sem=Dm[0])
        Dm[1] += 16
        tok = (Dm[0], Dm[1], "dma")
        self.engs[eng]["prog"].append(("dma", fn, Dm[0], 1))
        Dm[2].update(writes)
        self._commit(tok, reads, writes)
        self.n_ins += 1

    def seal(self, dsem):
        for key, Dm in self.dsems.items():
            if key[0] != dsem:
                continue
            tok = (Dm[0], Dm[1], "dma")
            for k in Dm[2]:
                if k in self.lastw and self.lastw[k][0] is Dm[0]:
                    self.lastw[k] = tok

    def wait_all_dma(self, eng="sp"):
        E = self.engs[eng]
        for name, Dm in self.dsems.items():
            if Dm[1] > 0:
                E["prog"].append(("wait", Dm[0], Dm[1]))

    def emit(self):
        nc = self.nc
        handles = dict(pe="tensor", act="scalar", dve="vector", pool="gpsimd", sp="sync")
        with nc.Block() as block:
            for name in self.ENGS:
                prog = self.engs[name]["prog"]

                def body(eng, prog=prog):
                    for item in prog:
                        if item[0] == "wait":
                            eng.wait_ge(item[1], item[2])
                        elif item[0] == "ins0":
                            item[1](eng)
                        elif item[0] == "ins":
                            item[1](eng).then_inc(item[2], 1)
                        else:
                            item[1](eng).then_inc(item[2], 16)

                getattr(block, handles[name])(body)


def _win_cols():
    q0 = 1280
    cols = []
    for m in range(4):
        cols += list(range(q0 + m * 64, q0 + (m + 1) * 64)) + list(range(q0 + (m + 4) * 64, q0 + (m + 5) * 64))
    cols += list(range(2048, 2176))
    cols += list(range(2304, 2432))
    cols += list(range(1792, 1856)) + list(range(1920, 1984))
    cols += list(range(1856, 1920)) + list(range(1984, 2048))
    cols += list(range(0, 256))
    cols += list(range(512, 768))
    cols += list(range(256, 512))
    cols += list(range(768, 1024))
    cols += list(range(1024, 1280))
    cols += list(range(2560, 2584)) + [-1] * 104
    cols += [-1] * 128
    cols += list(range(2176, 2304))
    cols += list(range(2432, 2560))
    cols += [-1] * 256
    return np.array(cols)


def _wout_rows():
    rows = list(range(0, 512))
    for m in range(4):
        rows += list(range(512 + m * 64, 512 + (m + 1) * 64)) + list(range(512 + (m + 4) * 64, 512 + (m + 5) * 64))
    return np.array(rows)


def _consts():
    c = {}
    c["IDENT"] = np.eye(128, dtype=np.float32)
    qq = np.arange(128)
    c["TRILT"] = (qq[:, None] <= qq[None, :]).astype(np.float32)
    k = np.arange(128)[:, None, None]
    a = (np.arange(8) - 4)[None, :, None]
    q = np.arange(512)[None, None, :]
    dist = q - 128 * a - k
    c["WM"] = np.where((dist >= 0) & (dist < 512), 0.0, NEGM).astype(np.float32).reshape(128, 8 * 512)
    ip = np.arange(4)[None, :, None]
    c["CM"] = np.where(16 * k + 15 <= 512 * ip + q, 0.0, NEGM).astype(np.float32).reshape(128, 4 * 512)
    j = np.arange(64)[:, None]
    cc = np.arange(4096)[None, :]
    em1 = np.where(j == cc // 64, -NEGM, 0.0).astype(np.float32)
    c["EM"] = np.concatenate([em1, em1], axis=0)
    ql = np.arange(128)[:, None]
    m = np.arange(128)[None, :] - 64
    cur = ql // 64
    rt = np.zeros((128, 128), np.float32)
    rt[(m == cur) | (m == cur - 1)] = 1e30
    rt[m > cur] = -1e30
    c["RT"] = rt
    ov = np.zeros((128, 2, 65), np.float32)
    for ch in range(2):
        for nl in range(128):
            npr = 128 * ch + nl
            if npr == 0:
                continue
            n = npr - 1
            st = 16 * n
            for jb in range(64):
                o = min(st + 32, jb * 64 + 64) - max(st, jb * 64)
                if o > 0:
                    ov[nl, ch, jb] = o / 32.0
            ov[nl, ch, 64] = 1.0
    c["OV"] = ov.reshape(128, 130)
    c["SELG"] = np.eye(24, dtype=np.float32)
    return c


def _pack(inp):
    f = np.float32
    P = {}
    GU = np.empty((L, 2, NFB, 128, 4096), f)
    DN = np.empty((L, 2, NOB, 128, 5632), f)
    for l in range(L):
        for w, nm in enumerate(("ffn1", "ffn2")):
            g = inp[nm + "_gate"][l].reshape(8, 128, NFB, 256).transpose(2, 1, 0, 3)
            u = inp[nm + "_up"][l].reshape(8, 128, NFB, 256).transpose(2, 1, 0, 3)
            GU[l, w] = np.stack([g, u], axis=2).reshape(NFB, 128, 4096)
            DN[l, w] = inp[nm + "_down"][l].reshape(22, 128, NOB, 256).transpose(2, 1, 0, 3).reshape(NOB, 128, 5632)
    P["GU"] = GU.reshape(L * 2 * NFB * 128, 4096)
    P["DN"] = DN.reshape(L * 2 * NOB * 128, 5632)
    cols = _win_cols()
    valid = cols >= 0
    WIN = np.empty((L, 6, 128, 4096), f)
    WOUT = np.empty((L, 2, 128, 4096), f)
    W1 = np.empty((L, 128, 4096), f)
    SP = np.zeros((L, 128, SPW), f)
    GW = np.empty((L, 128, 512), f)
    GB = np.empty((L, 1, 512), f)
    rows = _wout_rows()
    for l in range(L):
        wp = np.zeros((1024, 3072), f)
        wp[:, valid] = inp["w_in"][l][:, cols[valid]]
        WIN[l] = wp.reshape(8, 128, 6, 512).transpose(2, 1, 0, 3).reshape(6, 128, 4096)
        wo = inp["w_out"][l][rows, :]
        WOUT[l] = wo.reshape(8, 128, 2, 512).transpose(2, 1, 0, 3).reshape(2, 128, 4096)
        W1[l] = np.concatenate([inp["ck_w1"][l].transpose(1, 0, 2), inp["cv_w1"][l].transpose(1, 0, 2)], axis=0).reshape(128, 4096)
        for jn, nm in enumerate(("ln1_g", "ln1_b", "ln2_g", "ln2_b", "ln3_g", "ln3_b")):
            SP[l, :, jn * 8:(jn + 1) * 8] = inp[nm][l].reshape(8, 128).T
        SP[l, :, 48:54] = inp["conv_w"][l].reshape(3, 2, 128).transpose(2, 1, 0).reshape(128, 6)
        SP[l, :, 54:56] = inp["gmlp_ln_g"][l].reshape(2, 128).T
        SP[l, :, 56:58] = inp["gmlp_ln_b"][l].reshape(2, 128).T
        SP[l, :, 58] = inp["ck_b1"][l]
        SP[l, :, 59] = inp["cv_b1"][l]
        SP[l, :64, 60:92] = inp["ck_pe"][l].T
        SP[l, 64:, 60:92] = inp["cv_pe"][l].T
        SP[l, :, 92:156] = inp["ck_w2"][l]
        SP[l, :, 284:348] = inp["ck_w2"][l]
        SP[l, :, 348:412] = inp["cv_w2"][l]
        GW[l] = inp["gmlp_w"][l].transpose(2, 0, 1).reshape(128, 512)
        GB[l] = inp["gmlp_b"][l].reshape(1, 512)
    P["WIN"] = WIN.reshape(L * 6 * 128, 4096)
    P["WOUT"] = WOUT.reshape(L * 2 * 128, 4096)
    P["W1"] = W1.reshape(L * 128, 4096)
    P["SP"] = SP.reshape(L * 128, SPW)
    P["GW"] = GW.reshape(L * 128, 512)
    P["GB"] = GB.reshape(L, 512)
    P.update(_consts())
    return P


SHAPES = dict(GU=(L * 2 * NFB * 128, 4096), DN=(L * 2 * NOB * 128, 5632), WIN=(L * 6 * 128, 4096),
              WOUT=(L * 2 * 128, 4096), W1=(L * 128, 4096), SP=(L * 128, SPW), GW=(L * 128, 512), GB=(L, 512),
              IDENT=(128, 128), TRILT=(128, 128), WM=(128, 4096), CM=(128, 2048), EM=(128, 4096), RT=(128, 128),
              OV=(128, 130), SELG=(24, 24))


def build(n_tiles=NT, n_layers=L, dbg=None, stages=99):
    nc = bass.Bass("TRN2", target_bir_lowering=False)
    dram = {k: nc.dram_tensor(k, list(s), F32, kind="ExternalInput").ap() for k, s in SHAPES.items()}
    x_d = nc.dram_tensor("x", [NT * 128, 8 * T], F32, kind="ExternalInput").ap()
    o_d = nc.dram_tensor("out", [NT * 128, 8 * T], F32, kind="ExternalOutput").ap()
    scr = {k: nc.dram_tensor(k + "b", list(SHAPES[k]), BF16, kind="Internal").ap() for k in ("GU", "DN", "WIN", "WOUT", "W1")}
    xs_d = nc.dram_tensor("xs", [NT * 128, 8 * T], F32, kind="Internal").ap()
    dbg_out = {}

    with ExitStack() as st:
        fl = Flow(nc, st)
        _sb = {}

        def sb(name, shape, dt=F32):
            t = st.enter_context(nc.sbuf_tensor(name, list(shape), dt))
            _sb[name] = t
            return t

        psb = [st.enter_context(nc.psum_tensor(f"ps{b}", [128, 512], F32)) for b in range(8)]
        ps_state = dict(n=0, held=set())

        def ps_get(hold=False):
            for _ in range(16):
                b = ps_state["n"] % 8
                ps_state["n"] += 1
                if b not in ps_state["held"]:
                    if hold:
                        ps_state["held"].add(b)
                    return b
            raise RuntimeError("no psum bank")

        def ps_rel(b):
            ps_state["held"].discard(b)

        def PK(b):
            return ("ps", b)

        def dump(name, ap, key, shape, dt=F32):
            if dbg is None or name not in dbg:
                return
            if key == "x32all":
                fl.op("dve", lambda e: e.memset(gj[0][0:1, 0:1], 1.0), reads=X32ALL + [("gj", 0)], writes=["x32all", ("gj", 0)])
            o = nc.dram_tensor("dbg_" + name, list(shape), dt, kind="ExternalOutput").ap()
            dbg_out[name] = o
            fl.dma("pool", lambda e: e.dma_start(out=o, in_=ap), reads=[key], dsem="dbg")

        ident = sb("ident", [128, 128])
        identb = sb("identb", [128, 128], BF16)
        trilt = sb("trilt", [128, 128])
        wm = sb("wm", [128, 8, 512], BF16)
        cm = sb("cm", [128, 4, 512], BF16)
        rt = sb("rt", [128, 128])
        ovt = sb("ovt", [128, 2, 65], BF16)
        selg = sb("selg", [24, 24])
        ones24 = sb("ones24", [24, 128], BF16)
        gj = [sb(f"gj{i}", [24, T], BF16) for i in range(2)]
        ones = sb("ones", [128, 128])
        onesb = sb("onesb", [1, 64], BF16)
        spt = sb("spt", [128, L, SPW])
        w2b = sb("w2b", [128, L, 320], BF16)
        gwm = sb("gwm", [128, L, 4, 128], BF16)
        gbr = sb("gbr", [1, L, 512], BF16)
        cbias = sb("cbias", [128, L, 2])

        def cload(eng, dst, src, key):
            fl.dma(eng, lambda e: e.dma_start(out=dst, in_=src), writes=[key], dsem="const")

        cload("sp", ident[:], dram["IDENT"], "ident")
        cload("sp", trilt[:], dram["TRILT"], "trilt")
        cload("sp", rt[:], dram["RT"], "rt")
        cload("pool", identb[:], dram["IDENT"], "identb")
        cload("pool", wm[:], dram["WM"].rearrange("p (a q) -> p a q", a=8), "wm")
        cload("pool", cm[:], dram["CM"].rearrange("p (a q) -> p a q", a=4), "cm")
        cload("pool", ovt[:], dram["OV"].rearrange("p (a q) -> p a q", a=2), "ovt")
        cload("sp", selg[:], dram["SELG"], "selg")
        for l in range(L):
            cload("sp", spt[:, l, :], dram["SP"][l * 128:(l + 1) * 128, :], "spt")
            cload("pool", w2b[:, l, :], dram["SP"][l * 128:(l + 1) * 128, 92:412], "w2b")
            cload("pool", gbr[:, l, :], dram["GB"][l:l + 1, :], "gbr")
        fl.seal("const")
        fl.op("dve", lambda e: e.memset(ones[:], 1.0), writes=["ones"])
        fl.op("dve", lambda e: e.memset(onesb[:], 1.0), writes=["onesb"])
        fl.op("dve", lambda e: e.memset(ones24[:], 1.0), writes=["ones24"])

        pc_groups = []

        def precast(name, r0, r1, grp):
            if grp == (0, "f1"):
                grp = (0, "f1", name, r0)
            if grp not in pc_groups:
                pc_groups.append(grp)
            fl.dma("pool", lambda e: e.dma_start(out=scr[name][r0:r1, :], in_=dram[name][r0:r1, :]),
                   writes=[("pc", grp)], dsem=("pc", grp))

        pc_pending = []
        for l in range(n_layers):
            for fbk in range(NFB):
                r = ((l * 2 + 0) * NFB + fbk) * 128
                pc_pending.append(("GU", r, r + 128, (l, "f1")))
            for ob in range(NOB):
                r = ((l * 2 + 0) * NOB + ob) * 128
                pc_pending.append(("DN", r, r + 128, (l, "f1")))
            for u in range(6):
                r = (l * 6 + u) * 128
                pc_pending.append(("WIN", r, r + 128, (l, "mix")))
            pc_pending.append(("W1", l * 128, (l + 1) * 128, (l, "mix")))
            for u in range(2):
                r = (l * 2 + u) * 128
                pc_pending.append(("WOUT", r, r + 128, (l, "mix")))
            for fbk in range(NFB):
                r = ((l * 2 + 1) * NFB + fbk) * 128
                pc_pending.append(("GU", r, r + 128, (l, "f2")))
            for ob in range(NOB):
                r = ((l * 2 + 1) * NOB + ob) * 128
                pc_pending.append(("DN", r, r + 128, (l, "f2")))
        pc_l1 = [p for p in pc_pending if p[3][0] == 1]
        pc_l0 = [p for p in pc_pending if p[3][0] == 0]
        def emit_pc(items):
            grps = []
            for (name, r0, r1, grp) in items:
                precast(name, r0, r1, grp)
            for g_ in list(pc_groups):
                fl.seal(("pc", g_))

        gwf = sb("gwf", [128, 4, 128])
        for l in range(n_layers):
            fl.dma("sp", lambda e, l=l: e.dma_start(out=gwf[:], in_=dram["GW"][l * 128:(l + 1) * 128, :].rearrange("p (h q) -> p h q", h=4)),
                   writes=["gwf"])
            for hh in range(4):
                fl.op("dve", lambda e, l=l, hh=hh: e.tensor_tensor(out=gwm[:, l, hh, :], in0=gwf[:, hh, :], in1=trilt[:], op=ALU.mult),
                      reads=["gwf", "trilt"], writes=["gwm"])

        NA, NB = 3, 2
        ringA = [sb(f"ringA{i}", [128, 4096], BF16) for i in range(NA)]
        ringB = [sb(f"ringB{i}", [128, 12, 256], BF16) for i in range(NB)]
        rstate = dict(a=0, b=0)

        def streamA(name, row0, grp):
            s_ = rstate["a"] % NA
            rstate["a"] += 1
            t = ringA[s_]
            if grp == (0, "f1"):
                grp = (0, "f1", name, row0)
            fl.dma("sp", lambda e: e.dma_start(out=t[:], in_=scr[name][row0:row0 + 128, :]),
                   reads=[("pc", grp)], writes=[("rA", s_)])
            return t, ("rA", s_)

        def streamB(row0, grp, fc0, nfc):
            s_ = rstate["b"] % NB
            rstate["b"] += 1
            t = ringB[s_]
            if grp == (0, "f1"):
                grp = (0, "f1", "DN", row0)
            fl.dma("sp", lambda e: e.dma_start(out=t[:, 0:nfc, :], in_=scr["DN"][row0:row0 + 128, fc0 * 256:(fc0 + nfc) * 256].rearrange("p (f c) -> p f c", f=nfc)),
                   reads=[("pc", grp)], writes=[("rB", s_)])
            return t, ("rB", s_)

        x32 = sb("x32", [128, 8, T])
        X32ALL = [("x32", m_) for m_ in range(8)]
        XBALL = [("xb", m_) for m_ in range(8)]
        xb = sb("xb", [128, 8, T], BF16)
        hbuf = sb("hbuf", [128, 12, T], BF16)
        NSCR = 7
        scrp = [sb(f"scrp{i}", [128, T]) for i in range(NSCR)]
        scr_n = dict(n=0)

        def scr_get():
            k_ = scr_n["n"] % NSCR
            scr_n["n"] += 1
            return scrp[k_], ("scrp", k_)
        mean = sb("mean", [128, T])
        rstd = sb("rstd", [128, T])
        ybuf = sb("ybuf", [128, 8, T], BF16)
        YKEYS = [[("ybuf", 0)], [("ybuf", 1)], [("ybuf", 2, 0), ("ybuf", 2, 1)], [("ybuf", 3, 0), ("ybuf", 3, 1)]] + [[("yc", m_), ("yc", m_ + 4)] for m_ in range(4)]
        YBALL = [k_ for ks_ in YKEYS[:4] for k_ in ks_]
        qs = [sb(f"qs{h}", [128, T], BF16) for h in range(8)]
        g24 = sb("g24", [24, T], BF16)
        ha = sb("ha", [128, 2, T], BF16)
        ub = sb("ub", [128, 2, T], BF16)
        vg = sb("vg", [128, 2, T])
        vnT = sb("vnT", [128, NST, 256], BF16)
        pt = [sb(f"pt{i}", [128, T], BF16) for i in range(4)]
        kb = sb("kb", [128, 32, 64], BF16)
        impa = sb("impa", [128, NST, 64])
        rd4 = sb("rd4", [128, NST, 4])
        m8a = sb("m8a", [128, NST, 8])
        m8b = sb("m8b", [128, NST, 8])
        sc2 = sb("sc2", [128, NST, 64])
        selm = sb("selm", [128, 2 * NST, 64])
        impsb = sb("impsb", [128, NST // 2, 512])
        state = []
        for l in range(1):
            stl = dict(
                ke=[sb(f"ke{l}_{g}", [128, S], BF16) for g in range(2)],
                vsl=sb(f"vsl{l}", [128, S // 128, 192], BF16),
                kw=sb(f"kw{l}", [128, 2, T], BF16),
                vw=sb(f"vw{l}", [128, 2 * NST, 192], BF16),
                kcvc=[sb(f"kcvc{l}_{g}", [128, T + 16], BF16) for g in range(2)],
                ht=sb(f"ht{l}", [128, 2, 2, 256], BF16),
                kct=sb(f"kct{l}", [128, 256], BF16),
                vce=sb(f"vce{l}", [128, 2, 192], BF16),
                zc=sb(f"zc{l}", [128, 2, T + 2]),
            )
            state.append(stl)
            fl.dma("pool", lambda e: e.dma_start(out=stl["ke"][0][64:128, :], in_=dram["EM"][64:128, :]), writes=[("ke", 0)], dsem="const2")
            fl.dma("pool", lambda e: e.dma_start(out=stl["ke"][1][0:64, :], in_=dram["EM"][0:64, :]), writes=[("ke", 1)], dsem="const2")
            fl.seal("const2")
            for nm in ("vsl", "vw", "vce"):
                t_ = stl[nm]
                fl.op("pool", lambda e, t_=t_: e.memset(t_[:], 1.0), writes=[(nm, l)])
            fl.op("pool", lambda e, t_=stl["vce"]: e.memset(t_[0:1, 0, :], 0.0), writes=[("vce", l)])
            fl.op("pool", lambda e, t_=stl["ht"]: e.memset(t_[:], 0.0), writes=[("ht", l)])
            fl.op("pool", lambda e, t_=stl["kct"]: e.memset(t_[:], 0.0), writes=[("kct", l)])
            for g in range(2):
                fl.op("pool", lambda e, t_=stl["kcvc"][g]: e.memset(t_[:], 0.0), writes=[("kcvc", l, g)])
            fl.op("pool", lambda e, t_=stl["zc"]: e.memset(t_[:], 0.0), writes=[("zc", 0), ("zc", 1)])

        import os as _os
        DENSE = bool(_os.environ.get("DENSE"))

        def mm(out_ap, lhsT, rhs, start, stop, reads, wkey, chain=False):
            fl.op("pe", lambda e: e.matmul(out_ap, lhsT=lhsT, rhs=rhs, start=start, stop=stop), reads=reads, writes=[wkey], inc=(bool(stop) or not chain or DENSE))

        sqb = [sb(f"sqb{i}", [128, T], BF16) for i in range(3)]
        sqb_n = dict(n=0)
        ones16 = sb("ones16", [128, 128], BF16)
        fl.op("dve", lambda e: e.memset(ones16[:], 1.0), writes=["ones16"])

        def ln_begin():
            return dict(b1=ps_get(hold=True), b2=ps_get(hold=True), n=0)

        def ln_feed(st_, m):
            b1, b2, k = st_["b1"], st_["b2"], st_["n"]
            mm(psb[b1][:, :], ones[:], x32[:, m, :], k == 0, k == 7, ["ones", ("x32", m)], PK(b1))
            q_ = sqb_n["n"] % 3
            sqb_n["n"] += 1
            fl.op("act", lambda e: e.activation(out=sqb[q_][:], in_=x32[:, m, :], func=AF.Square), reads=[("x32", m)], writes=[("sqb", q_)])
            mm(psb[b2][:, :], ones16[:], sqb[q_][:], k == 0, k == 7, ["ones16", ("sqb", q_)], PK(b2))
            st_["n"] += 1

        def ln_finish(l, j, st_, last_stage=False):
            b1, b2 = st_["b1"], st_["b2"]
            assert st_["n"] == 8
            fl.op("dve", lambda e: e.tensor_scalar(out=mean[:], in0=psb[b1][:, :], scalar1=1.0 / D, scalar2=None, op0=ALU.mult),
                  reads=[PK(b1)], writes=["mean"])
            fl.op("dve", lambda e: e.tensor_tensor(out=rstd[:], in0=mean[:], in1=mean[:], op=ALU.mult), reads=["mean"], writes=["rstd"])
            fl.op("dve", lambda e: e.scalar_tensor_tensor(out=rstd[:], in0=psb[b2][:, :], scalar=1.0 / D, in1=rstd[:], op0=ALU.mult, op1=ALU.subtract),
                  reads=[PK(b2), "rstd"], writes=["rstd"])
            ps_rel(b1)
            ps_rel(b2)
            fl.op("dve", lambda e: e.tensor_scalar(out=rstd[:], in0=rstd[:], scalar1=EPSP, scalar2=None, op0=ALU.add), reads=["rstd"], writes=["rstd"])
            fl.op("act", lambda e: e.activation(out=rstd[:], in_=rstd[:], func=AF.Ln), reads=["rstd"], writes=["rstd"])
            fl.op("act", lambda e: e.activation(out=rstd[:], in_=rstd[:], func=AF.Exp, scale=-0.5), reads=["rstd"], writes=["rstd"])
            for m0 in range(0, 8, 2):
                pr = [(m0, ) + scr_get(), (m0 + 1, ) + scr_get()]
                for (m, t_, tk) in pr:
                    fl.op("dve", lambda e, t_=t_, m=m: e.tensor_tensor(out=t_[:], in0=x32[:, m, :], in1=mean[:], op=ALU.subtract),
                          reads=[("x32", m), "mean"], writes=[tk])
                for (m, t_, tk) in pr:
                    fl.op("dve", lambda e, t_=t_: e.tensor_tensor(out=t_[:], in0=t_[:], in1=rstd[:], op=ALU.mult),
                          reads=[tk, "rstd"], writes=[tk])
                for (m, t_, tk) in pr:
                    gcol = 2 * j * 8 + m
                    bcol = (2 * j + 1) * 8 + m
                    if last_stage:
                        fl.op("act", lambda e, t_=t_, m=m, gcol=gcol, bcol=bcol: e.activation(
                            out=x32[:, m, :], in_=t_[:], func=AF.Identity, bias=spt[:, l, bcol:bcol + 1], scale=spt[:, l, gcol:gcol + 1]),
                            reads=[tk, "spt"], writes=[("x32", m)])
                    else:
                        fl.op("act", lambda e, t_=t_, m=m, gcol=gcol, bcol=bcol: e.activation(
                            out=xb[:, m, :], in_=t_[:], func=AF.Identity, bias=spt[:, l, bcol:bcol + 1], scale=spt[:, l, gcol:gcol + 1]),
                            reads=[tk, "spt"], writes=[("xb", m)])
                for (m, t_, tk) in pr:
                    gcol = 2 * j * 8 + m
                    bcol = (2 * j + 1) * 8 + m
                    if last_stage:
                        continue
                    fl.op("pool", lambda e, t_=t_, m=m, gcol=gcol, bcol=bcol: e.tensor_scalar(
                        out=x32[:, m, :], in0=t_[:], scalar1=spt[:, l, gcol:gcol + 1], scalar2=spt[:, l, bcol:bcol + 1], op0=ALU.mult, op1=ALU.add),
                        reads=[tk, "spt"], writes=[("x32", m)])

        def ffn(l, w):
            grp = (l, "f1" if w == 0 else "f2")
            for hf, (fb0, fb1) in enumerate(((0, 6), (6, NFB))):
                for fbk in range(fb0, fb1):
                    slot, skey = streamA("GU", ((l * 2 + w) * NFB + fbk) * 128, grp)
                    sv = slot[:].rearrange("p (a d c) -> p a d c", a=2, d=8)
                    for half in range(2):
                        fcl = (fbk - fb0) * 2 + half
                        bg = ps_get()
                        bu = ps_get()
                        for dc in range(8):
                            mm(psb[bg][:, :], sv[:, 0, dc, half * 128:(half + 1) * 128], xb[:, dc, :], dc == 0, dc == 7, [skey, ("xb", dc)], PK(bg), chain=True)
                        for dc in range(8):
                            mm(psb[bu][:, :], sv[:, 1, dc, half * 128:(half + 1) * 128], xb[:, dc, :], dc == 0, dc == 7, [skey, ("xb", dc)], PK(bu), chain=True)
                        s_, sk = scr_get()
                        fl.op("act", lambda e, s_=s_, bg=bg: e.activation(out=s_[:], in_=psb[bg][:, :], func=AF.Silu), reads=[PK(bg)], writes=[sk])
                        fl.op("dve", lambda e, s_=s_, bu=bu, fcl=fcl: e.tensor_tensor(out=hbuf[:, fcl, :], in0=psb[bu][:, :], in1=s_[:], op=ALU.mult),
                              reads=[PK(bu), sk], writes=[("hbuf", fcl)])
                nfc = (fb1 - fb0) * 2
                if hf == 1 and w == 1 and NEXT_TILE[0] is not None:
                    prefetch_xb(*NEXT_TILE[0])
                lns = ln_begin() if hf == 1 else None
                pend_m = None
                for ob in range(NOB):
                    slot, skey = streamB(((l * 2 + w) * NOB + ob) * 128, grp, fb0 * 2, nfc)
                    for half in range(2):
                        m = ob * 2 + half
                        by = ps_get()
                        for fc in range(nfc):
                            mm(psb[by][:, :], slot[:, fc, half * 128:(half + 1) * 128], hbuf[:, fc, :], fc == 0, fc == nfc - 1, [skey, ("hbuf", fc)], PK(by), chain=True)
                        if lns is not None and pend_m is not None:
                            ln_feed(lns, pend_m)
                        pend_m = m
                        fl.op("dve", lambda e, by=by, m=m: e.scalar_tensor_tensor(out=x32[:, m, :], in0=psb[by][:, :], scalar=0.5 / ALPHA, in1=x32[:, m, :],
                                                                                    op0=ALU.mult, op1=ALU.add), reads=[PK(by), ("x32", m)], writes=[("x32", m)])
            ln_feed(lns, pend_m)
            ln_finish(l, 0 if w == 0 else 2, lns, last_stage=(w == 1))

        def x_src(l):
            return x_d if l == 0 else xs_d

        def prefetch_xb(l, i):
            src = x_src(l)
            rd = [("xs", i)] if l > 0 else []
            fl.dma("pool", lambda e: e.dma_start(out=xb[:].rearrange("p m t -> p (m t)"), in_=src[i * 128:(i + 1) * 128, :]), reads=rd, writes=XBALL, dsem="xbf")

        def load_x32(l, i):
            src = x_src(l)
            rd = [("xs", i)] if l > 0 else []
            fl.dma("pool", lambda e: e.dma_start(out=x32[:].rearrange("p m t -> p (m t)"), in_=src[i * 128:(i + 1) * 128, :]), reads=rd, writes=X32ALL, dsem="xf")

        def store_x(i):
            fl.dma("pool", lambda e: e.dma_start(out=o_d[i * 128:(i + 1) * 128, :], in_=x32[:].rearrange("p m t -> p (m t)")), reads=X32ALL, dsem="out")

        def spill_x(i):
            fl.dma("pool", lambda e: e.dma_start(out=xs_d[i * 128:(i + 1) * 128, :], in_=x32[:].rearrange("p m t -> p (m t)")), reads=X32ALL, writes=[("xs", i)], dsem=("xs", i))

        def mixer(l, i):
            stl = state[0]
            grp = (l, "mix")
            ke, vsl, kw, vw, kcvc, ht, kct, vce = (stl[k] for k in ("ke", "vsl", "kw", "vw", "kcvc", "ht", "kct", "vce"))
            wslot = i % 2
            zc = stl["zc"]

            def inproj_chunk(slot, skey, c, M=128):
                b = ps_get()
                sv = slot[:].rearrange("p (d c) -> p d c", d=8)
                for dc in range(8):
                    mm(psb[b][0:M, :], sv[:, dc, c * 128:c * 128 + M], xb[:, dc, :], dc == 0, dc == 7, [skey, ("xb", dc)], PK(b), chain=True)
                return b

            if i == 0:
                fl.op("pool", lambda e: e.memset(zc[:, :, 0:2], 0.0), writes=[("zc", 0), ("zc", 1)])
                for g in range(2):
                    fl.op("pool", lambda e, g=g: e.memset(kcvc[g][:, 0:16], 0.0), writes=[("kcvc", l, g)])
            else:
                for g in range(2):
                    fl.op("pool", lambda e, g=g: e.tensor_copy(out=kcvc[g][:, 0:16], in_=kcvc[g][:, T:T + 16]), reads=[("kcvc", l, g)], writes=[("kcvc", l, g)])
            slot, skey = streamA("WIN", (l * 6 + 0) * 128, grp)
            for c in range(4):
                b = inproj_chunk(slot, skey, c)
                fl.op("act", lambda e, b=b, c=c: e.copy(out=qs[c][0:64, :], in_=psb[b][0:64, :]), reads=[PK(b)], writes=[("qs", c)])
                fl.op("act", lambda e, b=b, c=c: e.copy(out=qs[c + 4][64:128, :], in_=psb[b][64:128, :]), reads=[PK(b)], writes=[("qs", c + 4)])
            slot, skey = streamA("WIN", (l * 6 + 1) * 128, grp)
            b = inproj_chunk(slot, skey, 0)
            fl.op("act", lambda e, b=b: e.copy(out=ke[0][0:64, i * T:(i + 1) * T], in_=psb[b][0:64, :]), reads=[PK(b)], writes=[("ke", 0)])
            fl.op("act", lambda e, b=b: e.copy(out=ke[1][64:128, i * T:(i + 1) * T], in_=psb[b][64:128, :]), reads=[PK(b)], writes=[("ke", 1)])
            b = inproj_chunk(slot, skey, 1)
            fl.op("dve", lambda e, b=b: e.tensor_copy(out=kw[:, wslot, :], in_=psb[b][:, :]), reads=[PK(b)], writes=[("kw", l)])
            for g in range(2):
                b = inproj_chunk(slot, skey, 2 + g)
                fl.op("act", lambda e, b=b, g=g: e.copy(out=kcvc[g][:, 16:16 + T], in_=psb[b][:, :]), reads=[PK(b)], writes=[("kcvc", l, g)])
            slot, skey = streamA("WIN", (l * 6 + 2) * 128, grp)
            for cc in range(2):
                b = inproj_chunk(slot, skey, cc)
                fl.op("act", lambda e, b=b, cc=cc: e.copy(out=ha[:, cc, :], in_=psb[b][:, :]), reads=[PK(b)], writes=[("ha", cc)])
            for cc in range(2):
                b = inproj_chunk(slot, skey, 2 + cc)
                fl.op("dve", lambda e, b=b, cc=cc: e.tensor_tensor(out=zc[:, cc, 2:2 + T], in0=psb[b][:, :], in1=ha[:, cc, :], op=ALU.mult),
                      reads=[PK(b), ("ha", cc)], writes=[("zc", cc)])
            slot, skey = streamA("WIN", (l * 6 + 3) * 128, grp)
            for cc in range(2):
                b = inproj_chunk(slot, skey, cc)
                w0 = spt[:, l, 48 + cc * 3 + 0:48 + cc * 3 + 1]
                w1_ = spt[:, l, 48 + cc * 3 + 1:48 + cc * 3 + 2]
                w2_ = spt[:, l, 48 + cc * 3 + 2:48 + cc * 3 + 3]
                cacc, ck_ = scr_get()
                fl.op("dve", lambda e, cc=cc, w2_=w2_, cacc=cacc: e.tensor_scalar(out=cacc[:], in0=zc[:, cc, 2:2 + T], scalar1=w2_, scalar2=None, op0=ALU.mult),
                      reads=[("zc", cc), "spt"], writes=[ck_])
                fl.op("dve", lambda e, cc=cc, w1_=w1_, cacc=cacc: e.scalar_tensor_tensor(out=cacc[:], in0=zc[:, cc, 1:1 + T], scalar=w1_, in1=cacc[:], op0=ALU.mult, op1=ALU.add),
                      reads=[("zc", cc), "spt", ck_], writes=[ck_])
                fl.op("dve", lambda e, cc=cc, w0=w0, cacc=cacc: e.scalar_tensor_tensor(out=cacc[:], in0=zc[:, cc, 0:T], scalar=w0, in1=cacc[:], op0=ALU.mult, op1=ALU.add),
                      reads=[("zc", cc), "spt", ck_], writes=[ck_])
                fl.op("dve", lambda e, b=b, cc=cc, cacc=cacc: e.tensor_tensor(out=ybuf[:, cc, :], in0=psb[b][:, :], in1=cacc[:], op=ALU.mult),
                      reads=[PK(b), ck_], writes=[("ybuf", cc)])
                fl.op("pool", lambda e, cc=cc: e.tensor_copy(out=zc[:, cc, 0:2], in_=zc[:, cc, T:T + 2]), reads=[("zc", cc)], writes=[("zc", cc)])
            for cc in range(2):
                b = inproj_chunk(slot, skey, 2 + cc)
                fl.op("act", lambda e, b=b, cc=cc: e.activation(out=ub[:, cc, :], in_=psb[b][:, :], func=AF.Gelu_apprx_tanh), reads=[PK(b)], writes=[("ub", cc)])
            slot, skey = streamA("WIN", (l * 6 + 4) * 128, grp)
            for cc in range(2):
                b = inproj_chunk(slot, skey, cc)
                fl.op("act", lambda e, b=b, cc=cc: e.activation(out=vg[:, cc, :], in_=psb[b][:, :], func=AF.Gelu_apprx_tanh), reads=[PK(b)], writes=[("vg", cc)])
            b = inproj_chunk(slot, skey, 2, M=24)
            fl.op("act", lambda e, b=b: e.activation(out=g24[:], in_=psb[b][0:24, :], func=AF.Sigmoid), reads=[PK(b)], writes=["g24"])
            slot, skey = streamA("WIN", (l * 6 + 5) * 128, grp)
            sv5 = slot[:].rearrange("p (d c) -> p d c", d=8)
            for stn in range(NST):
                b = ps_get()
                for dc in range(8):
                    mm(psb[b][:, 0:256], xb[:, dc, stn * 128:(stn + 1) * 128], sv5[:, dc, 0:256], dc == 0, dc == 7, [skey, ("xb", dc)], PK(b))
                kt = i * NST + stn
                wk = wslot * NST + stn
                fl.op("act", lambda e, b=b, kt=kt: e.copy(out=vsl[:, kt, 0:64], in_=psb[b][:, 0:64]), reads=[PK(b)], writes=[("vsl", l)])
                fl.op("act", lambda e, b=b, kt=kt: e.copy(out=vsl[:, kt, 128:192], in_=psb[b][:, 64:128]), reads=[PK(b)], writes=[("vsl", l)])
                fl.op("dve", lambda e, b=b, wk=wk: e.tensor_copy(out=vw[:, wk, 0:64], in_=psb[b][:, 128:192]), reads=[PK(b), ("vsl", l)], writes=[("vw", l)])
                fl.op("dve", lambda e, b=b, wk=wk: e.tensor_copy(out=vw[:, wk, 128:192], in_=psb[b][:, 192:256]), reads=[PK(b), ("vsl", l)], writes=[("vw", l)])
            dump(f"qt_{l}_{i}", qs[0][:], ("qs", 0), [128, T], BF16)
            if i < 2:
                for h_ in range(8):
                    g_ = h_ // 4
                    fl.op("pool", lambda e, h_=h_, g_=g_: e.memset(qs[h_][(1 - g_) * 64:(1 - g_) * 64 + 64, :], 0.0), writes=[("qs", h_)])

            if stages < 2.1:
                return
            import os
            SKIPG = os.environ.get("SKIPG", "")
            b1 = ps_get()
            b2 = ps_get()
            for cc in range(2):
                mm(psb[b1][:, :], ones[:], vg[:, cc, :], cc == 0, cc == 1, ["ones", ("vg", cc)], PK(b1))
            for cc in range(2):
                s_, sk = scr_get()
                fl.op("act", lambda e, s_=s_, cc=cc: e.activation(out=s_[:], in_=vg[:, cc, :], func=AF.Square), reads=[("vg", cc)], writes=[sk])
                mm(psb[b2][:, :], ones[:], s_[:], cc == 0, cc == 1, ["ones", sk], PK(b2))
            fl.op("dve", lambda e: e.tensor_scalar(out=mean[:], in0=psb[b1][:, :], scalar1=1.0 / 256, scalar2=None, op0=ALU.mult), reads=[PK(b1)], writes=["mean"])
            fl.op("pool", lambda e: e.tensor_tensor(out=rstd[:], in0=mean[:], in1=mean[:], op=ALU.mult), reads=["mean"], writes=["rstd"])
            fl.op("dve", lambda e: e.scalar_tensor_tensor(out=rstd[:], in0=psb[b2][:, :], scalar=1.0 / 256, in1=rstd[:], op0=ALU.mult, op1=ALU.subtract),
                  reads=[PK(b2), "rstd"], writes=["rstd"])
            fl.op("dve", lambda e: e.tensor_scalar(out=rstd[:], in0=rstd[:], scalar1=EPS, scalar2=None, op0=ALU.add), reads=["rstd"], writes=["rstd"])
            fl.op("act", lambda e: e.activation(out=rstd[:], in_=rstd[:], func=AF.Ln), reads=["rstd"], writes=["rstd"])
            fl.op("act", lambda e: e.activation(out=rstd[:], in_=rstd[:], func=AF.Exp, scale=-0.5), reads=["rstd"], writes=["rstd"])
            for cc in range(2):
                fl.op("dve", lambda e, cc=cc: e.tensor_tensor(out=vg[:, cc, :], in0=vg[:, cc, :], in1=mean[:], op=ALU.subtract), reads=[("vg", cc), "mean"], writes=[("vg", cc)])
                fl.op("dve", lambda e, cc=cc: e.tensor_tensor(out=vg[:, cc, :], in0=vg[:, cc, :], in1=rstd[:], op=ALU.mult), reads=[("vg", cc), "rstd"], writes=[("vg", cc)])
                fl.op("act", lambda e, cc=cc: e.activation(out=vg[:, cc, :], in_=vg[:, cc, :], func=AF.Identity, bias=spt[:, l, 56 + cc:57 + cc], scale=spt[:, l, 54 + cc:55 + cc]),
                      reads=[("vg", cc), "spt"], writes=[("vg", cc)])
            for stn in range(NST if "t" not in SKIPG else 0):
                b = ps_get()
                for cc in range(2):
                    fl.op("pe", lambda e, b=b, cc=cc, stn=stn: e.transpose(out=psb[b][:, cc * 128:(cc + 1) * 128], in_=vg[:, cc, stn * 128:(stn + 1) * 128], identity=ident[:]),
                          reads=[("vg", cc), "ident"], writes=[PK(b)])
                fl.op("act", lambda e, b=b, stn=stn: e.copy(out=vnT[:, stn, :], in_=psb[b][:, 0:256]), reads=[PK(b)], writes=["vnT"])
            for hd in range(4 if "h" not in SKIPG else 0):
                cc, hh = hd // 2, hd % 2
                b = ps_get()
                for stn in range(NST):
                    mm(psb[b][0:64, stn * 128:(stn + 1) * 128], vnT[:, stn, hd * 64:(hd + 1) * 64], gwm[:, l, hd, :], True, False, ["vnT", "gwm"], PK(b))
                    mm(psb[b][0:64, stn * 128:(stn + 1) * 128], onesb[0:1, :], gbr[0:1, l, hd * 128:(hd + 1) * 128], False, True, ["onesb", "gbr"], PK(b))
                fl.op("dve", lambda e, b=b, cc=cc, hh=hh: e.tensor_tensor(out=ybuf[hh * 64:(hh + 1) * 64, 2 + cc, :], in0=psb[b][0:64, :], in1=ub[hh * 64:(hh + 1) * 64, cc, :], op=ALU.mult),
                      reads=[PK(b), ("ub", cc)], writes=[("ybuf", 2 + cc, hh)])

            if stages < 2.2:
                return
            w1slot, w1key = streamA("W1", l * 128, grp)
            w1v = w1slot[:].rearrange("p (a f) -> p a f", a=32)
            import os
            if i == 0:
                bpe = [ps_get(), ps_get()]
                for kv in range(2):
                    for lp in range(32):
                        mm(psb[bpe[kv]][:, 0:1], w1v[kv * 64:(kv + 1) * 64, lp, :], spt_b[kv * 64:(kv + 1) * 64, l, lp:lp + 1], lp == 0, lp == 31, [w1key, "sptb"], PK(bpe[kv]), chain=True)
                for kv in range(2):
                    fl.op("dve", lambda e, kv=kv: e.tensor_tensor(out=cbias[:, l, kv:kv + 1], in0=psb[bpe[kv]][:, 0:1], in1=spt[:, l, 58 + kv:59 + kv], op=ALU.add),
                          reads=[PK(bpe[kv]), "spt"], writes=["cbias"])
            for g in range(2):
                for lp in range(32):
                    fl.op("dve", lambda e, g=g, lp=lp: e.tensor_copy(out=kb[:, lp, g * 32:(g + 1) * 32], in_=kcvc[g][:, lp:lp + 497:16]),
                          reads=[("kcvc", l, g)], writes=[("kb", g, lp)])
            bkv = [ps_get(), ps_get()]
            for kv in range(2):
                for lp in range(32):
                    mm(psb[bkv[kv]][:, 0:64], w1v[kv * 64:(kv + 1) * 64, lp, :], kb[kv * 64:(kv + 1) * 64, lp, :], lp == 0, lp == 31,
                       [w1key, ("kb", 0, lp), ("kb", 1, lp)], PK(bkv[kv]), chain=True)
            for kv in range(2):
                fl.op("act", lambda e, kv=kv: e.activation(out=ht[:, kv, :, i * 32:(i + 1) * 32], in_=psb[bkv[kv]][:, 0:64].rearrange("p (g n) -> p g n", g=2),
                                                             func=AF.Gelu_apprx_tanh, bias=cbias[:, l, kv:kv + 1]), reads=[PK(bkv[kv]), "cbias"], writes=[("ht", l)])
            b = ps_get()
            for g in range(2):
                mm(psb[b][:, 0:32], w2b[:, l, g * 128:(g + 1) * 128], ht[:, 0, g, i * 32:(i + 1) * 32], g == 0, g == 1, ["w2b", ("ht", l)], PK(b))
            fl.op("act", lambda e, b=b: e.copy(out=kct[:, i * 32:(i + 1) * 32], in_=psb[b][:, 0:32]), reads=[PK(b)], writes=[("kct", l)])
            cch = i // 4
            b = ps_get()
            for g in range(2):
                mm(psb[b][:, g * 64:(g + 1) * 64], ht[:, 1, g, cch * 128:(cch + 1) * 128], w2b[:, l, 256:320], True, True, [("ht", l), "w2b"], PK(b))
            fl.op("act", lambda e, b=b: e.copy(out=vce[:, cch, 0:64], in_=psb[b][:, 0:64]), reads=[PK(b)], writes=[("vce", l)])
            fl.op("act", lambda e, b=b: e.copy(out=vce[:, cch, 128:192], in_=psb[b][:, 64:128]), reads=[PK(b)], writes=[("vce", l)])
            if cch == 0:
                fl.op("pool", lambda e: e.memset(vce[0:1, 0, :], 0.0), writes=[("vce", l)])
            dump(f"kct_{l}_{i}", kct[:], ("kct", l), [128, 256], BF16)

            if stages < 2.3:
                return
            def gate_bcast(h, br):
                g = h // 4
                D0 = (1 - g) * 64
                bgt = ps_get()
                jcol = h * 3 + br
                k_ = (h * 3 + br) % 2
                fl.op("dve", lambda e: e.tensor_scalar(out=gj[k_][:], in0=g24[:], scalar1=selg[:, jcol:jcol + 1], scalar2=None, op0=ALU.mult),
                      reads=["g24", "selg"], writes=[("gj", k_)])
                mm(psb[bgt][:, :], ones24[:], gj[k_][:], True, True, ["ones24", ("gj", k_)], PK(bgt))
                gs_, gk = scr_get()
                fl.op("act", lambda e: e.copy(out=gs_[D0:D0 + 64, :], in_=psb[bgt][D0:D0 + 64, :]), reads=[PK(bgt)], writes=[gk])
                return gs_, gk

            def combine_multi(h, items):
                g, m = h // 4, h % 4
                N0, D0 = g * 64, (1 - g) * 64
                st_ = []
                for (br, acc, first) in items:
                    gs_, gk = gate_bcast(h, br)
                    cf_, ck = scr_get()
                    st_.append((br, acc, first, gs_, gk, cf_, ck))
                for (br, acc, first, gs_, gk, cf_, ck) in st_:
                    fl.op("dve", lambda e, acc=acc, cf_=cf_: e.tensor_scalar(out=cf_[D0:D0 + 64, :], in0=psb[acc][D0:D0 + 64, :], scalar1=1e-30, scalar2=None, op0=ALU.add),
                          reads=[PK(acc)], writes=[ck])
                for (br, acc, first, gs_, gk, cf_, ck) in st_:
                    fl.op("dve", lambda e, cf_=cf_: e.reciprocal(out=cf_[D0:D0 + 64, :], in_=cf_[D0:D0 + 64, :]), reads=[ck], writes=[ck])
                for (br, acc, first, gs_, gk, cf_, ck) in st_:
                    fl.op("pool", lambda e, cf_=cf_, gs_=gs_: e.tensor_tensor(out=cf_[D0:D0 + 64, :], in0=cf_[D0:D0 + 64, :], in1=gs_[D0:D0 + 64, :], op=ALU.mult),
                          reads=[ck, gk], writes=[ck])
                for (br, acc, first, gs_, gk, cf_, ck) in st_:
                    if first:
                        fl.op("dve", lambda e, acc=acc, cf_=cf_: e.tensor_tensor(out=ybuf[N0:N0 + 64, 4 + m, :], in0=psb[acc][N0:N0 + 64, :], in1=cf_[D0:D0 + 64, :], op=ALU.mult),
                              reads=[PK(acc), ck], writes=[("yc", h)])
                    else:
                        fl.op("dve", lambda e, acc=acc, cf_=cf_, gs_=gs_: e.tensor_tensor(out=gs_[N0:N0 + 64, :], in0=psb[acc][N0:N0 + 64, :], in1=cf_[D0:D0 + 64, :], op=ALU.mult),
                              reads=[PK(acc), ck, gk], writes=[gk])
                for (br, acc, first, gs_, gk, cf_, ck) in st_:
                    if not first:
                        fl.op("pool", lambda e, gs_=gs_: e.tensor_tensor(out=ybuf[N0:N0 + 64, 4 + m, :], in0=ybuf[N0:N0 + 64, 4 + m, :], in1=gs_[N0:N0 + 64, :], op=ALU.add),
                              reads=[("yc", h), gk], writes=[("yc", h)])

            ptn = dict(n=0)

            def next_pt():
                k_ = ptn["n"] % 4
                ptn["n"] += 1
                return k_

            nch = 1 if i < 4 else 2
            topkB1 = []
            cmp_prev = []
            do_sel = i >= 2
            for g in range(2):
                bimp = [ps_get(hold=True) for _ in range(NST // 2)] if do_sel else None
                for r in range(4):
                    h = g * 4 + r
                    m = r
                    acc = ps_get(hold=True)
                    kcs = []
                    bss = []
                    for c in range(nch):
                        bs = ps_get()
                        bss.append(bs)
                        ip = i - 4 * c
                        masked = ip < 4
                        mm(psb[bs][:, :], kct[g * 64:(g + 1) * 64, c * 128:(c + 1) * 128], qs[h][g * 64:(g + 1) * 64, :], True, not masked,
                           [("kct", l), ("qs", h)], PK(bs))
                        if masked:
                            mm(psb[bs][:, :], identb[:], cm[:, ip, :], False, True, ["identb", "cm"], PK(bs))
                    for c in range(nch):
                        bs = bss[c]
                        k_ = next_pt()
                        kcs.append(k_)
                        fl.op("act", lambda e, bs=bs, k_=k_: e.activation(out=pt[k_][:], in_=psb[bs][:, :], func=AF.Exp, scale=SCALE), reads=[PK(bs)], writes=[("pt", k_)])
                    for c in range(nch):
                        k_ = kcs[c]
                        mm(psb[acc][:, :], vce[:, c, g * 64:g * 64 + 128], pt[k_][:], c == 0, c == nch - 1, [("vce", l), ("pt", k_)], PK(acc))
                    if do_sel:
                        for stn in range(NST):
                            bi = bimp[stn // 2]
                            o0 = (stn % 2) * 256 + r * 64
                            for c in range(nch):
                                mm(psb[bi][:, o0:o0 + 64], pt[kcs[c]][:, stn * 128:(stn + 1) * 128], ovt[:, c, 0:64], c == 0, c == nch - 1, [("pt", kcs[c]), "ovt"], PK(bi))
                    combine_multi(h, [(0, acc, True)])
                    for b_ in cmp_prev:
                        ps_rel(b_)
                    cmp_prev[:] = [acc]
                if do_sel:
                    def topkA(g=g, bimp=bimp):
                        SR = range(NST)
                        for bk_ in range(NST // 2):
                            fl.op("dve", lambda e, bk_=bk_, bimp=bimp: e.tensor_copy(out=impsb[:, bk_, :], in_=psb[bimp[bk_]][:, :]), reads=[PK(bimp[bk_])], writes=[("impsb", bk_)])
                        v3s = [impsb[:, stn // 2, (stn % 2) * 256:(stn % 2) * 256 + 256].rearrange("p (r c) -> p r c", r=4) for stn in SR]
                        for r in range(4):
                            for stn in SR:
                                fl.op("dve", lambda e, stn=stn, r=r, v3=v3s[stn]: e.reduce_sum(out=rd4[:, stn, r:r + 1], in_=v3[:, r, :], axis=mybir.AxisListType.X),
                                      reads=[("impsb", stn // 2)], writes=[("rd4", stn)])
                        for stn in SR:
                            fl.op("dve", lambda e, stn=stn: e.tensor_scalar(out=rd4[:, stn, :], in0=rd4[:, stn, :], scalar1=1e-30, scalar2=None, op0=ALU.add),
                                  reads=[("rd4", stn)], writes=[("rd4", stn)])
                        for stn in SR:
                            fl.op("dve", lambda e, stn=stn: e.reciprocal(out=rd4[:, stn, :], in_=rd4[:, stn, :]), reads=[("rd4", stn)], writes=[("rd4", stn)])
                        for stn in SR:
                            fl.op("dve", lambda e, stn=stn, v3=v3s[stn]: e.tensor_scalar(out=impa[:, stn, :], in0=v3[:, 0, 0:64], scalar1=rd4[:, stn, 0:1], scalar2=None, op0=ALU.mult),
                                  reads=[("impsb", stn // 2), ("rd4", stn)], writes=[("impa", stn)])
                        for r in range(1, 4):
                            for stn in SR:
                                fl.op("dve", lambda e, stn=stn, r=r, v3=v3s[stn]: e.scalar_tensor_tensor(out=impa[:, stn, :], in0=v3[:, r, 0:64], scalar=rd4[:, stn, r:r + 1], in1=impa[:, stn, :],
                                                                                             op0=ALU.mult, op1=ALU.add),
                                      reads=[("impsb", stn // 2), ("rd4", stn), ("impa", stn)], writes=[("impa", stn)])
                        for stn in SR:
                            sg_ = i * NST + stn
                            fl.op("dve", lambda e, stn=stn, sg_=sg_: e.tensor_tensor(out=impa[:, stn, :], in0=impa[:, stn, :], in1=rt[:, 64 - 2 * sg_:128 - 2 * sg_], op=ALU.add),
                                  reads=[("impa", stn), "rt"], writes=[("impa", stn)])
                        for stn in SR:
                            fl.op("dve", lambda e, stn=stn: e.memset(impa[:, stn, 0:1], 1e30), reads=[("impa", stn)], writes=[("impa", stn)])
                        for stn in SR:
                            fl.op("dve", lambda e, stn=stn: e.max(out=m8a[:, stn, :], in_=impa[:, stn, :]), reads=[("impa", stn)], writes=[("m8a", stn)])
                        for stn in SR:
                            fl.op("dve", lambda e, stn=stn: e.match_replace(out=sc2[:, stn, :], in_to_replace=m8a[:, stn, :], in_values=impa[:, stn, :], imm_value=-3e38),
                                  reads=[("impa", stn), ("m8a", stn)], writes=[("sc2", stn)])
                        for stn in SR:
                            fl.op("dve", lambda e, stn=stn: e.max(out=m8b[:, stn, :], in_=sc2[:, stn, :]), reads=[("sc2", stn)], writes=[("m8b", stn)])
                        for stn in SR:
                            fl.op("dve", lambda e, stn=stn: e.tensor_scalar(out=selm[:, g * NST + stn, :], in0=impa[:, stn, :], scalar1=m8b[:, stn, 7:8], scalar2=1.0, op0=ALU.is_ge, op1=ALU.subtract),
                                  reads=[("impa", stn), ("m8b", stn)], writes=[("selm", g, stn)])
                    def topkB(g=g):
                        SR = range(NST)
                        h0_ = g * 4
                        rs_ = slice((1 - g) * 64, (1 - g) * 64 + 64)
                        for stn in SR:
                            bt = ps_get()
                            fl.op("pe", lambda e, bt=bt, stn=stn: e.transpose(out=psb[bt][0:64, 0:128], in_=selm[:, g * NST + stn, :], identity=ident[:]), reads=[("selm", g, stn), "ident"], writes=[PK(bt)])
                            fl.op("act", lambda e, bt=bt, stn=stn, h0_=h0_, rs_=rs_: e.copy(out=qs[h0_][rs_, stn * 128:(stn + 1) * 128], in_=psb[bt][0:64, 0:128]),
                                  reads=[PK(bt)], writes=[("qs", h0_)])
                        for r_ in range(1, 4):
                            h_ = g * 4 + r_
                            fl.op("dve", lambda e, h_=h_, h0_=h0_, rs_=rs_: e.tensor_copy(out=qs[h_][rs_, :], in_=qs[h0_][rs_, :]), reads=[("qs", h0_)], writes=[("qs", h_)])
                    if g == 0:
                        topkA()
                        topkB0 = topkB
                    else:
                        topkB0()
                        topkA()
                        topkB1.append(topkB)
                    for bb in bimp:
                        ps_rel(bb)
            if do_sel:
                pass

            if stages < 2.4:
                return
            for b_ in cmp_prev:
                ps_rel(b_)
            cmp_prev[:] = []
            prev_accs = []
            for h in range(8):
                if h == 4 and topkB1:
                    topkB1[0]()
                g, m = h // 4, h % 4
                gs = slice(g * 64, (g + 1) * 64)
                acc_s = ps_get(hold=True)
                acc_w = ps_get(hold=True)
                tasks = []
                nks = NST * i + NST
                for kt in range(nks):
                    tasks.append(("s", kt, kt == 0, kt == nks - 1))
                wk = [a for a in ([-1, -4, -3, -2, 0, 1, 2, 3] if i >= 1 else [0, 1, 2, 3])]
                for n_, a in enumerate(wk):
                    tasks.append(("w", a, n_ == 0, n_ == len(wk) - 1))
                st_ = [t_ for t_ in tasks if t_[0] == "s"]
                wt_ = [t_ for t_ in tasks if t_[0] == "w"]
                order = []
                while st_ or wt_:
                    if st_:
                        order.append(st_.pop(0))
                    if wt_:
                        order.append(wt_.pop(0))

                def score(task):
                    kind, idx, first, last = task
                    bs = ps_get()
                    if kind == "s":
                        kt = idx
                        a = kt - NST * i
                        q0 = 128 * a if a > 0 else 0
                        q1 = T
                        need_sel = do_sel
                        need_c = a >= 0
                        mm(psb[bs][:, q0:q1], ke[g][:, kt * 128:(kt + 1) * 128], qs[h][:, q0:q1], True, not need_c, [("ke", g), ("qs", h)], PK(bs))
                        if need_c:
                            mm(psb[bs][:, q0:q1], identb[:], wm[:, a + 4, q0:q1], False, True, ["identb", "wm"], PK(bs))
                        vap = vsl[:, kt, g * 64:g * 64 + 128]
                        vkey = ("vsl", l)
                        acc = acc_s
                    else:
                        a = idx
                        kt = NST * i + a
                        q0 = 128 * a if a > 0 else 0
                        q1 = T if a >= -1 else 128 * (a + 5)
                        sl_ = (kt // NST) % 2
                        c0 = (kt % NST) * 128
                        mm(psb[bs][:, q0:q1], kw[gs, sl_, c0:c0 + 128], qs[h][gs, q0:q1], True, False, [("kw", l), ("qs", h)], PK(bs))
                        mm(psb[bs][:, q0:q1], identb[:], wm[:, a + 4, q0:q1], False, True, ["identb", "wm"], PK(bs))
                        vap = vw[:, sl_ * NST + kt % NST, g * 64:g * 64 + 128]
                        vkey = ("vw", l)
                        acc = acc_w
                    k_ = next_pt()
                    fl.op("act", lambda e: e.activation(out=pt[k_][:, q0:q1], in_=psb[bs][:, q0:q1], func=AF.Exp, scale=SCALE), reads=[PK(bs)], writes=[("pt", k_)])
                    return (acc, vap, vkey, k_, q0, q1, first, last)

                def pv(info):
                    acc, vap, vkey, k_, q0, q1, first, last = info
                    mm(psb[acc][:, q0:q1], vap, pt[k_][:, q0:q1], first, last, [vkey, ("pt", k_)], PK(acc))

                pend = []
                for task in order:
                    pend.append(score(task))
                    if len(pend) > 2:
                        pv(pend.pop(0))
                while pend:
                    pv(pend.pop(0))
                combine_multi(h, [(1, acc_s, False), (2, acc_w, False)])
                for b_ in prev_accs:
                    ps_rel(b_)
                prev_accs[:] = [acc_s, acc_w]
            for b_ in prev_accs:
                ps_rel(b_)
            if dbg:
                fl.op("dve", lambda e: e.memset(gj[0][0:1, 0:1], 1.0), reads=[("yc", hh_) for hh_ in range(8)] + YBALL + [("gj", 0)], writes=["ybuf_all", ("gj", 0)])
            dump(f"y_{l}_{i}", ybuf[:], "ybuf_all", [128, 8, T], BF16)

            if stages < 2.5:
                return
            lns = ln_begin()
            pend_m = None
            for u in range(2):
                slot, skey = streamA("WOUT", (l * 2 + u) * 128, grp)
                sv = slot[:].rearrange("p (k c) -> p k c", k=8)
                for c in range(4):
                    m = u * 4 + c
                    b = ps_get()
                    for kc in range(8):
                        mm(psb[b][:, :], sv[:, kc, c * 128:(c + 1) * 128], ybuf[:, kc, :], kc == 0, kc == 7, [skey] + YKEYS[kc], PK(b))
                    if pend_m is not None:
                        ln_feed(lns, pend_m)
                    pend_m = m
                    fl.op("dve", lambda e, b=b, m=m: e.scalar_tensor_tensor(out=x32[:, m, :], in0=psb[b][:, :], scalar=1.0 / ALPHA, in1=x32[:, m, :], op0=ALU.mult, op1=ALU.add),
                          reads=[PK(b), ("x32", m)], writes=[("x32", m)])
            ln_feed(lns, pend_m)
            ln_finish(l, 1, lns)

        spt_b = sb("spt_b", [128, L, 32], BF16)
        for l in range(L):
            fl.op("dve", lambda e, l=l: e.tensor_copy(out=spt_b[:, l, :], in_=spt[:, l, 60:92]), reads=["spt"], writes=["sptb"])

        NEXT_TILE = [None]
        for l in range(n_layers):
            for i in range(n_tiles):
                if l == 0 and i == 0:
                    prefetch_xb(0, 0)
                load_x32(l, i)
                NEXT_TILE[0] = (l, i + 1) if i + 1 < n_tiles else ((l + 1, 0) if l + 1 < n_layers else None)
                if l == 0 and i == 0:
                    emit_pc([p for p in pc_l0 if p[3][1] == "f1"])
                if l == 0 and i == 1:
                    emit_pc(pc_l1[:len(pc_l1) // 2])
                if l == 0 and i == 2:
                    emit_pc(pc_l1[len(pc_l1) // 2:])
                if l == 0 and i == n_tiles - 1 and n_tiles < 3 and n_layers > 1:
                    emit_pc(pc_l1)
                if stages >= 1:
                    ffn(l, 0)
                if l == 0 and i == 0:
                    emit_pc([p for p in pc_l0 if p[3][1] != "f1"])
                if stages >= 1:
                    dump(f"x1_{l}_{i}", x32[:], "x32all", [128, 8, T])
                if stages >= 2:
                    mixer(l, i)
                    dump(f"x2_{l}_{i}", x32[:], "x32all", [128, 8, T])
                if stages >= 3:
                    ffn(l, 1)
                    dump(f"x3_{l}_{i}", x32[:], "x32all", [128, 8, T])
                if l == n_layers - 1:
                    store_x(i)
                else:
                    spill_x(i)

        fl.wait_all_dma("sp")
        fl.emit()
        print('sbuf_bytes_remaining', nc.sbuf_bytes_remaining)
        stats = dict(n_ins=fl.n_ins, n_wait=fl.n_wait, counts={k: v["count"] for k, v in fl.engs.items()}, nsem=len(fl.dsems))
    return nc, dbg_out, stats


_CACHE = {}


def kernel(**inputs):
    P = _pack({k: np.asarray(v) for k, v in inputs.items()})
    x = np.ascontiguousarray(np.asarray(inputs["x"], dtype=np.float32))
    B = x.shape[0]
    if "nc" not in _CACHE:
        _CACHE["nc"] = build()[0]
    nc = _CACHE["nc"]
    in_maps = []
    for b in range(B):
        d = dict(P)
        d["x"] = np.ascontiguousarray(x[b].reshape(NT, T, 8, 128).transpose(0, 3, 2, 1).reshape(NT * 128, 8 * T))
        in_maps.append(d)
    res = run_bass_kernel_spmd(nc, in_maps, core_ids=list(range(B)))
    out = np.stack([np.asarray(r["out"]).reshape(NT, 128, 8, T).transpose(0, 3, 2, 1).reshape(S, D) for r in res.results], axis=0)
    return out.astype(np.float32)
```
